# Optimizing a Trainium2 kernel written in Bass

```python
import jax
import jax.numpy as jnp
from jax import lax
import numpy as np

D_MODEL = 4096
BATCH = 4
SEQ = 4096
DEPTH = 4

GRID_W = 64
CTX_LEN = 256
N_MIXERS = 2
N_A_LAYERS = (DEPTH + 1) // 2
N_B_LAYERS = DEPTH // 2
N_MOD = 6
ADA_RANK = 256
MLP_HIDDEN = 4 * D_MODEL
NORM_EPS = 1e-6

HG_HEAD_DIM = 128
HG_HEADS = D_MODEL // HG_HEAD_DIM
HG_CHUNK = 64
HG_N_PROJ = 5

MLA_HEADS = D_MODEL // 128
MLA_NOPE = 128
MLA_ROPE = 64
MLA_V = 128
MLA_Q_RANK = 1024
MLA_KV_RANK = 512
MLA_Q_BLOCK = 128
MLA_SCALE = (MLA_NOPE + MLA_ROPE) ** -0.5
ROPE_THETA = 10000.0
ROPE_AXIS_DIM = MLA_ROPE // 2

kernel_name = 'hybrid_hgrn2_mla_flow_block'


def rms_norm(x, gain):
    xf = x.astype(jnp.float32)
    xf = xf * lax.rsqrt(jnp.mean(xf * xf, axis=-1, keepdims=True) + NORM_EPS)
    return (xf * gain.astype(jnp.float32)).astype(x.dtype)


def ada_modulation(cond, w_down, w_up, bias):
    m = (jax.nn.silu(cond) @ w_down) @ w_up + bias
    return jnp.split(m, N_MOD, axis=-1)


def modulate(h, shift, scale):
    return h * (1 + scale) + shift


def axial_rope_tables(rows, dtype):
    t = jnp.arange(rows * GRID_W)
    row = (t // GRID_W).astype(jnp.float32)
    col = (t % GRID_W).astype(jnp.float32)
    inv_freq = ROPE_THETA ** (-jnp.arange(0, ROPE_AXIS_DIM, 2, dtype=jnp.float32) / ROPE_AXIS_DIM)
    ang_r = row[:, None] * inv_freq
    ang_c = col[:, None] * inv_freq
    ang = jnp.concatenate([ang_r, ang_r, ang_c, ang_c], axis=-1)
    return jnp.cos(ang).astype(dtype), jnp.sin(ang).astype(dtype)


def apply_rope(x, cos, sin):
    r1, r2, c1, c2 = jnp.split(x, 4, axis=-1)
    return x * cos + jnp.concatenate([-r2, r1, -c2, c1], axis=-1) * sin


def mla_queries(cq, q_norm, w_uq):
    B, T, _ = cq.shape
    q = (rms_norm(cq, q_norm) @ w_uq).reshape(B, T, MLA_HEADS, MLA_NOPE + MLA_ROPE)
    return q[..., :MLA_NOPE], q[..., MLA_NOPE:]


def mla_keys_values(ckv, kv_norm, w_ukv):
    B, T, _ = ckv.shape
    kv = (rms_norm(ckv, kv_norm) @ w_ukv).reshape(B, T, MLA_HEADS, MLA_NOPE + MLA_V)
    return kv[..., :MLA_NOPE], kv[..., MLA_NOPE:]


def mla_attend(q_nope, q_rope, k_nope, k_rope, v):
    s = jnp.einsum('bqhd,bkhd->bhqk', q_nope, k_nope) + jnp.einsum('bqhr,bkr->bhqk', q_rope, k_rope)
    p = jax.nn.softmax(s.astype(jnp.float32) * MLA_SCALE, axis=-1).astype(v.dtype)
    return jnp.einsum('bhqk,bkhd->bqhd', p, v)


def mla_mixer(h_lat, h_ctx, cos, sin, w_down, q_norm, w_uq, kv_norm, w_ukv, w_out, with_ctx_out):
    B, T, _ = h_lat.shape
    L = h_ctx.shape[1]
    split_at = [MLA_Q_RANK, MLA_Q_RANK + MLA_KV_RANK]
    cq_l, ckv_l, kr_l = jnp.split(h_lat @ w_down, split_at, axis=-1)
    cq_c, ckv_c, kr_c = jnp.split(h_ctx @ w_down, split_at, axis=-1)
    qn_l, qr_l = mla_queries(cq_l, q_norm, w_uq)
    qr_l = apply_rope(qr_l, cos[:, None, :], sin[:, None, :])
    kr_l = apply_rope(kr_l, cos, sin)
    kn_l, v_l = mla_keys_values(ckv_l, kv_norm, w_ukv)
    kn_c, v_c = mla_keys_values(ckv_c, kv_norm, w_ukv)
    kn = jnp.concatenate([kn_c, kn_l], axis=1)
    kr = jnp.concatenate([kr_c, kr_l], axis=1)
    v = jnp.concatenate([v_c, v_l], axis=1)
    nb = T // MLA_Q_BLOCK

    def to_blocks(a):
        return a.reshape(B, nb, MLA_Q_BLOCK, *a.shape[2:]).swapaxes(0, 1)

    o = lax.map(lambda qb: mla_attend(qb[0], qb[1], kn, kr, v), (to_blocks(qn_l), to_blocks(qr_l)))
    y_lat = o.swapaxes(0, 1).reshape(B, T, MLA_HEADS * MLA_V) @ w_out
    if not with_ctx_out:
        return y_lat, None
    qn_c, qr_c = mla_queries(cq_c, q_norm, w_uq)
    y_ctx = mla_attend(qn_c, qr_c, kn_c, kr_c, v_c).reshape(B, L, MLA_HEADS * MLA_V) @ w_out
    return y_lat, y_ctx


def gla_scan(q, k, v, log_f, s0):
    B, T, H, K = q.shape
    n = T // HG_CHUNK

    def to_chunks(a):
        return a.reshape(B, n, HG_CHUNK, H, a.shape[-1]).transpose(1, 0, 3, 2, 4)

    incl = jnp.tril(jnp.ones((HG_CHUNK, HG_CHUNK), dtype=bool))[None, None, :, :, None]

    def step(S, xs):
        qc, kc, vc, gc = xs
        b = jnp.cumsum(gc, axis=2)
        o_inter = jnp.einsum('bhtk,bhkv->bhtv', qc * jnp.exp(b), S)
        rel = jnp.where(incl, b[:, :, :, None, :] - b[:, :, None, :, :], -jnp.inf)
        att = jnp.einsum('bhtk,bhsk,bhtsk->bhts', qc, kc, jnp.exp(rel))
        o = o_inter + jnp.einsum('bhts,bhsv->bhtv', att, vc)
        b_last = b[:, :, -1:, :]
        S_new = (jnp.exp(b_last[:, :, 0, :])[..., None] * S
                 + jnp.einsum('bhsk,bhsv->bhkv', kc * jnp.exp(b_last - b), vc))
        return S_new, o

    S_final, o = lax.scan(step, s0, (to_chunks(q), to_chunks(k), to_chunks(v), to_chunks(log_f)))
    o = o.transpose(1, 0, 3, 2, 4).reshape(B, T, H, v.shape[-1])
    return o, S_final


def hgrn2_mixer(h_lat, h_ctx, w_in, lower_bound, g_norm, w_out, with_ctx_out):
    lb = lower_bound.astype(jnp.float32)
    log_lb, log_1m_lb = jnp.log(lb), jnp.log1p(-lb)

    def project(h):
        B, T, _ = h.shape
        heads = lambda a: a.astype(jnp.float32).reshape(B, T, HG_HEADS, HG_HEAD_DIM)
        q, i, zf_fwd, zf_bwd, g = jnp.split(h @ w_in, HG_N_PROJ, axis=-1)

        def log_forget(z, d):
            return heads(jnp.logaddexp(log_lb[d], log_1m_lb[d] + jax.nn.log_sigmoid(z.astype(jnp.float32))))

        return heads(jax.nn.silu(q)), heads(i), log_forget(zf_fwd, 0), log_forget(zf_bwd, 1), g

    def run(h, s0_fwd, s0_bwd):
        q, i, lf_f, lf_b, g = project(h)
        flip = lambda a: jnp.flip(a, axis=1)
        o_f, s_f = gla_scan(q, -jnp.expm1(lf_f), i, lf_f, s0_fwd)
        o_b, s_b = gla_scan(flip(q), -jnp.expm1(flip(lf_b)), flip(i), flip(lf_b), s0_bwd)
        return o_f + flip(o_b), g, s_f, s_b

    def readout(o, g):
        B, T = o.shape[:2]
        o = o.reshape(B, T, D_MODEL).astype(g.dtype)
        return (rms_norm(o, g_norm) * jax.nn.silu(g)) @ w_out

    B = h_ctx.shape[0]
    s0 = jnp.zeros((B, HG_HEADS, HG_HEAD_DIM, HG_HEAD_DIM), jnp.float32)
    o_c, g_c, s_f, s_b = run(h_ctx, s0, s0)
    o_l, g_l, _, _ = run(h_lat, s_f, s_b)
    y_lat = readout(o_l, g_l)
    return y_lat, (readout(o_c, g_c) if with_ctx_out else None)


def sq_relu_mlp(h, w1, w2):
    return jnp.square(jax.nn.relu(h @ w1)) @ w2


def setup_inputs(seed: int = 0) -> dict:
    key = jax.random.key(seed)
    ks = jax.random.split(key, 23)
    f32 = jnp.float32

    def dense(k, shape, fan_in, gain=1.0):
        return jax.random.normal(k, shape, f32) * (gain * fan_in ** -0.5)

    def norm_gain(k, shape):
        return 1.0 + 0.05 * jax.random.normal(k, shape, f32)

    return {
        'x': jax.random.normal(ks[0], (BATCH, SEQ, D_MODEL), f32),
        'c': jax.random.normal(ks[1], (BATCH, D_MODEL), f32),
        'ctx': jax.random.normal(ks[2], (BATCH, CTX_LEN, D_MODEL), f32),
        'c_ctx': jax.random.normal(ks[3], (D_MODEL,), f32),
        'ada_down': dense(ks[4], (DEPTH, D_MODEL, ADA_RANK), D_MODEL),
        'ada_up': dense(ks[5], (DEPTH, ADA_RANK, N_MOD * D_MODEL), ADA_RANK, 0.5),
        'ada_bias': 0.02 * jax.random.normal(ks[6], (DEPTH, N_MOD * D_MODEL), f32),
        'norm_mix_pre': norm_gain(ks[7], (DEPTH, D_MODEL)),
        'norm_mix_post': norm_gain(ks[8], (DEPTH, D_MODEL)),
        'norm_mlp_pre': norm_gain(ks[9], (DEPTH, D_MODEL)),
        'norm_mlp_post': norm_gain(ks[10], (DEPTH, D_MODEL)),
        'mlp_w1': dense(ks[11], (DEPTH, D_MODEL, MLP_HIDDEN), D_MODEL),
        'mlp_w2': dense(ks[12], (DEPTH, MLP_HIDDEN, D_MODEL), MLP_HIDDEN),
        'hg_w_in': dense(ks[13], (N_A_LAYERS, D_MODEL, HG_N_PROJ * D_MODEL), D_MODEL),
        'hg_lb_logits': 0.5 * jax.random.normal(ks[14], (N_A_LAYERS, 2, D_MODEL), f32),
        'hg_norm': norm_gain(ks[15], (N_A_LAYERS, D_MODEL)),
        'hg_w_out': dense(ks[16], (N_A_LAYERS, D_MODEL, D_MODEL), D_MODEL),
        'mla_w_down': dense(ks[17], (N_B_LAYERS, D_MODEL, MLA_Q_RANK + MLA_KV_RANK + MLA_ROPE), D_MODEL),
        'mla_q_norm': norm_gain(ks[18], (N_B_LAYERS, MLA_Q_RANK)),
        'mla_w_uq': dense(ks[19], (N_B_LAYERS, MLA_Q_RANK, MLA_HEADS * (MLA_NOPE + MLA_ROPE)), MLA_Q_RANK),
        'mla_kv_norm': norm_gain(ks[20], (N_B_LAYERS, MLA_KV_RANK)),
        'mla_w_ukv': dense(ks[21], (N_B_LAYERS, MLA_KV_RANK, MLA_HEADS * (MLA_NOPE + MLA_V)), MLA_KV_RANK),
        'mla_w_out': dense(ks[22], (N_B_LAYERS, MLA_HEADS * MLA_V, D_MODEL), MLA_HEADS * MLA_V),
    }


def reference(x, c, ctx, c_ctx, ada_down, ada_up, ada_bias, norm_mix_pre, norm_mix_post, norm_mlp_pre,
              norm_mlp_post, mlp_w1, mlp_w2, hg_w_in, hg_lb_logits, hg_norm, hg_w_out, mla_w_down,
              mla_q_norm, mla_w_uq, mla_kv_norm, mla_w_ukv, mla_w_out):
    rows = x.shape[1] // GRID_W
    cos, sin = axial_rope_tables(rows, x.dtype)
    lb = jnp.cumsum(jax.nn.softmax(hg_lb_logits.astype(jnp.float32), axis=0), axis=0)
    lb = lb - lb[:1]
    x_lat, x_ctx = x, ctx
    for layer in range(DEPTH):
        last = layer == DEPTH - 1
        j = layer // N_MIXERS
        sh1, sc1, gt1, sh2, sc2, gt2 = ada_modulation(c[:, None, :], ada_down[layer], ada_up[layer], ada_bias[layer])
        csh1, csc1, cgt1, csh2, csc2, cgt2 = ada_modulation(c_ctx[None, None, :], ada_down[layer], ada_up[layer], ada_bias[layer])
        h_lat = modulate(rms_norm(x_lat, norm_mix_pre[layer]), sh1, sc1)
        h_ctx = modulate(rms_norm(x_ctx, norm_mix_pre[layer]), csh1, csc1)
        if layer % N_MIXERS == 0:
            y_lat, y_ctx = hgrn2_mixer(h_lat, h_ctx, hg_w_in[j], lb[j], hg_norm[j], hg_w_out[j], not last)
        else:
            y_lat, y_ctx = mla_mixer(h_lat, h_ctx, cos, sin, mla_w_down[j], mla_q_norm[j], mla_w_uq[j],
                                     mla_kv_norm[j], mla_w_ukv[j], mla_w_out[j], not last)
        x_lat = x_lat + gt1 * rms_norm(y_lat, norm_mix_post[layer])
        m_lat = sq_relu_mlp(modulate(rms_norm(x_lat, norm_mlp_pre[layer]), sh2, sc2), mlp_w1[layer], mlp_w2[layer])
        x_lat = x_lat + gt2 * rms_norm(m_lat, norm_mlp_post[layer])
        if not last:
            x_ctx = x_ctx + cgt1 * rms_norm(y_ctx, norm_mix_post[layer])
            m_ctx = sq_relu_mlp(modulate(rms_norm(x_ctx, norm_mlp_pre[layer]), csh2, csc2), mlp_w1[layer], mlp_w2[layer])
            x_ctx = x_ctx + cgt2 * rms_norm(m_ctx, norm_mlp_post[layer])
    return x_lat
```

```python
import contextlib
import numpy as np
import ml_dtypes
import concourse.bass as bass
import concourse.mybir as mybir
from concourse.bass_utils import run_bass_kernel_spmd

F32 = mybir.dt.float32
BF16 = mybir.dt.bfloat16
AF = mybir.ActivationFunctionType
ALU = mybir.AluOpType

NDSEM = 32
NPSEM = 8
NCSEM = 16
NCORES = 8
EPS = 1e-6


import os
_CAP = int(os.environ.get('KCAP', str(768 * 1024)))


def chunk_rows(K, N, nbytes, cap=_CAP):
    cr = K // 4
    while cr > 1 and cr * N * nbytes > cap:
        cr //= 2
    return cr


_UNIQ = [0]


def _uniq(n):
    _UNIQ[0] += 1
    return f"sb_{n}_{_UNIQ[0]}"


class Cfg:
    def __init__(self, D=4096, HID=16384, NTL=2048, CTX=256, QR=1024, KVR=512, ADAR=256, DEPTH=4,
                 TT=768, TT2=384, GRID_W=64):
        self.D, self.HID, self.NTL, self.CTX, self.QR, self.KVR, self.ADAR = D, HID, NTL, CTX, QR, KVR, ADAR
        self.DEPTH, self.TT, self.TT2, self.GRID_W = DEPTH, TT, TT2, GRID_W
        self.H = D // 128
        self.NT = NTL + CTX
        self.NB = self.NT // 128
        self.NBC = CTX // 128
        self.NK = CTX + 2 * NTL
        self.DC = D // 128
        self.NA = (DEPTH + 1) // 2
        self.NBL = DEPTH // 2
        self.DW = QR + KVR + 128
        self.KVX = KVR + 64


class Buf:
    __slots__ = ("w", "r", "const", "strict")

    def __init__(self, const=False, strict=False):
        self.w = None
        self.r = []
        self.const = const
        self.strict = strict


class Op:
    __slots__ = ("eng", "fn", "kind", "deps", "need_inc", "ms", "sem", "val", "phase", "sbuf")

    def __init__(self, eng, fn, kind, phase, sbuf):
        self.eng, self.fn, self.kind, self.phase, self.sbuf = eng, fn, kind, phase, sbuf
        self.deps = []
        self.need_inc = False
        self.ms = 0
        self.sem = None
        self.val = 0


class Prog:
    NAMES = ["pe", "act", "dve", "pool", "sp"]

    def __init__(self, nc, st):
        self.nc = nc
        self.pending = []
        self.phase = 0
        self.dma_n = 0
        self.pdma_n = 0
        self.cc_n = 0
        self.dcount = [0] * NDSEM
        self.dlast = [None] * NDSEM
        self.ccount = [0] * NCSEM
        self.clast = [None] * NCSEM
        self.cnt = {n: 0 for n in self.NAMES}
        self.esem = {n: st.enter_context(nc.semaphore("es_" + n)) for n in self.NAMES}
        self.dsem = [st.enter_context(nc.semaphore(f"ds_{i}")) for i in range(NDSEM)]
        self.csem = [st.enter_context(nc.semaphore(f"cs_{i}")) for i in range(NCSEM)]
        self.block = st.enter_context(nc.Block())
        self.engobj = {"pe": nc.tensor, "act": nc.scalar, "dve": nc.vector, "pool": nc.gpsimd, "sp": nc.sync}
        self.waited = {n: {} for n in self.NAMES}
        self.last = {n: None for n in self.NAMES}
        self.phase_dmas = []
        self.n_ins = 0

    def op(self, eng, fn, reads=(), writes=(), kind="c", sbuf=True):
        o = Op(eng, fn, kind, self.phase, sbuf)
        deps = {}
        strict = set()
        for b in reads:
            if b.w is not None:
                deps[id(b.w)] = b.w
                if b.strict:
                    strict.add(id(b.w))
        for b in writes:
            if b.w is not None:
                deps[id(b.w)] = b.w
                if b.strict:
                    strict.add(id(b.w))
            for r in b.r:
                deps[id(r)] = r
        if kind == "d":
            if eng == "pool":
                k = NDSEM - NPSEM + (self.pdma_n % NPSEM)
                self.pdma_n += 1
            else:
                k = self.dma_n % (NDSEM - NPSEM)
                self.dma_n += 1
            self.dcount[k] += 16
            o.sem = ("d", k)
            o.val = self.dcount[k]
            if self.dlast[k] is not None:
                deps[id(self.dlast[k])] = self.dlast[k]
            self.dlast[k] = o
            if sbuf:
                self.phase_dmas.append(o)
        elif kind == "k":
            k = self.cc_n % NCSEM
            self.cc_n += 1
            self.ccount[k] += 1
            o.sem = ("k", k)
            o.val = self.ccount[k]
            if self.clast[k] is not None:
                deps[id(self.clast[k])] = self.clast[k]
            self.clast[k] = o
        for d in deps.values():
            if d.kind == "c":
                if d.phase < self.phase:
                    continue
                if d.eng == eng and kind == "c" and id(d) not in strict:
                    continue
                d.need_inc = True
            o.deps.append(d)
        for b in reads:
            if not b.const:
                b.r.append(o)
        for b in writes:
            b.w = o
            b.r = []
        self.pending.append(o)
        if kind == "c":
            self.last[eng] = o
        return o

    def mm(self, out, lhsT, rhs, start, stop, reads, writes):
        return self.op("pe", lambda e: e.matmul(out, lhsT, rhs, start=start, stop=stop), reads, writes)

    def tr(self, out, in_, ident, reads, writes):
        return self.op("pe", lambda e: e.transpose(out, in_, ident), reads, writes)

    def actf(self, out, in_, func, reads, writes, scale=None, bias=None, accum=None, eng="act"):
        kw = {}
        if scale is not None:
            kw["scale"] = scale
        if bias is not None:
            kw["bias"] = bias
        if accum is not None:
            kw["accum_out"] = accum
        return self.op(eng, lambda e: e.activation(out, in_, func, **kw), reads, writes)

    def tt(self, eng, out, in0, in1, op, reads, writes):
        return self.op(eng, lambda e: e.tensor_tensor(out, in0, in1, op), reads, writes)

    def ts(self, eng, out, in0, s1, s2, op0, op1, reads, writes):
        if op1 is None:
            return self.op(eng, lambda e: e.tensor_scalar(out, in0, s1, None, op0), reads, writes)
        return self.op(eng, lambda e: e.tensor_scalar(out, in0, s1, s2, op0, op1), reads, writes)

    def stt(self, eng, out, in0, scalar, in1, op0, op1, reads, writes):
        return self.op(eng, lambda e: e.scalar_tensor_tensor(out, in0, scalar, in1, op0, op1), reads, writes)

    def cp(self, eng, out, in_, reads, writes):
        if eng == "act":
            return self.op(eng, lambda e: e.activation(out, in_, AF.Copy), reads, writes)
        return self.op(eng, lambda e: e.tensor_copy(out, in_), reads, writes)

    def recip(self, out, in_, reads, writes):
        return self.op("dve", lambda e: e.reciprocal(out, in_), reads, writes)

    def memset(self, eng, ap, val, writes):
        return self.op(eng, lambda e: e.memset(ap, val), (), writes)

    def dma(self, eng, out, in_, reads=(), writes=(), sbuf=True, slow=False):
        if slow:
            return self.op(eng, lambda e: e.dma_start(out=out, in_=in_, allow_slow_non_contiguous=True), reads, writes, kind="d", sbuf=sbuf)
        return self.op(eng, lambda e: e.dma_start(out=out, in_=in_), reads, writes, kind="d", sbuf=sbuf)

    def coll(self, kind, op, groups, in_ap, out_ap, reads, writes):
        return self.op("pool", lambda e: e.collective_compute(kind, op, replica_groups=groups, ins=[in_ap], outs=[out_ap]),
                       reads, writes, kind="k", sbuf=False)

    def _semof(self, d):
        if d.kind == "c":
            return ("e", d.eng), self.esem[d.eng], d.ms
        if d.kind == "d":
            return d.sem, self.dsem[d.sem[1]], d.val
        return d.sem, self.csem[d.sem[1]], d.val

    def _emit_waits(self, eng, deps):
        E = self.engobj[eng]
        w = self.waited[eng]
        for d in deps:
            key, s, v = self._semof(d)
            if w.get(key, 0) < v:
                E.wait_ge(s, v)
                self.n_ins += 1
                w[key] = v

    def end_phase(self):
        lasts = [o for o in self.last.values() if o is not None and o.phase == self.phase]
        for o in lasts:
            o.need_inc = True
        for o in self.pending:
            if o.kind == "c" and o.need_inc:
                self.cnt[o.eng] += 1
                o.ms = self.cnt[o.eng]
        for o in self.pending:
            self._emit_waits(o.eng, o.deps)
            ins = o.fn(self.engobj[o.eng])
            self.n_ins += 1
            if o.kind == "d":
                ins.then_inc(self.dsem[o.sem[1]], 16)
            elif o.kind == "k":
                ins.then_inc(self.csem[o.sem[1]], 1)
            elif o.need_inc:
                ins.then_inc(self.esem[o.eng], 1)
            o.fn = None
        bar = lasts + self.phase_dmas
        for n in self.NAMES:
            self._emit_waits(n, [d for d in bar if not (d.kind == "c" and d.eng == n)])
        self.pending = []
        self.phase_dmas = []
        self.phase += 1

    def finish(self):
        self.end_phase()
        E = self.engobj["sp"]
        for k in range(NDSEM):
            if self.dcount[k]:
                E.wait_ge(self.dsem[k], self.dcount[k])
        for k in range(NCSEM):
            if self.ccount[k]:
                E.wait_ge(self.csem[k], self.ccount[k])


class K:
    def __init__(self, cfg, stop=None, dbg=()):
        self.cfg = cfg
        self.stop = stop
        self.dbg = list(dbg)
        self.nc = bass.Bass("TRN2", target_bir_lowering=False)
        self.ext_in = {}
        self.bufs = {}

    def check_stop(self, name):
        if self.stop == name:
            raise StopIteration

    def din(self, name, shape, dt=F32):
        t = self.nc.dram_tensor(name, list(shape), dt, kind="ExternalInput")
        self.ext_in[name] = (tuple(shape), dt)
        return t

    def dint(self, name, shape, dt):
        return self.nc.dram_tensor(name, list(shape), dt)

    def B(self, key):
        b = self.bufs.get(key)
        if b is None:
            b = self.bufs[key] = Buf()
        return b

    def build(self):
        c = self.cfg
        nc = self.nc
        with contextlib.ExitStack() as gst:
            self.P = P = Prog(nc, gst)
            self.gst = gst
            self.declare_io()
            self.alloc_global(gst)
            self.xs = self.dint("xs", [c.NT, c.D], F32)
            self.ybuf = self.dint("ybuf", [c.NT, c.D], F32)
            try:
                self.prologue()
                self.cur_x = self.x_loc
                self.pending = None
                self.check_stop("prologue")
                for l in range(c.DEPTH):
                    if l % 2 == 0:
                        self.hgrn2_layer(l)
                    else:
                        self.mla_layer(l)
                    self.check_stop(f"mix{l}")
                    self.mlp_layer(l)
                    self.check_stop(f"mlp{l}")
                self.final_out()
            except StopIteration:
                pass
            if self.dbg:
                P.end_phase()
                E = P.engobj["sp"]
                for k in range(NDSEM):
                    if P.dcount[k]:
                        E.wait_ge(P.dsem[k], P.dcount[k])
                        P.waited["sp"][("d", k)] = P.dcount[k]
                for k in range(NCSEM):
                    if P.ccount[k]:
                        E.wait_ge(P.csem[k], P.ccount[k])
                        P.waited["sp"][("k", k)] = P.ccount[k]
                for name in self.dbg:
                    if name in ("hlf0", "hlf1"):
                        t = self.hlf[int(name[-1])]
                    else:
                        t = getattr(self, name) if hasattr(self, name) else self.wfull[name]
                    shp = list(t.shape)
                    o = nc.dram_tensor("dbg_" + name, shp, t.dtype, kind="ExternalOutput")
                    if len(shp) == 2:
                        P.dma("pool", o[:, :], t[:, :], sbuf=False)
                    elif len(shp) == 3:
                        P.dma("pool", o[:, :, :], t[:, :, :], sbuf=False)
                    else:
                        P.dma("pool", o[:, :, :, :], t[:, :, :, :], sbuf=False)
            P.finish()
        return nc

    def declare_io(self):
        c = self.cfg
        D = c.D
        L = c.DEPTH
        self.x_loc = self.din("x_loc", [c.NT, D])
        self.condT = self.din("condT", [128, c.DC, 2])
        self.out = self.nc.dram_tensor("out", [c.NTL, D], F32, kind="ExternalOutput")
        self.ws = {}
        for l in range(L):
            self.ws[f"adown{l}"] = (self.din(f"adown{l}", [D // 4, c.ADAR]), [D, c.ADAR], F32, 4)
            self.ws[f"aup{l}"] = (self.din(f"aup{l}", [c.ADAR // 4, 6 * D]), [c.ADAR, 6 * D], F32, 4)
            self.ws[f"w1_{l}"] = (self.din(f"w1_{l}", [D // 4, c.HID]), [D, c.HID], BF16, 4)
            self.ws[f"w2_{l}"] = (self.din(f"w2_{l}", [c.HID // 4, D]), [c.HID, D], BF16, 4)
            if l % 2 == 0:
                self.ws[f"hqig{l}"] = (self.din(f"hqig{l}", [D // 4, 3 * D]), [D, 3 * D], BF16, 4)
                self.ws[f"hz{l}"] = (self.din(f"hz{l}", [D // 4, 2 * D]), [D, 2 * D], BF16, 4)
                self.ws[f"hout{l}"] = (self.din(f"hout{l}", [D // 4, D]), [D, D], BF16, 4)
            else:
                self.ws[f"mdown{l}"] = (self.din(f"mdown{l}", [D // 4, c.DW]), [D, c.DW], BF16, 4)
                self.ws[f"muq{l}"] = (self.din(f"muq{l}", [c.QR // 4, c.H * 256]), [c.QR, c.H * 256], BF16, 4)
                self.ws[f"mukv{l}"] = (self.din(f"mukv{l}", [c.KVR // 4, c.H * 256]), [c.KVR, c.H * 256], BF16, 4)
                self.ws[f"mout{l}"] = (self.din(f"mout{l}", [D // 4, D]), [D, D], BF16, 4)
        self.abias = self.din("abias", [L, 6 * D])
        self.gains = self.din("gains", [4, L, D])
        self.lblog = self.din("lblog", [c.NA, 2, D])
        self.hgnormT = self.din("hgnormT", [c.NA, 128, c.DC])
        self.qnormT = self.din("qnormT", [c.NBL, 128, c.QR // 128])
        self.kvnormT = self.din("kvnormT", [c.NBL, 128, c.KVR // 128])
        self.c_ident = self.din("c_ident", [128, 128])
        self.c_trif = self.din("c_trif", [128, 128])
        self.c_trifx = self.din("c_trifx", [128, 128])
        self.c_ind = self.din("c_ind", [128, 2])
        self.c_sel = self.din("c_sel", [128, 2])
        self.c_cos_tok = self.din("c_cos_tok", [c.NT, 64])
        self.c_sin_tok = self.din("c_sin_tok", [c.NT, 64])
        self.c_cosT = self.din("c_cosT", [64, c.NT])
        self.c_sinT = self.din("c_sinT", [64, c.NT])

    def alloc_global(self, st):
        nc = self.nc
        c = self.cfg
        self.pf = [st.enter_context(nc.psum_tensor(f"pf{i}", [128, 512], F32)) for i in range(8)]
        self.Bpf = [Buf() for _ in range(8)]
        self.pb = [self.pf[6][:].bitcast(BF16), self.pf[7][:].bitcast(BF16)]
        self.Bpb = [self.Bpf[6], self.Bpf[7]]
        sb = lambda n, s, d: st.enter_context(nc.sbuf_tensor(_uniq(n), s, d))
        self.identf = sb("identf", [128, 128], F32)
        self.identb = sb("identb", [128, 128], BF16)
        self.trif = sb("trif", [128, 128], F32)
        self.trifx = sb("trifx", [128, 128], F32)
        self.trib = sb("trib", [128, 128], F32)
        self.tribx = sb("tribx", [128, 128], F32)
        self.ind = sb("ind", [128, 2], F32)
        self.maskf = sb("maskf", [128, 512], F32)
        self.maskb = sb("maskb", [128, 512], F32)
        self.epsc = sb("epsc", [128, 1], F32)
        self.selv = sb("selv", [128, 2], F32)
        self.Bconst = Buf()
        P = self.P
        Bc = self.Bconst
        P.dma("sp", self.identf[:], self.c_ident[:, :], writes=[Bc])
        P.dma("sp", self.trif[:], self.c_trif[:, :], writes=[Buf()])
        P.dma("sp", self.trifx[:], self.c_trifx[:, :], writes=[Buf()])
        P.dma("sp", self.ind[:], self.c_ind[:, :], writes=[Buf()])
        P.dma("sp", self.selv[:], self.c_sel[:, :], writes=[Buf()])
        P.end_phase()
        P.cp("dve", self.identb[:], self.identf[:], [], [Bc])
        P.memset("dve", self.epsc[:], EPS, [Bc])
        P.tr(self.pf[0][:, 0:128], self.trif[:], self.identf[:], [], [self.Bpf[0]])
        P.tr(self.pf[0][:, 128:256], self.trifx[:], self.identf[:], [], [self.Bpf[0]])
        P.cp("dve", self.trib[:], self.pf[0][:, 0:128], [self.Bpf[0]], [Bc])
        P.cp("dve", self.tribx[:], self.pf[0][:, 128:256], [self.Bpf[0]], [Bc])
        for h in range(4):
            P.cp("dve", self.maskf[:, h * 128:(h + 1) * 128], self.trif[:], [], [Bc])
            P.cp("dve", self.maskb[:, h * 128:(h + 1) * 128], self.trib[:], [Bc], [Bc])
        P.end_phase()
        self.Bconst = Buf(const=True)

    def prologue(self):
        c = self.cfg
        P = self.P
        D = c.D
        self.wfull = {}
        self.Bw = {}
        g8 = [[0, 1, 2, 3], [4, 5, 6, 7]]
        order = []
        for l in range(c.DEPTH):
            order += [f"adown{l}", f"aup{l}"]
        for l in range(c.DEPTH):
            if l % 2 == 0:
                order += [f"hqig{l}", f"hz{l}", f"hout{l}"]
            else:
                order += [f"mdown{l}", f"muq{l}", f"mukv{l}", f"mout{l}"]
            order += [f"w1_{l}", f"w2_{l}"]
        shards = {}
        for name in order:
            src, full_shape, dt, nr = self.ws[name]
            rows = full_shape[0] // nr
            sh = self.dint(name + "_s", [rows, full_shape[1]], dt)
            full = self.dint(name + "_f", full_shape, dt)
            self.wfull[name] = full
            bsh = Buf()
            step = max(1, (1 << 20) // full_shape[1])
            r0 = 0
            while r0 < rows:
                r1 = min(rows, r0 + step)
                P.dma("pool", sh[r0:r1, :], src[r0:r1, :], writes=[Buf()], sbuf=False)
                r0 = r1
            shards[name] = sh
        P.end_phase()
        for k in range(NDSEM):
            if P.dcount[k]:
                P.engobj["pool"].wait_ge(P.dsem[k], P.dcount[k])
                P.waited["pool"][("d", k)] = P.dcount[k]
        for name in order:
            src, full_shape, dt, nr = self.ws[name]
            Kf, Nf = full_shape
            cr = chunk_rows(Kf, Nf, 4 if dt == F32 else 2)
            bl = []
            for i in range((Kf // 4) // cr):
                b = Buf()
                bl.append(b)
                P.coll("AllGather", ALU.bypass, g8, shards[name][i * cr:(i + 1) * cr, :], self.wfull[name][i * 4 * cr:(i + 1) * 4 * cr, :], [], [b])
            self.Bw[name] = bl
        P.end_phase()
        self.vec = self.dint("vec", [c.DEPTH, 6, 2, D], F32)
        self.Bvec = Buf()
        self.lbv = self.dint("lbv", [c.NA, 2, 2, D], F32)
        self.Blbv = Buf()
        nc = self.nc
        with contextlib.ExitStack() as st:
            sb = lambda n, s, d: st.enter_context(nc.sbuf_tensor(_uniq(n), s, d))
            cs = sb("cs", [128, c.DC, 2], F32)
            adn = sb("adn", [128, c.DC, c.ADAR], F32)
            RC = c.ADAR // 128
            tT = sb("tT", [128, RC, 2], F32)
            CW = 2048
            aup = [sb(f"aup{i}", [128, RC, CW], F32) for i in range(2)]
            modk = sb("modk", [2, D], F32)
            biask = sb("biask", [2, D], F32)
            gnk = sb("gnk", [2, D], F32)
            resk = sb("resk", [2, D], F32)
            Bcs, Badn, BtT, Bmod, Bbias, Bgn, Bres = (Buf() for _ in range(7))
            Baup = [Buf(), Buf()]
            P.dma("sp", cs[:], self.condT[:, :, :], writes=[Bcs])
            P.actf(cs[:], cs[:], AF.Silu, [Bcs], [Bcs])
            kmap = {0: (1, None), 1: (0, 0), 2: (2, 1), 3: (4, None), 4: (3, 2), 5: (5, 3)}
            npk = D // CW if D >= CW else 1
            cw = min(CW, D)
            api = 0
            for l in range(c.DEPTH):
                P.dma("sp", adn[:], self.wfull[f"adown{l}"][:, :].rearrange("(c p) r -> p c r", p=128), reads=self.Bw[f"adown{l}"], writes=[Badn])
                for rc in range(RC):
                    for kc in range(c.DC):
                        P.mm(self.pf[0][:, 0:2], adn[:, kc, rc * 128:(rc + 1) * 128], cs[:, kc, :], kc == 0, kc == c.DC - 1,
                             [Badn, Bcs], [self.Bpf[0]])
                    P.cp("dve", tT[:, rc, :], self.pf[0][:, 0:2], [self.Bpf[0]], [BtT])
                for kd in range(6):
                    outk, gi = kmap[kd]
                    P.dma("sp", biask[:], self.abias[l:l + 1, kd * D:(kd + 1) * D].broadcast_to([2, D]), writes=[Bbias])
                    if gi is not None:
                        P.dma("sp", gnk[:], self.gains[gi, l:l + 1, :].broadcast_to([2, D]), writes=[Bgn])
                    for pc in range(npk):
                        ab = api % 2
                        api += 1
                        col0 = kd * D + pc * cw
                        P.dma("sp", aup[ab][:, :, 0:cw], self.wfull[f"aup{l}"][:, col0:col0 + cw].rearrange("(c p) n -> p c n", p=128),
                              reads=self.Bw[f"aup{l}"], writes=[Baup[ab]])
                        for jj in range(cw // 512):
                            pi = 1 + (jj % 2)
                            for rc in range(RC):
                                P.mm(self.pf[pi][0:2, :], tT[:, rc, :], aup[ab][:, rc, jj * 512:(jj + 1) * 512], rc == 0, rc == RC - 1,
                                     [BtT, Baup[ab]], [self.Bpf[pi]])
                            n0 = pc * cw + jj * 512
                            P.tt("dve", modk[:, n0:n0 + 512], self.pf[pi][0:2, :], biask[:, n0:n0 + 512], ALU.add, [self.Bpf[pi], Bbias], [Bmod])
                    if kd in (1, 4):
                        P.stt("dve", resk[:], modk[:], 1.0, gnk[:], ALU.add, ALU.mult, [Bmod, Bgn], [Bres])
                    elif kd in (2, 5):
                        P.tt("dve", resk[:], modk[:], gnk[:], ALU.mult, [Bmod, Bgn], [Bres])
                    else:
                        P.cp("dve", resk[:], modk[:], [Bmod], [Bres])
                    P.dma("sp", self.vec[l, outk, :, :], resk[:], reads=[Bres], writes=[self.Bvec])
            P.end_phase()
        with contextlib.ExitStack() as st:
            sb = lambda n, s, d: st.enter_context(nc.sbuf_tensor(_uniq(n), s, d))
            lg = sb("lg", [2, c.NA, D], F32)
            ex = sb("ex", [2, c.NA, D], F32)
            sm = sb("sm", [2, D], F32)
            lbt = sb("lbt", [2, c.NA, 2, D], F32)
            Blg, Bex, Bsm, Blbt = Buf(), Buf(), Buf(), Buf()
            P.dma("sp", lg[:], self.lblog[:, :, :].rearrange("j r d -> r j d"), writes=[Blg])
            P.actf(ex[:], lg[:], AF.Exp, [Blg], [Bex])
            P.cp("dve", sm[:], ex[:, 0, :], [Bex], [Bsm])
            for j in range(1, c.NA):
                P.tt("dve", sm[:], sm[:], ex[:, j, :], ALU.add, [Bex, Bsm], [Bsm])
            P.recip(sm[:], sm[:], [Bsm], [Bsm])
            P.memset("dve", lbt[:, 0, 0, :], 0.0, [Blbt])
            P.memset("dve", lbt[:, 0, 1, :], 1.0, [Blbt])
            for j in range(1, c.NA):
                P.tt("dve", ex[:, j, :], ex[:, j, :], sm[:], ALU.mult, [Bex, Bsm], [Bex])
                if j == 1:
                    P.cp("dve", lbt[:, j, 0, :], ex[:, j, :], [Bex], [Blbt])
                else:
                    P.tt("dve", lbt[:, j, 0, :], lbt[:, j - 1, 0, :], ex[:, j, :], ALU.add, [Bex, Blbt], [Blbt])
                P.ts("dve", lbt[:, j, 1, :], lbt[:, j, 0, :], -1.0, 1.0, ALU.mult, ALU.add, [Blbt], [Blbt])
            P.dma("sp", self.lbv[:, :, :, :].rearrange("j r k d -> r j k d"), lbt[:], reads=[Blbt], writes=[self.Blbv])
            P.end_phase()

    def tiles(self, TT):
        c = self.cfg
        tb = TT // 128
        return [list(range(i, min(c.NB, i + tb))) for i in range(0, c.NB, tb)]

    def row_of(self, blk):
        return 1 if blk < self.cfg.NBC else 0

    def provider_norm(self, st, l, kindA, kindB, pend):
        c = self.cfg
        P = self.P
        nc = self.nc
        D = c.D
        sb = lambda n, s, d: st.enter_context(nc.sbuf_tensor(_uniq(n), s, d))
        xt = sb("pn_x", [128, D], F32)
        yt = sb("pn_y", [128, D], F32)
        gt = sb("pn_g", [128, D], F32) if pend else None
        ab = sb("pn_ab", [128, 2, 2, c.DC], F32)
        stt_ = sb("pn_st", [128, 4], F32)
        Bx, By, Bg, Bab, Bst = Buf(), Buf(), Buf(), Buf(), Buf(strict=True)
        for r in range(2):
            P.dma("sp", ab[:, r, 0, :], self.vec[l, kindA, r, :].rearrange("(c p) -> p c", p=128), reads=[self.Bvec], writes=[Bab], slow=True)
            P.dma("sp", ab[:, r, 1, :], self.vec[l, kindB, r, :].rearrange("(c p) -> p c", p=128), reads=[self.Bvec], writes=[Bab], slow=True)
        state = {"grow": None}
        xsrc = self.cur_x
        Bxsrc = self.B(("x",))
        xdst = self.xs
        rsq = float(D) ** -0.5

        def fill(blocks, actT, Bact):
            for i, blk in enumerate(blocks):
                row = self.row_of(blk)
                r0 = blk * 128
                P.dma("sp", xt[:], xsrc[r0:r0 + 128, :], reads=[self.B(("x", blk))], writes=[Bx])
                if pend:
                    ydram, kindG = pend
                    if state["grow"] != row:
                        P.dma("sp", gt[:], self.vec[l if kindG == 2 else l - 1, kindG, row:row + 1, :].broadcast_to([128, D]),
                              reads=[self.Bvec], writes=[Bg])
                        state["grow"] = row
                    P.dma("sp", yt[:], ydram[r0:r0 + 128, :], reads=[self.B(("y", blk, n0)) for n0 in range(0, D, 512)], writes=[By])
                    P.actf(self.junk[:, 0:D], yt[:], AF.Square, [By], [self.Bjunk, Bst], scale=rsq, accum=stt_[:, 0:1])
                    P.actf(stt_[:, 1:2], stt_[:, 0:1], AF.Sqrt, [Bst], [Bst], bias=self.epsc[:, 0:1])
                    P.recip(stt_[:, 1:2], stt_[:, 1:2], [Bst], [Bst])
                    P.stt("dve", yt[:], yt[:], stt_[:, 1:2], gt[:], ALU.mult, ALU.mult, [By, Bst, Bg], [By])
                    P.tt("dve", xt[:], xt[:], yt[:], ALU.add, [Bx, By], [Bx])
                    P.dma("sp", xdst[r0:r0 + 128, :], xt[:], reads=[Bx], writes=[self.B(("x", blk))])
                P.actf(self.junk[:, 0:D], xt[:], AF.Square, [Bx], [self.Bjunk, Bst], scale=rsq, accum=stt_[:, 2:3])
                P.actf(stt_[:, 3:4], stt_[:, 2:3], AF.Sqrt, [Bst], [Bst], bias=self.epsc[:, 0:1])
                P.recip(stt_[:, 3:4], stt_[:, 3:4], [Bst], [Bst])
                P.ts("dve", yt[:], xt[:], stt_[:, 3:4], None, ALU.mult, None, [Bx, Bst], [By])
                self.transpose_f32(yt, By, c.DC, actT, Bact, i * 128, scale_cols=ab[:, row, 0, :], bias_cols=ab[:, row, 1, :], Bsc=Bab)
        return fill

    def transpose_f32(self, src, Bsrc, nch, actT, Bact, toff, scale_cols=None, bias_cols=None, Bsc=None, dst_c0=0):
        P = self.P
        k = 0
        for c0 in range(0, nch, 4):
            n = min(4, nch - c0)
            pi = 4 + (self._trk % 2)
            self._trk += 1
            for j in range(n):
                cc = c0 + j
                P.tr(self.pf[pi][:, j * 128:(j + 1) * 128], src[:, cc * 128:(cc + 1) * 128], self.identf[:], [Bsrc], [self.Bpf[pi]])
            for j in range(n):
                cc = c0 + j
                eng = "act"
                if scale_cols is not None:
                    P.actf(actT[:, dst_c0 + cc, toff:toff + 128], self.pf[pi][:, j * 128:(j + 1) * 128], AF.Identity,
                           [self.Bpf[pi], Bsc], [Bact], scale=scale_cols[:, cc:cc + 1],
                           bias=(bias_cols[:, cc:cc + 1] if bias_cols is not None else None))
                else:
                    P.cp("act" if (pi % 2) else "dve", actT[:, dst_c0 + cc, toff:toff + 128], self.pf[pi][:, j * 128:(j + 1) * 128],
                         [self.Bpf[pi]], [Bact])
                k += 1

    def transpose_b16(self, src, Bsrc, nch, actT, Bact, toff):
        P = self.P
        for c0 in range(0, nch, 8):
            n = min(8, nch - c0)
            pi = self._trk % 2
            self._trk += 1
            for j in range(n):
                P.tr(self.pb[pi][:, j * 128:(j + 1) * 128], src[:, (c0 + j) * 128:(c0 + j + 1) * 128], self.identb[:], [Bsrc], [self.Bpb[pi]])
            P.cp("act" if (pi % 2) else "dve", actT[:, c0:c0 + n, toff:toff + 128],
                 self.pb[pi][:, 0:n * 128].rearrange("p (c t) -> p c t", t=128), [self.Bpb[pi]], [Bact])

    def gemm_g1(self, st, K, ntiles, wsrc, tiles, fill, epi, TTW, nw=2):
        c = self.cfg
        P = self.P
        nc = self.nc
        KC = K // 128
        KP = (KC + 31) // 32
        KCP = KC // KP
        sb = lambda n, s, d: st.enter_context(nc.sbuf_tensor(_uniq(n), s, d))
        actT = sb("g1_act", [128, KC, TTW], BF16)
        Bact = Buf()
        wt = [sb(f"g1_w{i}", [128, KCP, 512], BF16) for i in range(nw)]
        Bwt = [Buf() for _ in range(nw)]
        wi = 0
        rot = 0
        for blocks in tiles:
            fill(blocks, actT, Bact)
            seq = [(nt, kp) for nt in range(len(ntiles)) for kp in range(KP)]
            loaded = {}

            def load(idx):
                nonlocal wi
                nt, kp = seq[idx]
                key, n0, width = ntiles[nt]
                ap, bw = wsrc(key, kp * KCP * 128, (kp + 1) * KCP * 128, n0, width)
                j = wi % nw
                wi += 1
                P.dma("sp", wt[j][:, :, 0:width], ap.rearrange("(c p) n -> p c n", p=128), reads=bw, writes=[Bwt[j]])
                loaded[idx] = j
            load(0)
            for idx, (nt, kp) in enumerate(seq):
                if idx + 1 < len(seq):
                    load(idx + 1)
                j = loaded.pop(idx)
                key, n0, width = ntiles[nt]
                for ti, blk in enumerate(blocks):
                    if KP == 1:
                        pi = rot % 4
                        rot += 1
                    else:
                        pi = (nt % 2) * 3 + ti
                    for kc in range(KCP):
                        P.mm(self.pf[pi][:, 0:width], actT[:, kp * KCP + kc, ti * 128:(ti + 1) * 128], wt[j][:, kc, 0:width],
                             kp == 0 and kc == 0, kp == KP - 1 and kc == KCP - 1, [Bact, Bwt[j]], [self.Bpf[pi]])
                    if kp == KP - 1:
                        epi(key, n0, width, blk, self.pf[pi][:, 0:width], self.Bpf[pi])

    def gemm_g2(self, st, K, ngroups, wsrc, tiles, fill, epi, TTW, sub=384):
        P = self.P
        nc = self.nc
        KC = K // 128
        sb = lambda n, s, d: st.enter_context(nc.sbuf_tensor(_uniq(n), s, d))
        actT = sb("g2_act", [128, KC, TTW], BF16)
        Bact = Buf()
        wt = [sb(f"g2_w{i}", [128, KC, 512], BF16) for i in range(2)]
        Bwt = [Buf(), Buf()]
        wi = 0
        rot = 0
        for blocks in tiles:
            fill(blocks, actT, Bact)
            tw = len(blocks) * 128
            subs = [(s0, min(sub, tw - s0)) for s0 in range(0, tw, sub)]

            def load(g):
                nonlocal wi
                key, n0, width = ngroups[g]
                ap, bw = wsrc(key, 0, K, n0, width)
                j = wi % 2
                wi += 1
                P.dma("sp", wt[j][:, :, 0:width], ap.rearrange("(c p) n -> p c n", p=128), reads=bw, writes=[Bwt[j]])
                return j
            jn = load(0)
            for g, (key, n0, width) in enumerate(ngroups):
                j = jn
                if g + 1 < len(ngroups):
                    jn = load(g + 1)
                for nb in range(width // 128):
                    for (s0, sw) in subs:
                        pi = rot % 4
                        rot += 1
                        for kc in range(KC):
                            P.mm(self.pf[pi][:, 0:sw], wt[j][:, kc, nb * 128:(nb + 1) * 128], actT[:, kc, s0:s0 + sw],
                                 kc == 0, kc == KC - 1, [Bact, Bwt[j]], [self.Bpf[pi]])
                        epi(key, n0 + nb * 128, blocks, s0, sw, self.pf[pi][:, 0:sw], self.Bpf[pi])

    def phase_scope(self):
        k = self

        class _S:
            def __enter__(s):
                s.st = contextlib.ExitStack()
                s.st.__enter__()
                k._trk = 0
                k.junk = s.st.enter_context(k.nc.sbuf_tensor(_uniq("junk"), [128, k.cfg.D], BF16))
                k.Bjunk = Buf()
                return s.st

            def __exit__(s, *a):
                if a[0] is None:
                    k.P.end_phase()
                return s.st.__exit__(*a)
        return _S()

    def store_epi(self, st, dram, bkey, dt=F32, func=None, nbuf=3):
        P = self.P
        nc = self.nc
        ob = [st.enter_context(nc.sbuf_tensor(_uniq(f"se_{bkey}_{i}"), [128, 512], dt)) for i in range(nbuf)]
        Bo = [Buf() for _ in range(nbuf)]
        cnt = [0]

        def epi(key, n0, width, blk, ps, Bps):
            j = cnt[0] % nbuf
            cnt[0] += 1
            if func is None:
                P.cp("act" if (cnt[0] % 2) else "dve", ob[j][:, 0:width], ps, [Bps], [Bo[j]])
            else:
                P.actf(ob[j][:, 0:width], ps, func, [Bps], [Bo[j]])
            P.dma("sp", dram[blk * 128:(blk + 1) * 128, n0:n0 + width], ob[j][:, 0:width], reads=[Bo[j]],
                  writes=[self.B((bkey, blk, n0))])
        return epi

    def mlp_layer(self, l):
        c = self.cfg
        P = self.P
        nc = self.nc
        D, HID = c.D, c.HID
        if not hasattr(self, "h1t"):
            self.h1t = self.dint("h1t", [HID, c.NT], BF16)
        w1 = self.wfull[f"w1_{l}"]
        w2 = self.wfull[f"w2_{l}"]
        with self.phase_scope() as st:
            fill = self.provider_norm(st, l, 3, 4, (self.ybuf, 2))
            self.cur_x = self.xs
            tmp = [st.enter_context(nc.sbuf_tensor(_uniq(f"m1_t{i}"), [128, 384], F32)) for i in range(2)]
            ob = [st.enter_context(nc.sbuf_tensor(_uniq(f"m1_o{i}"), [128, 384], BF16)) for i in range(3)]
            Bt = [Buf(), Buf()]
            Bo = [Buf() for _ in range(3)]
            cnt = [0]

            def epi(key, n0, blocks, s0, sw, ps, Bps):
                i = cnt[0]
                cnt[0] += 1
                a, b = i % 2, i % 3
                P.actf(tmp[a][:, 0:sw], ps, AF.Relu, [Bps], [Bt[a]])
                P.tt("dve", ob[b][:, 0:sw], tmp[a][:, 0:sw], tmp[a][:, 0:sw], ALU.mult, [Bt[a]], [Bo[b]])
                t0 = blocks[0] * 128 + s0
                P.dma("sp", self.h1t[n0:n0 + 128, t0:t0 + sw], ob[b][:, 0:sw], reads=[Bo[b]], writes=[self.B(("h1", n0, t0))])
            groups = [("w1", n0, 512) for n0 in range(0, HID, 512)]
            self.gemm_g2(st, D, groups, lambda key, k0, k1, n0, w: (w1[k0:k1, n0:n0 + w], self.Bw[f"w1_{l}"]),
                         self.tiles(c.TT), fill, epi, c.TT)
        with self.phase_scope() as st:
            KC = HID // 128

            def fill2(blocks, actT, Bact):
                t0 = blocks[0] * 128
                tw = len(blocks) * 128
                for kp in range(0, KC, 32):
                    n = min(32, KC - kp)
                    rd = [self.B(("h1", (kp + cc) * 128, t0s)) for cc in range(n) for t0s in self._h1_cols(t0, tw)]
                    P.dma("sp", actT[:, kp:kp + n, 0:tw], self.h1t[kp * 128:(kp + n) * 128, t0:t0 + tw].rearrange("(c p) t -> p c t", p=128),
                          reads=rd, writes=[Bact])
            epi2 = self.store_epi(st, self.ybuf, "y")
            nts = [("w2", n0, 512) for n0 in range(0, D, 512)]
            self.gemm_g1(st, HID, nts, lambda key, k0, k1, n0, w: (w2[k0:k1, n0:n0 + w], self.Bw[f"w2_{l}"]),
                         self.tiles(c.TT2), fill2, epi2, c.TT2)
        self.pending_kind = 5

    def _h1_cols(self, t0, tw):
        c = self.cfg
        out = []
        for blocks in self.tiles(c.TT):
            b0 = blocks[0] * 128
            w = len(blocks) * 128
            for s0 in range(0, w, 384):
                a = b0 + s0
                e = a + min(384, w - s0)
                if a < t0 + tw and e > t0:
                    out.append(a)
        return out

    def hgrn2_layer(self, l):
        c = self.cfg
        P = self.P
        nc = self.nc
        D = c.D
        j = l // 2
        if not hasattr(self, "hq"):
            self.hq = self.dint("hq", [c.NT, D], F32)
            self.hv = self.dint("hv", [c.NT, D], BF16)
            self.hg = self.dint("hg", [c.NT, D], F32)
            self.hlf = [self.dint(f"hlf{i}", [c.NT, D], F32) for i in range(2)]
            self.ho = self.dint("ho", [c.NT, D], F32)
            self.st_own = self.dint("st_own", [(c.H // 4) * 128, 512], F32)
            self.st_sum = self.dint("st_sum", [(c.H // 4) * 128, 512], F32)
        wq = self.wfull[f"hqig{l}"]
        wz = self.wfull[f"hz{l}"]
        with self.phase_scope() as st:
            pend = (self.ybuf, 5) if l > 0 else None
            fill = self.provider_norm(st, l, 0, 1, pend)
            if pend:
                self.cur_x = self.xs
            sbt = lambda n, s, d: st.enter_context(nc.sbuf_tensor(_uniq(n), s, d))
            lbt = sbt("h1_lb", [128, 2, 512], F32)
            Blb = Buf()
            e_q = self.store_epi(st, self.hq, "hq", F32, AF.Silu, nbuf=2)
            e_g = self.store_epi(st, self.hg, "hg", F32, AF.Silu, nbuf=2)
            e_v = self.store_epi(st, self.hv, "hv", BF16, None, nbuf=2)
            zt = [sbt(f"h1_z{i}", [128, 512], F32) for i in range(2)]
            Bz = [Buf(), Buf()]
            zc = [0]
            cur = {"key": None}

            def epi(key, n0, width, blk, ps, Bps):
                kind, col = key
                if kind == 0:
                    return e_q(key, col, width, blk, ps, Bps)
                if kind == 1:
                    return e_v(key, col, width, blk, ps, Bps)
                if kind == 4:
                    return e_g(key, col, width, blk, ps, Bps)
                d = kind - 2
                if cur["key"] != key:
                    cur["key"] = key
                    P.dma("sp", lbt[:, 0, :], self.lbv[j, d, 0:1, col:col + 512].broadcast_to([128, 512]), reads=[self.Blbv], writes=[Blb])
                    P.dma("sp", lbt[:, 1, :], self.lbv[j, d, 1:2, col:col + 512].broadcast_to([128, 512]), reads=[self.Blbv], writes=[Blb])
                i = zc[0] % 2
                zc[0] += 1
                z = zt[i]
                P.actf(z[:], ps, AF.Exp, [Bps], [Bz[i]], scale=-1.0)
                P.ts("dve", z[:], z[:], 1.0, None, ALU.add, None, [Bz[i]], [Bz[i]])
                P.recip(z[:], z[:], [Bz[i]], [Bz[i]])
                P.tt("dve", z[:], z[:], lbt[:, 1, :], ALU.mult, [Bz[i], Blb], [Bz[i]])
                P.tt("dve", z[:], z[:], lbt[:, 0, :], ALU.add, [Bz[i], Blb], [Bz[i]])
                P.actf(z[:], z[:], AF.Ln, [Bz[i]], [Bz[i]])
                P.dma("sp", self.hlf[d][blk * 128:(blk + 1) * 128, col:col + 512], z[:], reads=[Bz[i]], writes=[self.B(("hlf", d, blk, col))])
            nts = []
            for kind in (0, 1, 2, 3, 4):
                for col in range(0, D, 512):
                    nts.append(((kind, col), col, 512))

            def wsrc(key, k0, k1, n0, w):
                kind, col = key
                if kind in (2, 3):
                    return wz[k0:k1, (kind - 2) * D + col:(kind - 2) * D + col + w], self.Bw[f"hz{l}"]
                sel = {0: 0, 1: 1, 4: 2}[kind]
                return wq[k0:k1, sel * D + col:sel * D + col + w], self.Bw[f"hqig{l}"]
            self.gemm_g1(st, D, nts, wsrc, self.tiles(c.TT), fill, epi, c.TT)
        self.check_stop(f"h1_{l}")
        groups2 = [[0, 1], [2, 3], [4, 5], [6, 7]]
        for d in (0, 1):
            with self.phase_scope() as st:
                self.hg_scan(st, d)
            self.check_stop(f"scan{d}_{l}")
            if d == 0:
                for g in range(c.H // 4):
                    P.coll("AllReduce", ALU.add, groups2, self.st_own[g * 128:(g + 1) * 128, :], self.st_sum[g * 128:(g + 1) * 128, :],
                           [self.B(("st_own", g))], [self.B(("st_sum", g))])
        wo = self.wfull[f"hout{l}"]
        with self.phase_scope() as st:
            sbt = lambda n, s, d: st.enter_context(nc.sbuf_tensor(_uniq(n), s, d))
            ot = sbt("ro_o", [128, D], F32)
            gtile = sbt("ro_g", [128, D], F32)
            gn = sbt("ro_gn", [128, c.DC], F32)
            s4 = sbt("ro_st", [128, 2], F32)
            Bo, Bg, Bgn, Bs4 = Buf(), Buf(), Buf(), Buf(strict=True)
            P.dma("sp", gn[:], self.hgnormT[j, :, :], writes=[Bgn])
            rsq = float(D) ** -0.5

            def fill(blocks, actT, Bact):
                for i, blk in enumerate(blocks):
                    r0 = blk * 128
                    P.dma("sp", ot[:], self.ho[r0:r0 + 128, :], reads=[self.B(("ho", blk, g)) for g in range(c.H // 4)], writes=[Bo])
                    P.dma("sp", gtile[:], self.hg[r0:r0 + 128, :], reads=[self.B(("hg", blk, n0)) for n0 in range(0, D, 512)], writes=[Bg])
                    P.actf(self.junk[:, 0:D], ot[:], AF.Square, [Bo], [self.Bjunk, Bs4], scale=rsq, accum=s4[:, 0:1])
                    P.actf(s4[:, 1:2], s4[:, 0:1], AF.Sqrt, [Bs4], [Bs4], bias=self.epsc[:, 0:1])
                    P.recip(s4[:, 1:2], s4[:, 1:2], [Bs4], [Bs4])
                    P.stt("dve", ot[:], ot[:], s4[:, 1:2], gtile[:], ALU.mult, ALU.mult, [Bo, Bs4, Bg], [Bo])
                    self.transpose_f32(ot, Bo, c.DC, actT, Bact, i * 128, scale_cols=gn, Bsc=Bgn)
            epi = self.store_epi(st, self.ybuf, "y")
            nts = [("wo", n0, 512) for n0 in range(0, D, 512)]
            self.gemm_g1(st, D, nts, lambda key, k0, k1, n0, w: (wo[k0:k1, n0:n0 + w], self.Bw[f"hout{l}"]),
                         self.tiles(c.TT), fill, epi, c.TT)

    def hg_scan(self, st, d):
        c = self.cfg
        P = self.P
        nc = self.nc
        sbt = lambda n, s, dt: st.enter_context(nc.sbuf_tensor(_uniq(n), s, dt))
        NG = c.H // 4
        tri = self.trif if d == 0 else self.trib
        trix = self.trifx if d == 0 else self.tribx
        mask = self.maskf if d == 0 else self.maskb
        lf = [sbt(f"sc_lf{i}", [128, 512], F32) for i in range(2)]
        lfb = [sbt(f"sc_lfb{i}", [128, 512], F32) for i in range(2)]
        Blfb = [Buf(), Buf()]
        qt = [sbt(f"sc_q{i}", [128, 512], F32) for i in range(2)]
        vt = [sbt(f"sc_v{i}", [128, 512], BF16) for i in range(2)]
        o1 = [sbt(f"sc_o1{i}", [128, 512], F32) for i in range(2)]
        Blf, Bq, Bv, Bo1 = [Buf(), Buf()], [Buf(), Buf()], [Buf(), Buf()], [Buf(), Buf()]
        E1 = sbt("sc_e1", [128, 512], F32)
        E2 = sbt("sc_e2", [128, 512], F32)
        E3 = sbt("sc_e3", [128, 512], F32)
        kk = sbt("sc_k", [128, 512], F32)
        Qt = sbt("sc_Qt", [128, 512], BF16)
        Kt = sbt("sc_Kt", [128, 512], BF16)
        Kd = sbt("sc_Kd", [128, 512], BF16)
        QtT = sbt("sc_QtT", [128, 512], BF16)
        KtT = sbt("sc_KtT", [128, 512], BF16)
        attT = sbt("sc_att", [128, 512], BF16)
        dec = sbt("sc_dec", [128, 8], F32)
        S = sbt("sc_S", [128, 512], F32)
        Sb = sbt("sc_Sb", [128, 512], BF16)
        S2 = sbt("sc_S2", [128, 512], F32)
        osb = [sbt(f"sc_os{i}", [128, 512], F32) for i in range(2)]
        BE1, BE2, BE3, Bk, BQt, BKt, BKd, BQtT, BKtT, Batt, Bdec, BS, BSb, BS2 = (Buf() for _ in range(14))
        Bos = [Buf(), Buf()]
        b_ps, r_ps, bl_ps, att_ps, o_ps, ds_ps = (self.pf[i] for i in range(6))
        Bb, Br, Bbl, Batp, Bop, Bdsp = (self.Bpf[i] for i in range(6))
        pT = self.pb[0]
        BpT = self.Bpb[0]
        pT2 = self.pb[1]
        BpT2 = self.Bpb[1]
        if d == 0:
            order = list(range(c.NB))
        else:
            order = list(range(c.NBC - 1, -1, -1)) + list(range(c.NB - 1, c.NBC - 1, -1))
        corder = (0, 1) if d == 0 else (1, 0)
        it = 0
        for g in range(NG):
            c0 = g * 512
            P.memset("dve", S[:], 0.0, [BS])
            P.memset("dve", Sb[:], 0.0, [BSb])
            for bi, blk in enumerate(order):
                if d == 1 and bi == c.NBC:
                    P.dma("sp", S[:], self.st_sum[g * 128:(g + 1) * 128, :], reads=[self.B(("st_sum", g))], writes=[BS])
                    P.dma("sp", S2[:], self.st_own[g * 128:(g + 1) * 128, :], reads=[self.B(("st_own", g))], writes=[BS2])
                    P.tt("dve", S[:], S[:], S2[:], ALU.subtract, [BS, BS2], [BS])
                    P.cp("act", Sb[:], S[:], [BS], [BSb])
                i2 = it % 2
                it += 1
                r0 = blk * 128
                P.dma("sp", lf[i2][:], self.hlf[0][r0:r0 + 128, c0:c0 + 512], reads=[self.B(("hlf", 0, blk, c0))], writes=[Blf[i2]])
                P.dma("sp", lfb[i2][:], self.hlf[1][r0:r0 + 128, c0:c0 + 512], reads=[self.B(("hlf", 1, blk, c0))], writes=[Blfb[i2]])
                P.ts("dve", lf[i2][:], lf[i2][:], self.selv[:, d:d + 1], None, ALU.mult, None, [Blf[i2]], [Blf[i2]])
                P.stt("dve", lf[i2][:], lfb[i2][:], self.selv[:, 1 - d:2 - d], lf[i2][:], ALU.mult, ALU.add, [Blf[i2], Blfb[i2]], [Blf[i2]])
                P.dma("sp", qt[i2][:], self.hq[r0:r0 + 128, c0:c0 + 512], reads=[self.B(("hq", blk, c0))], writes=[Bq[i2]])
                P.dma("sp", vt[i2][:], self.hv[r0:r0 + 128, c0:c0 + 512], reads=[self.B(("hv", blk, c0))], writes=[Bv[i2]])
                if d == 1:
                    P.dma("sp", o1[i2][:], self.ho[r0:r0 + 128, c0:c0 + 512], reads=[self.B(("ho", blk, g))], writes=[Bo1[i2]])
                L_, Q_, V_ = lf[i2], qt[i2], vt[i2]
                import os
                CUT = int(os.environ.get("SCAN_CUT", "99"))
                if CUT < 2:
                    continue
                P.mm(b_ps[:], tri[:], L_[:], True, True, [Blf[i2]], [Bb])
                P.mm(r_ps[:], trix[:], L_[:], True, True, [Blf[i2]], [Br])
                for h in range(4):
                    P.mm(bl_ps[:, 2 * h:2 * h + 2], L_[:, h * 128:(h + 1) * 128], self.ind[:], True, True, [Blf[i2]], [Bbl])
                if CUT < 3:
                    continue
                MSK = int(os.environ.get("SCAN_MSK", "127"))
                if MSK & 1:
                    P.actf(E1[:], b_ps[:], AF.Exp, [Bb], [BE1])
                if MSK & 2:
                    P.recip(E2[:], E1[:], [BE1], [BE2])
                    P.ts("dve", E2[:], E2[:], 5.0e34, None, ALU.min, None, [BE2], [BE2])
                if MSK & 8:
                    P.actf(E3[:], r_ps[:], AF.Exp, [Br], [BE3])
                if MSK & 16:
                    P.actf(kk[:], L_[:], AF.Exp, [Blf[i2]], [Bk])
                if MSK & 32:
                    P.actf(dec[:], bl_ps[:, 0:8], AF.Exp, [Bbl], [Bdec])
                if MSK & 64:
                    P.ts("dve", kk[:], kk[:], -1.0, 1.0, ALU.mult, ALU.add, [Bk], [Bk])
                if CUT < 4:
                    continue
                P.tt("dve", Qt[:], Q_[:], E1[:], ALU.mult, [Bq[i2], BE1], [BQt])
                P.tt("dve", Kt[:], kk[:], E2[:], ALU.mult, [Bk, BE2], [BKt])
                P.tt("dve", Kd[:], kk[:], E3[:], ALU.mult, [Bk, BE3], [BKd])
                if CUT < 5:
                    continue
                for h in range(4):
                    P.tr(pT[:, h * 128:(h + 1) * 128], Qt[:, h * 128:(h + 1) * 128], self.identb[:], [BQt], [BpT])
                for h in range(4):
                    P.tr(pT2[:, h * 128:(h + 1) * 128], Kt[:, h * 128:(h + 1) * 128], self.identb[:], [BKt], [BpT2])
                P.cp("act", QtT[:], pT[:, 0:512], [BpT], [BQtT])
                P.cp("dve", KtT[:], pT2[:, 0:512], [BpT2], [BKtT])
                if CUT < 6:
                    continue
                for h in range(4):
                    hs = slice(h * 128, (h + 1) * 128)
                    P.mm(att_ps[:, hs], KtT[:, hs], QtT[:, hs], True, True, [BKtT, BQtT], [Batp])
                P.tt("dve", attT[:], att_ps[:], mask[:], ALU.mult, [Batp], [Batt])
                if CUT < 7:
                    continue
                for ci in corder:
                    ps_ = slice(ci * 64, ci * 64 + 64)
                    for h in range(4):
                        hs = slice(h * 128, (h + 1) * 128)
                        tcs = slice(h * 128 + ci * 64, h * 128 + ci * 64 + 64)
                        P.mm(o_ps[ps_, hs], attT[ps_, tcs], V_[ps_, hs], True, False, [Batt, Bv[i2]], [Bop])
                        P.mm(o_ps[ps_, hs], QtT[:, tcs], Sb[:, hs], False, True, [BQtT, BSb], [Bop])
                    if CUT < 8:
                        continue
                    for h in range(4):
                        hs = slice(h * 128, (h + 1) * 128)
                        P.mm(ds_ps[:, hs], Kd[ps_, hs], V_[ps_, hs], True, True, [BKd, Bv[i2]], [Bdsp])
                    for h in range(4):
                        hs = slice(h * 128, (h + 1) * 128)
                        P.stt("dve", S[:, hs], S[:, hs], dec[:, 2 * h + ci:2 * h + ci + 1], ds_ps[:, hs], ALU.mult, ALU.add,
                              [BS, Bdec, Bdsp], [BS])
                    P.cp("act", Sb[:], S[:], [BS], [BSb])
                oj = it % 2
                if d == 0:
                    P.cp("act", osb[oj][:], o_ps[:], [Bop], [Bos[oj]])
                else:
                    P.tt("dve", osb[oj][:], o_ps[:], o1[i2][:], ALU.add, [Bop, Bo1[i2]], [Bos[oj]])
                P.dma("sp", self.ho[r0:r0 + 128, c0:c0 + 512], osb[oj][:], reads=[Bos[oj]], writes=[self.B(("ho", blk, g))])
            if d == 0:
                P.dma("sp", self.st_own[g * 128:(g + 1) * 128, :], S[:], reads=[BS], writes=[self.B(("st_own", g))])

    def mla_layer(self, l):
        c = self.cfg
        P = self.P
        nc = self.nc
        D = c.D
        j = l // 2
        last = (l == c.DEPTH - 1)
        QC, KC2 = c.QR // 128, c.KVR // 128
        if not hasattr(self, "cd"):
            self.cd = self.dint("cd", [c.NT, c.DW], F32)
            self.cqt = self.dint("cqt", [c.QR, c.NT], BF16)
            self.kvc = self.dint("kvc", [c.KVX, c.CTX], BF16)
            self.kvx = self.dint("kvx", [c.KVX, c.NTL], BF16)
            self.kvall = self.dint("kvall", [2 * c.KVX, c.NTL], BF16)
            self.oa = self.dint("oa", [c.NT, D], BF16)
        wd = self.wfull[f"mdown{l}"]
        with self.phase_scope() as st:
            fill = self.provider_norm(st, l, 0, 1, (self.ybuf, 5))
            self.cur_x = self.xs
            epi = self.store_epi(st, self.cd, "cd")
            nts = [("wd", n0, min(512, c.DW - n0)) for n0 in range(0, c.DW, 512)]
            self.gemm_g1(st, D, nts, lambda key, k0, k1, n0, w: (wd[k0:k1, n0:n0 + w], self.Bw[f"mdown{l}"]),
                         self.tiles(c.TT), fill, epi, c.TT)
        with self.phase_scope() as st:
            sbt = lambda n, s, dt: st.enter_context(nc.sbuf_tensor(_uniq(n), s, dt))
            cdt = [sbt(f"a2_cd{i}", [128, c.DW], F32) for i in range(2)]
            Bcd = [Buf(), Buf()]
            qn = sbt("a2_qn", [128, QC], F32)
            kn = sbt("a2_kn", [128, KC2], F32)
            Bn = Buf()
            P.dma("sp", qn[:], self.qnormT[j, :, :], writes=[Bn])
            P.dma("sp", kn[:], self.kvnormT[j, :, :], writes=[Bn])
            s4 = sbt("a2_st", [128, 4], F32)
            Bs4 = Buf(strict=True)
            cs = [sbt(f"a2_cs{i}", [128, 2, 64], F32) for i in range(2)]
            Bcs = [Buf(), Buf()]
            kr = sbt("a2_kr", [128, 128], F32)
            Bkr = Buf()
            outq = [sbt(f"a2_oq{i}", [128, QC, 128], BF16) for i in range(2)]
            outk = [sbt(f"a2_ok{i}", [128, KC2 + 1, 128], BF16) for i in range(2)]
            Boq, Bok = [Buf(), Buf()], [Buf(), Buf()]
            for blk in range(c.NB):
                i = blk % 2
                r0 = blk * 128
                X = cdt[i]
                P.dma("sp", X[:], self.cd[r0:r0 + 128, :], reads=[self.B(("cd", blk, n0)) for n0 in range(0, c.DW, 512)], writes=[Bcd[i]])
                P.dma("sp", cs[i][:, 0, :], self.c_cos_tok[r0:r0 + 128, :], writes=[Bcs[i]])
                P.dma("sp", cs[i][:, 1, :], self.c_sin_tok[r0:r0 + 128, :], writes=[Bcs[i]])
                for (a0, w, col) in ((0, c.QR, 0), (c.QR, c.KVR, 2)):
                    P.actf(self.junk[:, 0:w], X[:, a0:a0 + w], AF.Square, [Bcd[i]], [self.Bjunk, Bs4], scale=float(w) ** -0.5, accum=s4[:, col:col + 1])
                    P.actf(s4[:, col + 1:col + 2], s4[:, col:col + 1], AF.Sqrt, [Bs4], [Bs4], bias=self.epsc[:, 0:1])
                    P.recip(s4[:, col + 1:col + 2], s4[:, col + 1:col + 2], [Bs4], [Bs4])
                    P.ts("dve", X[:, a0:a0 + w], X[:, a0:a0 + w], s4[:, col + 1:col + 2], None, ALU.mult, None, [Bcd[i], Bs4], [Bcd[i]])
                k0 = c.QR + c.KVR
                P.tt("dve", kr[:, 0:64], X[:, k0:k0 + 64], cs[i][:, 0, :], ALU.mult, [Bcd[i], Bcs[i]], [Bkr])
                P.tt("dve", kr[:, 64:128], X[:, k0 + 64:k0 + 128], cs[i][:, 1, :], ALU.mult, [Bcd[i], Bcs[i]], [Bkr])
                P.tt("dve", kr[:, 0:64], kr[:, 0:64], kr[:, 64:128], ALU.add, [Bkr], [Bkr])
                self.transpose_f32(X, Bcd[i], QC, outq[i], Boq[i], 0, scale_cols=qn, Bsc=Bn)
                Xk = X[:, c.QR:c.QR + c.KVR]
                self._tr_cols(Xk, Bcd[i], KC2, outk[i], Bok[i], kn, Bn)
                pi = 4 + (self._trk % 2)
                self._trk += 1
                P.tr(self.pf[pi][0:64, 0:128], kr[:, 0:64], self.identf[:], [Bkr], [self.Bpf[pi]])
                P.cp("dve", outk[i][0:64, KC2, :], self.pf[pi][0:64, 0:128], [self.Bpf[pi]], [Bok[i]])
                P.dma("sp", self.cqt[:, r0:r0 + 128].rearrange("(c p) t -> p c t", p=128), outq[i][:], reads=[Boq[i]], writes=[self.B(("cqt", blk))])
                if blk < c.NBC:
                    dst, t0, key = self.kvc, r0, ("kvc", blk)
                else:
                    dst, t0, key = self.kvx, r0 - c.CTX, ("kvx", blk)
                P.dma("sp", dst[0:c.KVR, t0:t0 + 128].rearrange("(c p) t -> p c t", p=128), outk[i][:, 0:KC2, :], reads=[Bok[i]], writes=[self.B(key)])
                P.dma("sp", dst[c.KVR:c.KVR + 64, t0:t0 + 128], outk[i][0:64, KC2, :], reads=[Bok[i]], writes=[self.B(key + ("r",))])
        self.check_stop(f"a2_{l}")
        Ball = self.B(("kvall",))
        rd = []
        for blk in range(c.NBC, c.NB):
            rd += [self.B(("kvx", blk)), self.B(("kvx", blk, "r"))]
        self.Bkvall = []
        for a in range(0, c.KVX, 128):
            b_ = min(c.KVX, a + 128)
            bb = Buf()
            self.Bkvall.append(bb)
            P.coll("AllGather", ALU.bypass, [[0, 1], [2, 3], [4, 5], [6, 7]], self.kvx[a:b_, :], self.kvall[2 * a:2 * b_, :], rd, [bb])
        with self.phase_scope() as st:
            self.mla_attention(st, l, j, last)
        self.check_stop(f"a3_{l}")
        wo = self.wfull[f"mout{l}"]
        with self.phase_scope() as st:
            sbt = lambda n, s, dt: st.enter_context(nc.sbuf_tensor(_uniq(n), s, dt))
            ot = [sbt(f"a4_o{i}", [128, D], BF16) for i in range(2)]
            Bo = [Buf(), Buf()]
            cnt = [0]

            def fill(blocks, actT, Bact):
                for i, blk in enumerate(blocks):
                    r0 = blk * 128
                    k = cnt[0] % 2
                    cnt[0] += 1
                    P.dma("sp", ot[k][:], self.oa[r0:r0 + 128, :], reads=[self.B(("oa", blk, h)) for h in range(c.H)], writes=[Bo[k]])
                    self.transpose_b16(ot[k], Bo[k], c.DC, actT, Bact, i * 128)
            epi = self.store_epi(st, self.ybuf, "y")
            nts = [("wo", n0, 512) for n0 in range(0, D, 512)]
            self.gemm_g1(st, D, nts, lambda key, k0, k1, n0, w: (wo[k0:k1, n0:n0 + w], self.Bw[f"mout{l}"]),
                         self.tiles(c.TT), fill, epi, c.TT)

    def _tr_cols(self, Xk, Bx, nch, out, Bout, scale, Bsc):
        P = self.P
        for c0 in range(0, nch, 4):
            n = min(4, nch - c0)
            pi = 4 + (self._trk % 2)
            self._trk += 1
            for jj in range(n):
                P.tr(self.pf[pi][:, jj * 128:(jj + 1) * 128], Xk[:, (c0 + jj) * 128:(c0 + jj + 1) * 128], self.identf[:], [Bx], [self.Bpf[pi]])
            for jj in range(n):
                P.actf(out[:, c0 + jj, :], self.pf[pi][:, jj * 128:(jj + 1) * 128], AF.Identity, [self.Bpf[pi], Bsc], [Bout],
                       scale=scale[:, c0 + jj:c0 + jj + 1])

    def mla_attention(self, st, l, j, last):
        c = self.cfg
        P = self.P
        nc = self.nc
        sbt = lambda n, s, dt: st.enter_context(nc.sbuf_tensor(_uniq(n), s, dt))
        QC, KC2 = c.QR // 128, c.KVR // 128
        NK, NT = c.NK, c.NT
        NKB = NK // 128
        scale = float(192 ** -0.5)
        kvT = sbt("at_kvT", [128, KC2, NK], BF16)
        krT = sbt("at_krT", [64, NK], BF16)
        cqT = sbt("at_cqT", [128, QC, NT], BF16)
        cosT = sbt("at_cos", [64, NT], F32)
        sinT = sbt("at_sin", [64, NT], F32)
        Bkv, Bcq, Bcs = Buf(), Buf(), Buf()
        rdc = []
        for blk in range(c.NBC):
            rdc += [self.B(("kvc", blk)), self.B(("kvc", blk, "r"))]
        P.dma("sp", kvT[:, :, 0:c.CTX], self.kvc[0:c.KVR, :].rearrange("(c p) t -> p c t", p=128), reads=rdc, writes=[Bkv])
        P.dma("sp", krT[:, 0:c.CTX], self.kvc[c.KVR:c.KVR + 64, :], reads=rdc, writes=[Bkv])
        Ball = self.B(("kvall",))
        for r in range(2):
            t0 = c.CTX + r * c.NTL
            for ch in range(KC2):
                a = ch * 128
                P.dma("sp", kvT[:, ch, t0:t0 + c.NTL], self.kvall[2 * a + r * 128:2 * a + (r + 1) * 128, :], reads=self.Bkvall, writes=[Bkv])
            a = c.KVR
            P.dma("sp", krT[:, t0:t0 + c.NTL], self.kvall[2 * a + r * 64:2 * a + (r + 1) * 64, :], reads=self.Bkvall, writes=[Bkv])
        P.dma("sp", cqT[:], self.cqt[:, :].rearrange("(c p) t -> p c t", p=128), reads=[self.B(("cqt", blk)) for blk in range(c.NB)], writes=[Bcq])
        P.dma("sp", cosT[:], self.c_cosT[:, :], writes=[Bcs])
        P.dma("sp", sinT[:], self.c_sinT[:, :], writes=[Bcs])
        wuq = self.wfull[f"muq{l}"]
        wukv = self.wfull[f"mukv{l}"]
        wq_sb = [sbt(f"at_wq{i}", [128, QC, 256], BF16) for i in range(2)]
        wk_sb = [sbt(f"at_wk{i}", [128, KC2, 256], BF16) for i in range(2)]
        Bwq, Bwk = [Buf(), Buf()], [Buf(), Buf()]
        kT = [sbt(f"at_kT{i}", [128, NK], BF16) for i in range(2)]
        V = [sbt(f"at_V{i}", [128, NKB, 132], BF16) for i in range(2)]
        qT = [sbt(f"at_qT{i}", [128, NT], BF16) for i in range(2)]
        qrT = [sbt(f"at_qr{i}", [64, NT], BF16) for i in range(2)]
        BkT, BV, BqT, Bqr = [Buf(), Buf()], [Buf(), Buf()], [Buf(), Buf()], [Buf(), Buf()]
        for i in range(2):
            P.memset("dve", V[i][:, :, 128:129], 1.0, [BV[i]])
        t1 = sbt("at_t1", [64, 384], F32)
        t2 = sbt("at_t2", [64, 384], F32)
        Bt1, Bt2 = Buf(), Buf()
        PT = [sbt(f"at_PT{i}", [128, 512], BF16) for i in range(3)]
        BPT = [Buf() for _ in range(3)]
        rs = sbt("at_rs", [128, 4], F32)
        Brs = Buf()
        osb = [sbt(f"at_o{i}", [128, 128], BF16) for i in range(4)]
        Bosb = [Buf() for _ in range(4)]
        pproj = [self.pf[4], self.pf[5]]
        Bpproj = [self.Bpf[4], self.Bpf[5]]
        pS = [self.pf[6], self.pf[7]]
        BpS = [self.Bpf[6], self.Bpf[7]]
        pO = [self.pf[0], self.pf[1], self.pf[2], self.pf[3]]
        BpO = [self.Bpf[0], self.Bpf[1], self.Bpf[2], self.Bpf[3]]
        prot = 0
        ocnt = 0
        ptc = 0
        qtiles = []
        if not last:
            for q0 in range(0, c.CTX, 512):
                qtiles.append((q0, min(512, c.CTX - q0), list(range(c.NBC))))
        for q0 in range(c.CTX, NT, 512):
            qtiles.append((q0, min(512, NT - q0), list(range(NKB))))
        for h in range(c.H):
            i = h % 2
            P.dma("sp", wq_sb[i][:], wuq[:, h * 256:(h + 1) * 256].rearrange("(c p) n -> p c n", p=128), reads=self.Bw[f"muq{l}"], writes=[Bwq[i]])
            P.dma("sp", wk_sb[i][:], wukv[:, h * 256:(h + 1) * 256].rearrange("(c p) n -> p c n", p=128), reads=self.Bw[f"mukv{l}"], writes=[Bwk[i]])
            for k0 in range(0, NK, 512):
                w = min(512, NK - k0)
                pi = prot % 2
                prot += 1
                for kc in range(KC2):
                    P.mm(pproj[pi][:, 0:w], wk_sb[i][:, kc, 0:128], kvT[:, kc, k0:k0 + w], kc == 0, kc == KC2 - 1, [Bwk[i], Bkv], [Bpproj[pi]])
                P.cp("dve" if (prot % 2) else "act", kT[i][:, k0:k0 + w], pproj[pi][:, 0:w], [Bpproj[pi]], [BkT[i]])
            for kb0 in range(0, NKB, 4):
                n = min(4, NKB - kb0)
                pi = prot % 2
                prot += 1
                for jj in range(n):
                    kb = kb0 + jj
                    for kc in range(KC2):
                        P.mm(pproj[pi][:, jj * 128:(jj + 1) * 128], kvT[:, kc, kb * 128:(kb + 1) * 128], wk_sb[i][:, kc, 128:256],
                             kc == 0, kc == KC2 - 1, [Bwk[i], Bkv], [Bpproj[pi]])
                P.cp("dve" if (prot % 2) else "act", V[i][:, kb0:kb0 + n, 0:128], pproj[pi][:, 0:n * 128].rearrange("p (k v) -> p k v", v=128),
                     [Bpproj[pi]], [BV[i]])
            for s0 in range(0, NT, 384):
                w = min(384, NT - s0)
                pi = prot % 2
                prot += 1
                for kc in range(QC):
                    P.mm(pproj[pi][:, 0:w], wq_sb[i][:, kc, 0:128], cqT[:, kc, s0:s0 + w], kc == 0, kc == QC - 1, [Bwq[i], Bcq], [Bpproj[pi]])
                P.cp("dve" if (prot % 2) else "act", qT[i][:, s0:s0 + w], pproj[pi][:, 0:w], [Bpproj[pi]], [BqT[i]])
                pa = prot % 2
                prot += 1
                for kc in range(QC):
                    P.mm(pproj[pa][0:64, 0:w], wq_sb[i][:, kc, 128:192], cqT[:, kc, s0:s0 + w], kc == 0, kc == QC - 1, [Bwq[i], Bcq], [Bpproj[pa]])
                P.tt("dve", t1[:, 0:w], pproj[pa][0:64, 0:w], cosT[:, s0:s0 + w], ALU.mult, [Bpproj[pa], Bcs], [Bt1])
                pb_ = prot % 2
                prot += 1
                for kc in range(QC):
                    P.mm(pproj[pb_][0:64, 0:w], wq_sb[i][:, kc, 192:256], cqT[:, kc, s0:s0 + w], kc == 0, kc == QC - 1, [Bwq[i], Bcq], [Bpproj[pb_]])
                P.tt("dve", t2[:, 0:w], pproj[pb_][0:64, 0:w], sinT[:, s0:s0 + w], ALU.mult, [Bpproj[pb_], Bcs], [Bt2])
                P.tt("dve", qrT[i][:, s0:s0 + w], t1[:, 0:w], t2[:, 0:w], ALU.add, [Bt1, Bt2], [Bqr[i]])
            for (q0, wq_, kbs) in qtiles:
                nqb = wq_ // 128

                def scores(kb):
                    si = kb % 2
                    P.mm(pS[si][:, 0:wq_], kT[i][:, kb * 128:(kb + 1) * 128], qT[i][:, q0:q0 + wq_], True, False, [BkT[i], BqT[i]], [BpS[si]])
                    P.mm(pS[si][:, 0:wq_], krT[:, kb * 128:(kb + 1) * 128], qrT[i][:, q0:q0 + wq_], False, True, [Bkv, Bqr[i]], [BpS[si]])
                scores(kbs[0])
                for n_, kb in enumerate(kbs):
                    if n_ + 1 < len(kbs):
                        scores(kbs[n_ + 1])
                    si = kb % 2
                    pj = ptc % 3
                    ptc += 1
                    P.actf(PT[pj][:, 0:wq_], pS[si][:, 0:wq_], AF.Exp, [BpS[si]], [BPT[pj]], scale=scale)
                    for qb in range(nqb):
                        P.mm(pO[qb][:, 0:129], PT[pj][:, qb * 128:(qb + 1) * 128], V[i][:, kb, 0:129], n_ == 0, n_ == len(kbs) - 1,
                             [BPT[pj], BV[i]], [BpO[qb]])
                for qb in range(nqb):
                    oj = ocnt % 4
                    ocnt += 1
                    P.recip(rs[:, qb:qb + 1], pO[qb][:, 128:129], [BpO[qb]], [Brs])
                    P.actf(osb[oj][:], pO[qb][:, 0:128], AF.Identity, [BpO[qb], Brs], [Bosb[oj]], scale=rs[:, qb:qb + 1])
                    blk = (q0 // 128) + qb
                    P.dma("sp", self.oa[blk * 128:(blk + 1) * 128, h * 128:(h + 1) * 128], osb[oj][:], reads=[Bosb[oj]], writes=[self.B(("oa", blk, h))])

    def final_out(self):
        c = self.cfg
        P = self.P
        nc = self.nc
        D = c.D
        l = c.DEPTH - 1
        with self.phase_scope() as st:
            sbt = lambda n, s, dt: st.enter_context(nc.sbuf_tensor(_uniq(n), s, dt))
            xt = [sbt(f"fo_x{i}", [128, D], F32) for i in range(2)]
            yt = [sbt(f"fo_y{i}", [128, D], F32) for i in range(2)]
            gt = sbt("fo_g", [128, D], F32)
            s4 = sbt("fo_s", [128, 2], F32)
            Bx, By, Bg, Bs = [Buf(), Buf()], [Buf(), Buf()], Buf(), Buf(strict=True)
            P.dma("sp", gt[:], self.vec[l, 5, 0:1, :].broadcast_to([128, D]), reads=[self.Bvec], writes=[Bg])
            rsq = float(D) ** -0.5
            for blk in range(c.NBC, c.NB):
                i = blk % 2
                r0 = blk * 128
                P.dma("sp", xt[i][:], self.cur_x[r0:r0 + 128, :], reads=[self.B(("x", blk))], writes=[Bx[i]])
                P.dma("sp", yt[i][:], self.ybuf[r0:r0 + 128, :], reads=[self.B(("y", blk, n0)) for n0 in range(0, D, 512)], writes=[By[i]])
                P.actf(self.junk[:, 0:D], yt[i][:], AF.Square, [By[i]], [self.Bjunk, Bs], scale=rsq, accum=s4[:, 0:1])
                P.actf(s4[:, 1:2], s4[:, 0:1], AF.Sqrt, [Bs], [Bs], bias=self.epsc[:, 0:1])
                P.recip(s4[:, 1:2], s4[:, 1:2], [Bs], [Bs])
                P.stt("dve", yt[i][:], yt[i][:], s4[:, 1:2], gt[:], ALU.mult, ALU.mult, [By[i], Bs, Bg], [By[i]])
                P.tt("dve", xt[i][:], xt[i][:], yt[i][:], ALU.add, [Bx[i], By[i]], [Bx[i]])
                P.dma("sp", self.out[r0 - c.CTX:r0 - c.CTX + 128, :], xt[i][:], reads=[Bx[i]], writes=[Buf()])


ROPE_PERM = np.concatenate([np.arange(16, 32), np.arange(0, 16), np.arange(48, 64), np.arange(32, 48)])
ROPE_SIGN = np.concatenate([-np.ones(16), np.ones(16), -np.ones(16), np.ones(16)]).astype(np.float32)


def _consts(cfg, half):
    s = np.arange(128)
    same = (s[:, None] // 64) == (s[None, :] // 64)
    trif = (same & (s[:, None] <= s[None, :])).astype(np.float32)
    trifx = (same & (s[:, None] > s[None, :])).astype(np.float32)
    ind = np.stack([(s < 64), (s >= 64)], 1).astype(np.float32)
    NTL = cfg.NTL
    loc = np.arange(NTL)
    pos = loc if half == 0 else (2 * NTL - 1 - loc)
    row = (pos // cfg.GRID_W).astype(np.float32)
    col = (pos % cfg.GRID_W).astype(np.float32)
    inv = (np.float32(10000.0) ** (-(np.arange(0, 32, 2, dtype=np.float32) / np.float32(32)))).astype(np.float32)
    ar = row[:, None] * inv
    ac = col[:, None] * inv
    ang = np.concatenate([ar, ar, ac, ac], -1).astype(np.float32)
    cos = np.concatenate([np.ones((cfg.CTX, 64), np.float32), np.cos(ang).astype(np.float32)], 0)
    sin = np.concatenate([np.zeros((cfg.CTX, 64), np.float32), np.sin(ang).astype(np.float32) * ROPE_SIGN[None, :]], 0)
    return {"c_ident": np.eye(128, dtype=np.float32), "c_trif": trif, "c_trifx": trifx, "c_ind": ind,
            "c_cos_tok": np.ascontiguousarray(cos), "c_sin_tok": np.ascontiguousarray(sin),
            "c_cosT": np.ascontiguousarray(cos.T), "c_sinT": np.ascontiguousarray(sin.T)}


def prep_inputs(cfg, inp):
    D, L = cfg.D, cfg.DEPTH
    f = lambda a: np.ascontiguousarray(np.asarray(a, dtype=np.float32))
    x, cc, ctx, c_ctx = f(inp["x"]), f(inp["c"]), f(inp["ctx"]), f(inp["c_ctx"])
    ada_down, ada_up, ada_bias = f(inp["ada_down"]), f(inp["ada_up"]), f(inp["ada_bias"])
    w1, w2 = inp["mlp_w1"], inp["mlp_w2"]
    hin, hout = inp["hg_w_in"], inp["hg_w_out"]
    mdown, muq, mukv, mout = inp["mla_w_down"], inp["mla_w_uq"], inp["mla_w_ukv"], inp["mla_w_out"]
    gains = f(np.stack([inp["norm_mix_pre"], inp["norm_mix_post"], inp["norm_mlp_pre"], inp["norm_mlp_post"]], 0))
    fm = lambda v: np.ascontiguousarray(np.asarray(v, np.float32).reshape(v.shape[0], -1, 128).transpose(0, 2, 1))
    hgnormT, qnormT, kvnormT = fm(inp["hg_norm"]), fm(inp["mla_q_norm"]), fm(inp["mla_kv_norm"])
    lbl = f(inp["hg_lb_logits"])
    maps = []
    consts = [_consts(cfg, 0), _consts(cfg, 1)]
    H = cfg.H
    for r in range(NCORES):
        b, half = r // 2, r % 2
        m = {}
        xl = x[b, half * cfg.NTL:(half + 1) * cfg.NTL]
        cx = ctx[b]
        if half:
            xl, cx = xl[::-1], cx[::-1]
        m["x_loc"] = np.ascontiguousarray(np.concatenate([cx, xl], 0))
        cond = np.stack([cc[b], c_ctx], 0)
        m["condT"] = np.ascontiguousarray(cond.reshape(2, cfg.DC, 128).transpose(2, 1, 0))
        r4 = r % 4
        def shw(a, bf=True):
            a = np.asarray(a)
            Kf, Nf = a.shape
            cr = chunk_rows(Kf, Nf, 2 if bf else 4)
            return np.ascontiguousarray(a.reshape(Kf // (4 * cr), 4, cr, Nf)[:, r4].reshape(Kf // 4, Nf), dtype=np.float32)
        sh8 = shw
        for l in range(L):
            m[f"adown{l}"] = shw(ada_down[l], False)
            m[f"aup{l}"] = shw(ada_up[l], False)
            m[f"w1_{l}"] = sh8(w1[l])
            m[f"w2_{l}"] = sh8(w2[l])
            j = l // 2
            if l % 2 == 0:
                wi = np.asarray(hin[j])
                m[f"hqig{l}"] = shw(np.concatenate([wi[:, 0:D], wi[:, D:2 * D], wi[:, 4 * D:5 * D]], 1))
                m[f"hz{l}"] = shw(wi[:, 2 * D:4 * D])
                m[f"hout{l}"] = sh8(hout[j])
            else:
                wd = np.asarray(mdown[j])
                kr = wd[:, cfg.QR + cfg.KVR:cfg.QR + cfg.KVR + 64]
                m[f"mdown{l}"] = shw(np.concatenate([wd, kr[:, ROPE_PERM]], 1))
                wq = np.asarray(muq[j]).reshape(-1, H, 192)
                m[f"muq{l}"] = shw(np.concatenate([wq, wq[:, :, 128:][:, :, ROPE_PERM]], 2).reshape(-1, H * 256))
                m[f"mukv{l}"] = sh8(mukv[j])
                m[f"mout{l}"] = sh8(mout[j])
        m["abias"] = ada_bias
        m["gains"] = gains
        m["lblog"] = lbl
        m["c_sel"] = np.tile(np.array([[1.0, 0.0]] if half == 0 else [[0.0, 1.0]], np.float32), (128, 1))
        m["hgnormT"], m["qnormT"], m["kvnormT"] = hgnormT, qnormT, kvnormT
        m.update(consts[half])
        maps.append(m)
    return maps


_NC_CACHE = {}


def run_cfg(cfg, inp, stop=None, dbg=(), raw=False):
    key = (cfg.D, cfg.HID, cfg.NTL, cfg.CTX, cfg.QR, cfg.KVR, cfg.ADAR, cfg.DEPTH, cfg.TT, cfg.TT2, stop, tuple(dbg))
    if key not in _NC_CACHE:
        _NC_CACHE[key] = K(cfg, stop, dbg).build()
    nc = _NC_CACHE[key]
    maps = prep_inputs(cfg, inp)
    res = run_bass_kernel_spmd(nc, maps, core_ids=list(range(NCORES)))
    if raw:
        return res.results
    B = NCORES // 2
    out = np.empty((B, 2 * cfg.NTL, cfg.D), np.float32)
    for r in range(NCORES):
        b, half = r // 2, r % 2
        o = np.asarray(res.results[r]["out"], dtype=np.float32)
        out[b, half * cfg.NTL:(half + 1) * cfg.NTL] = o[::-1] if half else o
    return out


def kernel(**inputs):
    return run_cfg(Cfg(), inputs)
```

```python
import contextlib
import numpy as np
import ml_dtypes
import concourse.bass as bass
import concourse.mybir as mybir
from concourse.bass_utils import run_bass_kernel_spmd

F32 = mybir.dt.float32
BF16 = mybir.dt.bfloat16
AF = mybir.ActivationFunctionType
ALU = mybir.AluOpType

NDSEM = 32
NPSEM = 8
NCSEM = 16
NCORES = 8
EPS = 1e-6


import os
_CAP = int(os.environ.get('KCAP', str(768 * 1024)))


def chunk_rows(K, N, nbytes, cap=_CAP):
    cr = K // 4
    while cr > 1 and cr * N * nbytes > cap:
        cr //= 2
    return cr


_UNIQ = [0]


def _uniq(n):
    _UNIQ[0] += 1
    return f"sb_{n}_{_UNIQ[0]}"


class Cfg:
    def __init__(self, D=4096, HID=16384, NTL=2048, CTX=256, QR=1024, KVR=512, ADAR=256, DEPTH=4,
                 TT=768, TT2=384, GRID_W=64):
        self.D, self.HID, self.NTL, self.CTX, self.QR, self.KVR, self.ADAR = D, HID, NTL, CTX, QR, KVR, ADAR
        self.DEPTH, self.TT, self.TT2, self.GRID_W = DEPTH, TT, TT2, GRID_W
        self.H = D // 128
        self.NT = NTL + CTX
        self.NB = self.NT // 128
        self.NBC = CTX // 128
        self.NK = CTX + 2 * NTL
        self.DC = D // 128
        self.NA = (DEPTH + 1) // 2
        self.NBL = DEPTH // 2
        self.DW = QR + KVR + 128
        self.KVX = KVR + 64


class Buf:
    __slots__ = ("w", "r", "const", "strict")

    def __init__(self, const=False, strict=False):
        self.w = None
        self.r = []
        self.const = const
        self.strict = strict


class Op:
    __slots__ = ("eng", "fn", "kind", "deps", "need_inc", "ms", "sem", "val", "phase", "sbuf")

    def __init__(self, eng, fn, kind, phase, sbuf):
        self.eng, self.fn, self.kind, self.phase, self.sbuf = eng, fn, kind, phase, sbuf
        self.deps = []
        self.need_inc = False
        self.ms = 0
        self.sem = None
        self.val = 0


class Prog:
    NAMES = ["pe", "act", "dve", "pool", "sp"]

    def __init__(self, nc, st):
        self.nc = nc
        self.pending = []
        self.phase = 0
        self.dma_n = 0
        self.pdma_n = 0
        self.cc_n = 0
        self.dcount = [0] * NDSEM
        self.dlast = [None] * NDSEM
        self.ccount = [0] * NCSEM
        self.clast = [None] * NCSEM
        self.cnt = {n: 0 for n in self.NAMES}
        self.esem = {n: st.enter_context(nc.semaphore("es_" + n)) for n in self.NAMES}
        self.dsem = [st.enter_context(nc.semaphore(f"ds_{i}")) for i in range(NDSEM)]
        self.csem = [st.enter_context(nc.semaphore(f"cs_{i}")) for i in range(NCSEM)]
        self.block = st.enter_context(nc.Block())
        self.engobj = {"pe": nc.tensor, "act": nc.scalar, "dve": nc.vector, "pool": nc.gpsimd, "sp": nc.sync}
        self.waited = {n: {} for n in self.NAMES}
        self.last = {n: None for n in self.NAMES}
        self.phase_dmas = []
        self.n_ins = 0

    def op(self, eng, fn, reads=(), writes=(), kind="c", sbuf=True):
        o = Op(eng, fn, kind, self.phase, sbuf)
        deps = {}
        strict = set()
        for b in reads:
            if b.w is not None:
                deps[id(b.w)] = b.w
                if b.strict:
                    strict.add(id(b.w))
        for b in writes:
            if b.w is not None:
                deps[id(b.w)] = b.w
                if b.strict:
                    strict.add(id(b.w))
            for r in b.r:
                deps[id(r)] = r
        if kind == "d":
            if eng == "pool":
                k = NDSEM - NPSEM + (self.pdma_n % NPSEM)
                self.pdma_n += 1
            else:
                k = self.dma_n % (NDSEM - NPSEM)
                self.dma_n += 1
            self.dcount[k] += 16
            o.sem = ("d", k)
            o.val = self.dcount[k]
            if self.dlast[k] is not None:
                deps[id(self.dlast[k])] = self.dlast[k]
            self.dlast[k] = o
            if sbuf:
                self.phase_dmas.append(o)
        elif kind == "k":
            k = self.cc_n % NCSEM
            self.cc_n += 1
            self.ccount[k] += 1
            o.sem = ("k", k)
            o.val = self.ccount[k]
            if self.clast[k] is not None:
                deps[id(self.clast[k])] = self.clast[k]
            self.clast[k] = o
        for d in deps.values():
            if d.kind == "c":
                if d.phase < self.phase:
                    continue
                if d.eng == eng and kind == "c" and id(d) not in strict:
                    continue
                d.need_inc = True
            o.deps.append(d)
        for b in reads:
            if not b.const:
                b.r.append(o)
        for b in writes:
            b.w = o
            b.r = []
        self.pending.append(o)
        if kind == "c":
            self.last[eng] = o
        return o

    def mm(self, out, lhsT, rhs, start, stop, reads, writes):
        return self.op("pe", lambda e: e.matmul(out, lhsT, rhs, start=start, stop=stop), reads, writes)

    def tr(self, out, in_, ident, reads, writes):
        return self.op("pe", lambda e: e.transpose(out, in_, ident), reads, writes)

    def actf(self, out, in_, func, reads, writes, scale=None, bias=None, accum=None, eng="act"):
        kw = {}
        if scale is not None:
            kw["scale"] = scale
        if bias is not None:
            kw["bias"] = bias
        if accum is not None:
            kw["accum_out"] = accum
        return self.op(eng, lambda e: e.activation(out, in_, func, **kw), reads, writes)

    def tt(self, eng, out, in0, in1, op, reads, writes):
        return self.op(eng, lambda e: e.tensor_tensor(out, in0, in1, op), reads, writes)

    def ts(self, eng, out, in0, s1, s2, op0, op1, reads, writes):
        if op1 is None:
            return self.op(eng, lambda e: e.tensor_scalar(out, in0, s1, None, op0), reads, writes)
        return self.op(eng, lambda e: e.tensor_scalar(out, in0, s1, s2, op0, op1), reads, writes)

    def stt(self, eng, out, in0, scalar, in1, op0, op1, reads, writes):
        return self.op(eng, lambda e: e.scalar_tensor_tensor(out, in0, scalar, in1, op0, op1), reads, writes)

    def cp(self, eng, out, in_, reads, writes):
        if eng == "act":
            return self.op(eng, lambda e: e.activation(out, in_, AF.Copy), reads, writes)
        return self.op(eng, lambda e: e.tensor_copy(out, in_), reads, writes)

    def recip(self, out, in_, reads, writes):
        return self.op("dve", lambda e: e.reciprocal(out, in_), reads, writes)

    def memset(self, eng, ap, val, writes):
        return self.op(eng, lambda e: e.memset(ap, val), (), writes)

    def dma(self, eng, out, in_, reads=(), writes=(), sbuf=True, slow=False):
        if slow:
            return self.op(eng, lambda e: e.dma_start(out=out, in_=in_, allow_slow_non_contiguous=True), reads, writes, kind="d", sbuf=sbuf)
        return self.op(eng, lambda e: e.dma_start(out=out, in_=in_), reads, writes, kind="d", sbuf=sbuf)

    def coll(self, kind, op, groups, in_ap, out_ap, reads, writes):
        return self.op("pool", lambda e: e.collective_compute(kind, op, replica_groups=groups, ins=[in_ap], outs=[out_ap]),
                       reads, writes, kind="k", sbuf=False)

    def _semof(self, d):
        if d.kind == "c":
            return ("e", d.eng), self.esem[d.eng], d.ms
        if d.kind == "d":
            return d.sem, self.dsem[d.sem[1]], d.val
        return d.sem, self.csem[d.sem[1]], d.val

    def _emit_waits(self, eng, deps):
        E = self.engobj[eng]
        w = self.waited[eng]
        for d in deps:
            key, s, v = self._semof(d)
            if w.get(key, 0) < v:
                E.wait_ge(s, v)
                self.n_ins += 1
                w[key] = v

    def end_phase(self):
        lasts = [o for o in self.last.values() if o is not None and o.phase == self.phase]
        for o in lasts:
            o.need_inc = True
        for o in self.pending:
            if o.kind == "c" and o.need_inc:
                self.cnt[o.eng] += 1
                o.ms = self.cnt[o.eng]
        for o in self.pending:
            self._emit_waits(o.eng, o.deps)
            ins = o.fn(self.engobj[o.eng])
            self.n_ins += 1
            if o.kind == "d":
                ins.then_inc(self.dsem[o.sem[1]], 16)
            elif o.kind == "k":
                ins.then_inc(self.csem[o.sem[1]], 1)
            elif o.need_inc:
                ins.then_inc(self.esem[o.eng], 1)
            o.fn = None
        bar = lasts + self.phase_dmas
        for n in self.NAMES:
            self._emit_waits(n, [d for d in bar if not (d.kind == "c" and d.eng == n)])
        self.pending = []
        self.phase_dmas = []
        self.phase += 1

    def finish(self):
        self.end_phase()
        E = self.engobj["sp"]
        for k in range(NDSEM):
            if self.dcount[k]:
                E.wait_ge(self.dsem[k], self.dcount[k])
        for k in range(NCSEM):
            if self.ccount[k]:
                E.wait_ge(self.csem[k], self.ccount[k])


class K:
    def __init__(self, cfg, stop=None, dbg=()):
        self.cfg = cfg
        self.stop = stop
        self.dbg = list(dbg)
        self.nc = bass.Bass("TRN2", target_bir_lowering=False)
        self.ext_in = {}
        self.bufs = {}

    def check_stop(self, name):
        if self.stop == name:
            raise StopIteration

    def din(self, name, shape, dt=F32):
        t = self.nc.dram_tensor(name, list(shape), dt, kind="ExternalInput")
        self.ext_in[name] = (tuple(shape), dt)
        return t

    def dint(self, name, shape, dt):
        return self.nc.dram_tensor(name, list(shape), dt)

    def B(self, key):
        b = self.bufs.get(key)
        if b is None:
            b = self.bufs[key] = Buf()
        return b

    def build(self):
        c = self.cfg
        nc = self.nc
        with contextlib.ExitStack() as gst:
            self.P = P = Prog(nc, gst)
            self.gst = gst
            self.declare_io()
            self.alloc_global(gst)
            self.xs = self.dint("xs", [c.NT, c.D], F32)
            self.ybuf = self.dint("ybuf", [c.NT, c.D], F32)
            try:
                self.prologue()
                self.cur_x = self.x_loc
                self.pending = None
                self.check_stop("prologue")
                for l in range(c.DEPTH):
                    if l % 2 == 0:
                        self.hgrn2_layer(l)
                    else:
                        self.mla_layer(l)
                    self.check_stop(f"mix{l}")
                    self.mlp_layer(l)
                    self.check_stop(f"mlp{l}")
                self.final_out()
            except StopIteration:
                pass
            if self.dbg:
                P.end_phase()
                E = P.engobj["sp"]
                for k in range(NDSEM):
                    if P.dcount[k]:
                        E.wait_ge(P.dsem[k], P.dcount[k])
                        P.waited["sp"][("d", k)] = P.dcount[k]
                for k in range(NCSEM):
                    if P.ccount[k]:
                        E.wait_ge(P.csem[k], P.ccount[k])
                        P.waited["sp"][("k", k)] = P.ccount[k]
                for name in self.dbg:
                    if name in ("hlf0", "hlf1"):
                        t = self.hlf[int(name[-1])]
                    else:
                        t = getattr(self, name) if hasattr(self, name) else self.wfull[name]
                    shp = list(t.shape)
                    o = nc.dram_tensor("dbg_" + name, shp, t.dtype, kind="ExternalOutput")
                    if len(shp) == 2:
                        P.dma("pool", o[:, :], t[:, :], sbuf=False)
                    elif len(shp) == 3:
                        P.dma("pool", o[:, :, :], t[:, :, :], sbuf=False)
                    else:
                        P.dma("pool", o[:, :, :, :], t[:, :, :, :], sbuf=False)
            P.finish()
        return nc

    def declare_io(self):
        c = self.cfg
        D = c.D
        L = c.DEPTH
        self.x_loc = self.din("x_loc", [c.NT, D])
        self.condT = self.din("condT", [128, c.DC, 2])
        self.out = self.nc.dram_tensor("out", [c.NTL, D], F32, kind="ExternalOutput")
        self.ws = {}
        for l in range(L):
            self.ws[f"adown{l}"] = (self.din(f"adown{l}", [D // 4, c.ADAR]), [D, c.ADAR], F32, 4)
            self.ws[f"aup{l}"] = (self.din(f"aup{l}", [c.ADAR // 4, 6 * D]), [c.ADAR, 6 * D], F32, 4)
            self.ws[f"w1_{l}"] = (self.din(f"w1_{l}", [D // 4, c.HID]), [D, c.HID], BF16, 4)
            self.ws[f"w2_{l}"] = (self.din(f"w2_{l}", [c.HID // 4, D]), [c.HID, D], BF16, 4)
            if l % 2 == 0:
                self.ws[f"hqig{l}"] = (self.din(f"hqig{l}", [D // 4, 3 * D]), [D, 3 * D], BF16, 4)
                self.ws[f"hz{l}"] = (self.din(f"hz{l}", [D // 4, 2 * D]), [D, 2 * D], BF16, 4)
                self.ws[f"hout{l}"] = (self.din(f"hout{l}", [D // 4, D]), [D, D], BF16, 4)
            else:
                self.ws[f"mdown{l}"] = (self.din(f"mdown{l}", [D // 4, c.DW]), [D, c.DW], BF16, 4)
                self.ws[f"muq{l}"] = (self.din(f"muq{l}", [c.QR // 4, c.H * 256]), [c.QR, c.H * 256], BF16, 4)
                self.ws[f"mukv{l}"] = (self.din(f"mukv{l}", [c.KVR // 4, c.H * 256]), [c.KVR, c.H * 256], BF16, 4)
                self.ws[f"mout{l}"] = (self.din(f"mout{l}", [D // 4, D]), [D, D], BF16, 4)
        self.abias = self.din("abias", [L, 6 * D])
        self.gains = self.din("gains", [4, L, D])
        self.lblog = self.din("lblog", [c.NA, 2, D])
        self.hgnormT = self.din("hgnormT", [c.NA, 128, c.DC])
        self.qnormT = self.din("qnormT", [c.NBL, 128, c.QR // 128])
        self.kvnormT = self.din("kvnormT", [c.NBL, 128, c.KVR // 128])
        self.c_ident = self.din("c_ident", [128, 128])
        self.c_trif = self.din("c_trif", [128, 128])
        self.c_trifx = self.din("c_trifx", [128, 128])
        self.c_ind = self.din("c_ind", [128, 2])
        self.c_sel = self.din("c_sel", [128, 2])
        self.c_cos_tok = self.din("c_cos_tok", [c.NT, 64])
        self.c_sin_tok = self.din("c_sin_tok", [c.NT, 64])
        self.c_cosT = self.din("c_cosT", [64, c.NT])
        self.c_sinT = self.din("c_sinT", [64, c.NT])

    def alloc_global(self, st):
        nc = self.nc
        c = self.cfg
        self.pf = [st.enter_context(nc.psum_tensor(f"pf{i}", [128, 512], F32)) for i in range(8)]
        self.Bpf = [Buf() for _ in range(8)]
        self.pb = [self.pf[6][:].bitcast(BF16), self.pf[7][:].bitcast(BF16)]
        self.Bpb = [self.Bpf[6], self.Bpf[7]]
        sb = lambda n, s, d: st.enter_context(nc.sbuf_tensor(_uniq(n), s, d))
        self.identf = sb("identf", [128, 128], F32)
        self.identb = sb("identb", [128, 128], BF16)
        self.trif = sb("trif", [128, 128], F32)
        self.trifx = sb("trifx", [128, 128], F32)
        self.trib = sb("trib", [128, 128], F32)
        self.tribx = sb("tribx", [128, 128], F32)
        self.ind = sb("ind", [128, 2], F32)
        self.maskf = sb("maskf", [128, 512], F32)
        self.maskb = sb("maskb", [128, 512], F32)
        self.epsc = sb("epsc", [128, 1], F32)
        self.selv = sb("selv", [128, 2], F32)
        self.Bconst = Buf()
        P = self.P
        Bc = self.Bconst
        P.dma("sp", self.identf[:], self.c_ident[:, :], writes=[Bc])
        P.dma("sp", self.trif[:], self.c_trif[:, :], writes=[Buf()])
        P.dma("sp", self.trifx[:], self.c_trifx[:, :], writes=[Buf()])
        P.dma("sp", self.ind[:], self.c_ind[:, :], writes=[Buf()])
        P.dma("sp", self.selv[:], self.c_sel[:, :], writes=[Buf()])
        P.end_phase()
        P.cp("dve", self.identb[:], self.identf[:], [], [Bc])
        P.memset("dve", self.epsc[:], EPS, [Bc])
        P.tr(self.pf[0][:, 0:128], self.trif[:], self.identf[:], [], [self.Bpf[0]])
        P.tr(self.pf[0][:, 128:256], self.trifx[:], self.identf[:], [], [self.Bpf[0]])
        P.cp("dve", self.trib[:], self.pf[0][:, 0:128], [self.Bpf[0]], [Bc])
        P.cp("dve", self.tribx[:], self.pf[0][:, 128:256], [self.Bpf[0]], [Bc])
        for h in range(4):
            P.cp("dve", self.maskf[:, h * 128:(h + 1) * 128], self.trif[:], [], [Bc])
            P.cp("dve", self.maskb[:, h * 128:(h + 1) * 128], self.trib[:], [Bc], [Bc])
        P.end_phase()
        self.Bconst = Buf(const=True)

    def prologue(self):
        c = self.cfg
        P = self.P
        D = c.D
        self.wfull = {}
        self.Bw = {}
        g8 = [[0, 1, 2, 3], [4, 5, 6, 7]]
        order = []
        for l in range(c.DEPTH):
            order += [f"adown{l}", f"aup{l}"]
        for l in range(c.DEPTH):
            if l % 2 == 0:
                order += [f"hqig{l}", f"hz{l}", f"hout{l}"]
            else:
                order += [f"mdown{l}", f"muq{l}", f"mukv{l}", f"mout{l}"]
            order += [f"w1_{l}", f"w2_{l}"]
        self.shards = {}
        self.Bcast = {}
        self.g8 = g8
        groups = {"ada": [n for n in order if n.startswith("adown") or n.startswith("aup")]}
        for l in range(c.DEPTH):
            groups[l] = [n for n in order if not (n.startswith("adown") or n.startswith("aup")) and n.endswith(str(l))]
        self.wgroups = groups
        self.cast_group(groups["ada"])
        self.gather_group(groups["ada"])
        self.cast_group(groups[0])
        self.gather_group(groups[0])
        for l in range(1, c.DEPTH):
            self.cast_group(groups[l])
        P.end_phase()
        self.vec = self.dint("vec", [c.DEPTH, 6, 2, D], F32)
        self.Bvec = Buf()
        self.lbv = self.dint("lbv", [c.NA, 2, 2, D], F32)
        self.Blbv = Buf()
        nc = self.nc
        with contextlib.ExitStack() as st:
            sb = lambda n, s, d: st.enter_context(nc.sbuf_tensor(_uniq(n), s, d))
            cs = sb("cs", [128, c.DC, 2], F32)
            adn = sb("adn", [128, c.DC, c.ADAR], F32)
            RC = c.ADAR // 128
            tT = sb("tT", [128, RC, 2], F32)
            CW = 2048
            aup = [sb(f"aup{i}", [128, RC, CW], F32) for i in range(2)]
            modk = sb("modk", [2, D], F32)
            biask = sb("biask", [2, D], F32)
            gnk = sb("gnk", [2, D], F32)
            resk = sb("resk", [2, D], F32)
            Bcs, Badn, BtT, Bmod, Bbias, Bgn, Bres = (Buf() for _ in range(7))
            Baup = [Buf(), Buf()]
            P.dma("sp", cs[:], self.condT[:, :, :], writes=[Bcs])
            P.actf(cs[:], cs[:], AF.Silu, [Bcs], [Bcs])
            kmap = {0: (1, None), 1: (0, 0), 2: (2, 1), 3: (4, None), 4: (3, 2), 5: (5, 3)}
            npk = D // CW if D >= CW else 1
            cw = min(CW, D)
            api = 0
            for l in range(c.DEPTH):
                P.dma("sp", adn[:], self.wfull[f"adown{l}"][:, :].rearrange("(c p) r -> p c r", p=128), reads=self.Bw[f"adown{l}"], writes=[Badn])
                for rc in range(RC):
                    for kc in range(c.DC):
                        P.mm(self.pf[0][:, 0:2], adn[:, kc, rc * 128:(rc + 1) * 128], cs[:, kc, :], kc == 0, kc == c.DC - 1,
                             [Badn, Bcs], [self.Bpf[0]])
                    P.cp("dve", tT[:, rc, :], self.pf[0][:, 0:2], [self.Bpf[0]], [BtT])
                for kd in range(6):
                    outk, gi = kmap[kd]
                    P.dma("sp", biask[:], self.abias[l:l + 1, kd * D:(kd + 1) * D].broadcast_to([2, D]), writes=[Bbias])
                    if gi is not None:
                        P.dma("sp", gnk[:], self.gains[gi, l:l + 1, :].broadcast_to([2, D]), writes=[Bgn])
                    for pc in range(npk):
                        ab = api % 2
                        api += 1
                        col0 = kd * D + pc * cw
                        P.dma("sp", aup[ab][:, :, 0:cw], self.wfull[f"aup{l}"][:, col0:col0 + cw].rearrange("(c p) n -> p c n", p=128),
                              reads=self.Bw[f"aup{l}"], writes=[Baup[ab]])
                        for jj in range(cw // 512):
                            pi = 1 + (jj % 2)
                            for rc in range(RC):
                                P.mm(self.pf[pi][0:2, :], tT[:, rc, :], aup[ab][:, rc, jj * 512:(jj + 1) * 512], rc == 0, rc == RC - 1,
                                     [BtT, Baup[ab]], [self.Bpf[pi]])
                            n0 = pc * cw + jj * 512
                            P.tt("dve", modk[:, n0:n0 + 512], self.pf[pi][0:2, :], biask[:, n0:n0 + 512], ALU.add, [self.Bpf[pi], Bbias], [Bmod])
                    if kd in (1, 4):
                        P.stt("dve", resk[:], modk[:], 1.0, gnk[:], ALU.add, ALU.mult, [Bmod, Bgn], [Bres])
                    elif kd in (2, 5):
                        P.tt("dve", resk[:], modk[:], gnk[:], ALU.mult, [Bmod, Bgn], [Bres])
                    else:
                        P.cp("dve", resk[:], modk[:], [Bmod], [Bres])
                    P.dma("sp", self.vec[l, outk, :, :], resk[:], reads=[Bres], writes=[self.Bvec])
            P.end_phase()
        with contextlib.ExitStack() as st:
            sb = lambda n, s, d: st.enter_context(nc.sbuf_tensor(_uniq(n), s, d))
            lg = sb("lg", [2, c.NA, D], F32)
            ex = sb("ex", [2, c.NA, D], F32)
            sm = sb("sm", [2, D], F32)
            lbt = sb("lbt", [2, c.NA, 2, D], F32)
            Blg, Bex, Bsm, Blbt = Buf(), Buf(), Buf(), Buf()
            P.dma("sp", lg[:], self.lblog[:, :, :].rearrange("j r d -> r j d"), writes=[Blg])
            P.actf(ex[:], lg[:], AF.Exp, [Blg], [Bex])
            P.cp("dve", sm[:], ex[:, 0, :], [Bex], [Bsm])
            for j in range(1, c.NA):
                P.tt("dve", sm[:], sm[:], ex[:, j, :], ALU.add, [Bex, Bsm], [Bsm])
            P.recip(sm[:], sm[:], [Bsm], [Bsm])
            P.memset("dve", lbt[:, 0, 0, :], 0.0, [Blbt])
            P.memset("dve", lbt[:, 0, 1, :], 1.0, [Blbt])
            for j in range(1, c.NA):
                P.tt("dve", ex[:, j, :], ex[:, j, :], sm[:], ALU.mult, [Bex, Bsm], [Bex])
                if j == 1:
                    P.cp("dve", lbt[:, j, 0, :], ex[:, j, :], [Bex], [Blbt])
                else:
                    P.tt("dve", lbt[:, j, 0, :], lbt[:, j - 1, 0, :], ex[:, j, :], ALU.add, [Bex, Blbt], [Blbt])
                P.ts("dve", lbt[:, j, 1, :], lbt[:, j, 0, :], -1.0, 1.0, ALU.mult, ALU.add, [Blbt], [Blbt])
            P.dma("sp", self.lbv[:, :, :, :].rearrange("j r k d -> r j k d"), lbt[:], reads=[Blbt], writes=[self.Blbv])
            P.end_phase()

    def cast_group(self, names):
        P = self.P
        for name in names:
            src, full_shape, dt, nr = self.ws[name]
            rows = full_shape[0] // nr
            sh = self.dint(name + "_s", [rows, full_shape[1]], dt)
            self.wfull[name] = self.dint(name + "_f", full_shape, dt)
            self.shards[name] = sh
            bl = []
            step = max(1, (1 << 20) // full_shape[1])
            r0 = 0
            while r0 < rows:
                r1 = min(rows, r0 + step)
                b = Buf()
                bl.append(b)
                P.dma("pool", sh[r0:r1, :], src[r0:r1, :], writes=[b], sbuf=False)
                r0 = r1
            self.Bcast[name] = bl

    def gather_group(self, names):
        P = self.P
        for name in names:
            src, full_shape, dt, nr = self.ws[name]
            Kf, Nf = full_shape
            cr = chunk_rows(Kf, Nf, 4 if dt == F32 else 2)
            bl = []
            for i in range((Kf // 4) // cr):
                b = Buf()
                bl.append(b)
                P.coll("AllGather", ALU.bypass, self.g8, self.shards[name][i * cr:(i + 1) * cr, :],
                       self.wfull[name][i * 4 * cr:(i + 1) * 4 * cr, :], self.Bcast[name], [b])
            self.Bw[name] = bl

    def gather_next(self, l):
        if l + 1 < self.cfg.DEPTH:
            self.gather_group(self.wgroups[l + 1])

    def tiles(self, TT):
        c = self.cfg
        tb = TT // 128
        return [list(range(i, min(c.NB, i + tb))) for i in range(0, c.NB, tb)]

    def row_of(self, blk):
        return 1 if blk < self.cfg.NBC else 0

    def provider_norm(self, st, l, kindA, kindB, pend):
        c = self.cfg
        P = self.P
        nc = self.nc
        D = c.D
        sb = lambda n, s, d: st.enter_context(nc.sbuf_tensor(_uniq(n), s, d))
        xt = sb("pn_x", [128, D], F32)
        yt = sb("pn_y", [128, D], F32)
        gt = sb("pn_g", [128, D], F32) if pend else None
        ab = sb("pn_ab", [128, 2, 2, c.DC], F32)
        stt_ = sb("pn_st", [128, 4], F32)
        Bx, By, Bg, Bab, Bst = Buf(), Buf(), Buf(), Buf(), Buf(strict=True)
        for r in range(2):
            P.dma("sp", ab[:, r, 0, :], self.vec[l, kindA, r, :].rearrange("(c p) -> p c", p=128), reads=[self.Bvec], writes=[Bab], slow=True)
            P.dma("sp", ab[:, r, 1, :], self.vec[l, kindB, r, :].rearrange("(c p) -> p c", p=128), reads=[self.Bvec], writes=[Bab], slow=True)
        state = {"grow": None}
        xsrc = self.cur_x
        Bxsrc = self.B(("x",))
        xdst = self.xs
        rsq = float(D) ** -0.5

        def fill(blocks, actT, Bact):
            for i, blk in enumerate(blocks):
                row = self.row_of(blk)
                r0 = blk * 128
                P.dma("sp", xt[:], xsrc[r0:r0 + 128, :], reads=[self.B(("x", blk))], writes=[Bx])
                if pend:
                    ydram, kindG = pend
                    if state["grow"] != row:
                        P.dma("sp", gt[:], self.vec[l if kindG == 2 else l - 1, kindG, row:row + 1, :].broadcast_to([128, D]),
                              reads=[self.Bvec], writes=[Bg])
                        state["grow"] = row
                    P.dma("sp", yt[:], ydram[r0:r0 + 128, :], reads=[self.B(("y", blk, n0)) for n0 in range(0, D, 512)], writes=[By])
                    P.actf(self.junk[:, 0:D], yt[:], AF.Square, [By], [self.Bjunk, Bst], scale=rsq, accum=stt_[:, 0:1])
                    P.actf(stt_[:, 1:2], stt_[:, 0:1], AF.Sqrt, [Bst], [Bst], bias=self.epsc[:, 0:1])
                    P.recip(stt_[:, 1:2], stt_[:, 1:2], [Bst], [Bst])
                    P.stt("dve", yt[:], yt[:], stt_[:, 1:2], gt[:], ALU.mult, ALU.mult, [By, Bst, Bg], [By])
                    P.tt("dve", xt[:], xt[:], yt[:], ALU.add, [Bx, By], [Bx])
                    P.dma("sp", xdst[r0:r0 + 128, :], xt[:], reads=[Bx], writes=[self.B(("x", blk))])
                P.actf(self.junk[:, 0:D], xt[:], AF.Square, [Bx], [self.Bjunk, Bst], scale=rsq, accum=stt_[:, 2:3])
                P.actf(stt_[:, 3:4], stt_[:, 2:3], AF.Sqrt, [Bst], [Bst], bias=self.epsc[:, 0:1])
                P.recip(stt_[:, 3:4], stt_[:, 3:4], [Bst], [Bst])
                P.ts("dve", yt[:], xt[:], stt_[:, 3:4], None, ALU.mult, None, [Bx, Bst], [By])
                self.transpose_f32(yt, By, c.DC, actT, Bact, i * 128, scale_cols=ab[:, row, 0, :], bias_cols=ab[:, row, 1, :], Bsc=Bab)
        return fill

    def transpose_f32(self, src, Bsrc, nch, actT, Bact, toff, scale_cols=None, bias_cols=None, Bsc=None, dst_c0=0):
        P = self.P
        k = 0
        for c0 in range(0, nch, 4):
            n = min(4, nch - c0)
            pi = 4 + (self._trk % 2)
            self._trk += 1
            for j in range(n):
                cc = c0 + j
                P.tr(self.pf[pi][:, j * 128:(j + 1) * 128], src[:, cc * 128:(cc + 1) * 128], self.identf[:], [Bsrc], [self.Bpf[pi]])
            for j in range(n):
                cc = c0 + j
                eng = "act"
                if scale_cols is not None:
                    P.actf(actT[:, dst_c0 + cc, toff:toff + 128], self.pf[pi][:, j * 128:(j + 1) * 128], AF.Identity,
                           [self.Bpf[pi], Bsc], [Bact], scale=scale_cols[:, cc:cc + 1],
                           bias=(bias_cols[:, cc:cc + 1] if bias_cols is not None else None))
                else:
                    P.cp("act" if (pi % 2) else "dve", actT[:, dst_c0 + cc, toff:toff + 128], self.pf[pi][:, j * 128:(j + 1) * 128],
                         [self.Bpf[pi]], [Bact])
                k += 1

    def transpose_b16(self, src, Bsrc, nch, actT, Bact, toff):
        P = self.P
        for c0 in range(0, nch, 8):
            n = min(8, nch - c0)
            pi = self._trk % 2
            self._trk += 1
            for j in range(n):
                P.tr(self.pb[pi][:, j * 128:(j + 1) * 128], src[:, (c0 + j) * 128:(c0 + j + 1) * 128], self.identb[:], [Bsrc], [self.Bpb[pi]])
            P.cp("act" if (pi % 2) else "dve", actT[:, c0:c0 + n, toff:toff + 128],
                 self.pb[pi][:, 0:n * 128].rearrange("p (c t) -> p c t", t=128), [self.Bpb[pi]], [Bact])

    def gemm_g1(self, st, K, ntiles, wsrc, tiles, fill, epi, TTW, nw=2):
        c = self.cfg
        P = self.P
        nc = self.nc
        KC = K // 128
        KP = (KC + 31) // 32
        KCP = KC // KP
        sb = lambda n, s, d: st.enter_context(nc.sbuf_tensor(_uniq(n), s, d))
        actT = sb("g1_act", [128, KC, TTW], BF16)
        Bact = Buf()
        wt = [sb(f"g1_w{i}", [128, KCP, 512], BF16) for i in range(nw)]
        Bwt = [Buf() for _ in range(nw)]
        wi = 0
        rot = 0
        for blocks in tiles:
            fill(blocks, actT, Bact)
            seq = [(nt, kp) for nt in range(len(ntiles)) for kp in range(KP)]
            loaded = {}

            def load(idx):
                nonlocal wi
                nt, kp = seq[idx]
                key, n0, width = ntiles[nt]
                ap, bw = wsrc(key, kp * KCP * 128, (kp + 1) * KCP * 128, n0, width)
                j = wi % nw
                wi += 1
                P.dma("sp", wt[j][:, :, 0:width], ap.rearrange("(c p) n -> p c n", p=128), reads=bw, writes=[Bwt[j]])
                loaded[idx] = j
            load(0)
            for idx, (nt, kp) in enumerate(seq):
                if idx + 1 < len(seq):
                    load(idx + 1)
                j = loaded.pop(idx)
                key, n0, width = ntiles[nt]
                for ti, blk in enumerate(blocks):
                    if KP == 1:
                        pi = rot % 4
                        rot += 1
                    else:
                        pi = (nt % 2) * 3 + ti
                    for kc in range(KCP):
                        P.mm(self.pf[pi][:, 0:width], actT[:, kp * KCP + kc, ti * 128:(ti + 1) * 128], wt[j][:, kc, 0:width],
                             kp == 0 and kc == 0, kp == KP - 1 and kc == KCP - 1, [Bact, Bwt[j]], [self.Bpf[pi]])
                    if kp == KP - 1:
                        epi(key, n0, width, blk, self.pf[pi][:, 0:width], self.Bpf[pi])

    def gemm_g2(self, st, K, ngroups, wsrc, tiles, fill, epi, TTW, sub=384):
        P = self.P
        nc = self.nc
        KC = K // 128
        sb = lambda n, s, d: st.enter_context(nc.sbuf_tensor(_uniq(n), s, d))
        actT = sb("g2_act", [128, KC, TTW], BF16)
        Bact = Buf()
        wt = [sb(f"g2_w{i}", [128, KC, 512], BF16) for i in range(2)]
        Bwt = [Buf(), Buf()]
        wi = 0
        rot = 0
        for blocks in tiles:
            fill(blocks, actT, Bact)
            tw = len(blocks) * 128
            subs = [(s0, min(sub, tw - s0)) for s0 in range(0, tw, sub)]

            def load(g):
                nonlocal wi
                key, n0, width = ngroups[g]
                ap, bw = wsrc(key, 0, K, n0, width)
                j = wi % 2
                wi += 1
                P.dma("sp", wt[j][:, :, 0:width], ap.rearrange("(c p) n -> p c n", p=128), reads=bw, writes=[Bwt[j]])
                return j
            jn = load(0)
            for g, (key, n0, width) in enumerate(ngroups):
                j = jn
                if g + 1 < len(ngroups):
                    jn = load(g + 1)
                for nb in range(width // 128):
                    for (s0, sw) in subs:
                        pi = rot % 4
                        rot += 1
                        for kc in range(KC):
                            P.mm(self.pf[pi][:, 0:sw], wt[j][:, kc, nb * 128:(nb + 1) * 128], actT[:, kc, s0:s0 + sw],
                                 kc == 0, kc == KC - 1, [Bact, Bwt[j]], [self.Bpf[pi]])
                        epi(key, n0 + nb * 128, blocks, s0, sw, self.pf[pi][:, 0:sw], self.Bpf[pi])

    def phase_scope(self):
        k = self

        class _S:
            def __enter__(s):
                s.st = contextlib.ExitStack()
                s.st.__enter__()
                k._trk = 0
                k.junk = s.st.enter_context(k.nc.sbuf_tensor(_uniq("junk"), [128, k.cfg.D], BF16))
                k.Bjunk = Buf()
                return s.st

            def __exit__(s, *a):
                if a[0] is None:
                    k.P.end_phase()
                return s.st.__exit__(*a)
        return _S()

    def store_epi(self, st, dram, bkey, dt=F32, func=None, nbuf=3):
        P = self.P
        nc = self.nc
        ob = [st.enter_context(nc.sbuf_tensor(_uniq(f"se_{bkey}_{i}"), [128, 512], dt)) for i in range(nbuf)]
        Bo = [Buf() for _ in range(nbuf)]
        cnt = [0]

        def epi(key, n0, width, blk, ps, Bps):
            j = cnt[0] % nbuf
            cnt[0] += 1
            if func is None:
                P.cp("act" if (cnt[0] % 2) else "dve", ob[j][:, 0:width], ps, [Bps], [Bo[j]])
            else:
                P.actf(ob[j][:, 0:width], ps, func, [Bps], [Bo[j]])
            P.dma("sp", dram[blk * 128:(blk + 1) * 128, n0:n0 + width], ob[j][:, 0:width], reads=[Bo[j]],
                  writes=[self.B((bkey, blk, n0))])
        return epi

    def mlp_layer(self, l):
        c = self.cfg
        P = self.P
        nc = self.nc
        D, HID = c.D, c.HID
        if not hasattr(self, "h1t"):
            self.h1t = self.dint("h1t", [HID, c.NT], BF16)
        w1 = self.wfull[f"w1_{l}"]
        w2 = self.wfull[f"w2_{l}"]
        with self.phase_scope() as st:
            fill = self.provider_norm(st, l, 3, 4, (self.ybuf, 2))
            self.cur_x = self.xs
            tmp = [st.enter_context(nc.sbuf_tensor(_uniq(f"m1_t{i}"), [128, 384], F32)) for i in range(2)]
            ob = [st.enter_context(nc.sbuf_tensor(_uniq(f"m1_o{i}"), [128, 384], BF16)) for i in range(3)]
            Bt = [Buf(), Buf()]
            Bo = [Buf() for _ in range(3)]
            cnt = [0]

            def epi(key, n0, blocks, s0, sw, ps, Bps):
                i = cnt[0]
                cnt[0] += 1
                a, b = i % 2, i % 3
                P.actf(tmp[a][:, 0:sw], ps, AF.Relu, [Bps], [Bt[a]])
                P.tt("dve", ob[b][:, 0:sw], tmp[a][:, 0:sw], tmp[a][:, 0:sw], ALU.mult, [Bt[a]], [Bo[b]])
                t0 = blocks[0] * 128 + s0
                P.dma("sp", self.h1t[n0:n0 + 128, t0:t0 + sw], ob[b][:, 0:sw], reads=[Bo[b]], writes=[self.B(("h1", n0, t0))])
            groups = [("w1", n0, 512) for n0 in range(0, HID, 512)]
            self.gemm_g2(st, D, groups, lambda key, k0, k1, n0, w: (w1[k0:k1, n0:n0 + w], self.Bw[f"w1_{l}"]),
                         self.tiles(c.TT), fill, epi, c.TT)
        with self.phase_scope() as st:
            KC = HID // 128

            def fill2(blocks, actT, Bact):
                t0 = blocks[0] * 128
                tw = len(blocks) * 128
                for kp in range(0, KC, 32):
                    n = min(32, KC - kp)
                    rd = [self.B(("h1", (kp + cc) * 128, t0s)) for cc in range(n) for t0s in self._h1_cols(t0, tw)]
                    P.dma("sp", actT[:, kp:kp + n, 0:tw], self.h1t[kp * 128:(kp + n) * 128, t0:t0 + tw].rearrange("(c p) t -> p c t", p=128),
                          reads=rd, writes=[Bact])
            epi2 = self.store_epi(st, self.ybuf, "y")
            nts = [("w2", n0, 512) for n0 in range(0, D, 512)]
            self.gemm_g1(st, HID, nts, lambda key, k0, k1, n0, w: (w2[k0:k1, n0:n0 + w], self.Bw[f"w2_{l}"]),
                         self.tiles(c.TT2), fill2, epi2, c.TT2)
        self.pending_kind = 5

    def _h1_cols(self, t0, tw):
        c = self.cfg
        out = []
        for blocks in self.tiles(c.TT):
            b0 = blocks[0] * 128
            w = len(blocks) * 128
            for s0 in range(0, w, 384):
                a = b0 + s0
                e = a + min(384, w - s0)
                if a < t0 + tw and e > t0:
                    out.append(a)
        return out

    def hgrn2_layer(self, l):
        c = self.cfg
        P = self.P
        nc = self.nc
        D = c.D
        j = l // 2
        if not hasattr(self, "hq"):
            self.hq = self.dint("hq", [c.NT, D], F32)
            self.hv = self.dint("hv", [c.NT, D], BF16)
            self.hg = self.dint("hg", [c.NT, D], F32)
            self.hlf = [self.dint(f"hlf{i}", [c.NT, D], F32) for i in range(2)]
            self.ho = self.dint("ho", [c.NT, D], F32)
            self.st_own = self.dint("st_own", [(c.H // 4) * 128, 512], F32)
            self.st_sum = self.dint("st_sum", [(c.H // 4) * 128, 512], F32)
        wq = self.wfull[f"hqig{l}"]
        wz = self.wfull[f"hz{l}"]
        with self.phase_scope() as st:
            pend = (self.ybuf, 5) if l > 0 else None
            fill = self.provider_norm(st, l, 0, 1, pend)
            if pend:
                self.cur_x = self.xs
            sbt = lambda n, s, d: st.enter_context(nc.sbuf_tensor(_uniq(n), s, d))
            lbt = sbt("h1_lb", [128, 2, 512], F32)
            Blb = Buf()
            e_q = self.store_epi(st, self.hq, "hq", F32, AF.Silu, nbuf=2)
            e_g = self.store_epi(st, self.hg, "hg", F32, AF.Silu, nbuf=2)
            e_v = self.store_epi(st, self.hv, "hv", BF16, None, nbuf=2)
            zt = [sbt(f"h1_z{i}", [128, 512], F32) for i in range(2)]
            Bz = [Buf(), Buf()]
            zc = [0]
            cur = {"key": None}

            def epi(key, n0, width, blk, ps, Bps):
                kind, col = key
                if kind == 0:
                    return e_q(key, col, width, blk, ps, Bps)
                if kind == 1:
                    return e_v(key, col, width, blk, ps, Bps)
                if kind == 4:
                    return e_g(key, col, width, blk, ps, Bps)
                d = kind - 2
                if cur["key"] != key:
                    cur["key"] = key
                    P.dma("sp", lbt[:, 0, :], self.lbv[j, d, 0:1, col:col + 512].broadcast_to([128, 512]), reads=[self.Blbv], writes=[Blb])
                    P.dma("sp", lbt[:, 1, :], self.lbv[j, d, 1:2, col:col + 512].broadcast_to([128, 512]), reads=[self.Blbv], writes=[Blb])
                i = zc[0] % 2
                zc[0] += 1
                z = zt[i]
                P.actf(z[:], ps, AF.Exp, [Bps], [Bz[i]], scale=-1.0)
                P.ts("dve", z[:], z[:], 1.0, None, ALU.add, None, [Bz[i]], [Bz[i]])
                P.recip(z[:], z[:], [Bz[i]], [Bz[i]])
                P.tt("dve", z[:], z[:], lbt[:, 1, :], ALU.mult, [Bz[i], Blb], [Bz[i]])
                P.tt("dve", z[:], z[:], lbt[:, 0, :], ALU.add, [Bz[i], Blb], [Bz[i]])
                P.actf(z[:], z[:], AF.Ln, [Bz[i]], [Bz[i]])
                P.dma("sp", self.hlf[d][blk * 128:(blk + 1) * 128, col:col + 512], z[:], reads=[Bz[i]], writes=[self.B(("hlf", d, blk, col))])
            nts = []
            for kind in (0, 1, 2, 3, 4):
                for col in range(0, D, 512):
                    nts.append(((kind, col), col, 512))

            def wsrc(key, k0, k1, n0, w):
                kind, col = key
                if kind in (2, 3):
                    return wz[k0:k1, (kind - 2) * D + col:(kind - 2) * D + col + w], self.Bw[f"hz{l}"]
                sel = {0: 0, 1: 1, 4: 2}[kind]
                return wq[k0:k1, sel * D + col:sel * D + col + w], self.Bw[f"hqig{l}"]
            self.gemm_g1(st, D, nts, wsrc, self.tiles(c.TT), fill, epi, c.TT)
        self.check_stop(f"h1_{l}")
        groups2 = [[0, 1], [2, 3], [4, 5], [6, 7]]
        for d in (0, 1):
            with self.phase_scope() as st:
                self.hg_scan(st, d)
            self.check_stop(f"scan{d}_{l}")
            if d == 0:
                for g in range(c.H // 4):
                    P.coll("AllReduce", ALU.add, groups2, self.st_own[g * 128:(g + 1) * 128, :], self.st_sum[g * 128:(g + 1) * 128, :],
                           [self.B(("st_own", g))], [self.B(("st_sum", g))])
                self.gather_next(l)
        wo = self.wfull[f"hout{l}"]
        with self.phase_scope() as st:
            sbt = lambda n, s, d: st.enter_context(nc.sbuf_tensor(_uniq(n), s, d))
            ot = sbt("ro_o", [128, D], F32)
            gtile = sbt("ro_g", [128, D], F32)
            gn = sbt("ro_gn", [128, c.DC], F32)
            s4 = sbt("ro_st", [128, 2], F32)
            Bo, Bg, Bgn, Bs4 = Buf(), Buf(), Buf(), Buf(strict=True)
            P.dma("sp", gn[:], self.hgnormT[j, :, :], writes=[Bgn])
            rsq = float(D) ** -0.5

            def fill(blocks, actT, Bact):
                for i, blk in enumerate(blocks):
                    r0 = blk * 128
                    P.dma("sp", ot[:], self.ho[r0:r0 + 128, :], reads=[self.B(("ho", blk, g)) for g in range(c.H // 4)], writes=[Bo])
                    P.dma("sp", gtile[:], self.hg[r0:r0 + 128, :], reads=[self.B(("hg", blk, n0)) for n0 in range(0, D, 512)], writes=[Bg])
                    P.actf(self.junk[:, 0:D], ot[:], AF.Square, [Bo], [self.Bjunk, Bs4], scale=rsq, accum=s4[:, 0:1])
                    P.actf(s4[:, 1:2], s4[:, 0:1], AF.Sqrt, [Bs4], [Bs4], bias=self.epsc[:, 0:1])
                    P.recip(s4[:, 1:2], s4[:, 1:2], [Bs4], [Bs4])
                    P.stt("dve", ot[:], ot[:], s4[:, 1:2], gtile[:], ALU.mult, ALU.mult, [Bo, Bs4, Bg], [Bo])
                    self.transpose_f32(ot, Bo, c.DC, actT, Bact, i * 128, scale_cols=gn, Bsc=Bgn)
            epi = self.store_epi(st, self.ybuf, "y")
            nts = [("wo", n0, 512) for n0 in range(0, D, 512)]
            self.gemm_g1(st, D, nts, lambda key, k0, k1, n0, w: (wo[k0:k1, n0:n0 + w], self.Bw[f"hout{l}"]),
                         self.tiles(c.TT), fill, epi, c.TT)

    def hg_scan(self, st, d):
        c = self.cfg
        P = self.P
        nc = self.nc
        sbt = lambda n, s, dt: st.enter_context(nc.sbuf_tensor(_uniq(n), s, dt))
        NG = c.H // 4
        tri = self.trif if d == 0 else self.trib
        trix = self.trifx if d == 0 else self.tribx
        mask = self.maskf if d == 0 else self.maskb
        lf = [sbt(f"sc_lf{i}", [128, 512], F32) for i in range(2)]
        lfb = [sbt(f"sc_lfb{i}", [128, 512], F32) for i in range(2)]
        Blfb = [Buf(), Buf()]
        qt = [sbt(f"sc_q{i}", [128, 512], F32) for i in range(2)]
        vt = [sbt(f"sc_v{i}", [128, 512], BF16) for i in range(2)]
        o1 = [sbt(f"sc_o1{i}", [128, 512], F32) for i in range(2)]
        Blf, Bq, Bv, Bo1 = [Buf(), Buf()], [Buf(), Buf()], [Buf(), Buf()], [Buf(), Buf()]
        E1 = sbt("sc_e1", [128, 512], F32)
        E2 = sbt("sc_e2", [128, 512], F32)
        E3 = sbt("sc_e3", [128, 512], F32)
        kk = sbt("sc_k", [128, 512], F32)
        Qt = sbt("sc_Qt", [128, 512], BF16)
        Kt = sbt("sc_Kt", [128, 512], BF16)
        Kd = sbt("sc_Kd", [128, 512], BF16)
        QtT = sbt("sc_QtT", [128, 512], BF16)
        KtT = sbt("sc_KtT", [128, 512], BF16)
        attT = sbt("sc_att", [128, 512], BF16)
        dec = sbt("sc_dec", [128, 8], F32)
        S = sbt("sc_S", [128, 512], F32)
        Sb = sbt("sc_Sb", [128, 512], BF16)
        S2 = sbt("sc_S2", [128, 512], F32)
        osb = [sbt(f"sc_os{i}", [128, 512], F32) for i in range(2)]
        BE1, BE2, BE3, Bk, BQt, BKt, BKd, BQtT, BKtT, Batt, Bdec, BS, BSb, BS2 = (Buf() for _ in range(14))
        Bos = [Buf(), Buf()]
        b_ps, r_ps, bl_ps, att_ps, o_ps, ds_ps = (self.pf[i] for i in range(6))
        Bb, Br, Bbl, Batp, Bop, Bdsp = (self.Bpf[i] for i in range(6))
        pT = self.pb[0]
        BpT = self.Bpb[0]
        pT2 = self.pb[1]
        BpT2 = self.Bpb[1]
        if d == 0:
            order = list(range(c.NB))
        else:
            order = list(range(c.NBC - 1, -1, -1)) + list(range(c.NB - 1, c.NBC - 1, -1))
        corder = (0, 1) if d == 0 else (1, 0)
        it = 0
        for g in range(NG):
            c0 = g * 512
            P.memset("dve", S[:], 0.0, [BS])
            P.memset("dve", Sb[:], 0.0, [BSb])
            for bi, blk in enumerate(order):
                if d == 1 and bi == c.NBC:
                    P.dma("sp", S[:], self.st_sum[g * 128:(g + 1) * 128, :], reads=[self.B(("st_sum", g))], writes=[BS])
                    P.dma("sp", S2[:], self.st_own[g * 128:(g + 1) * 128, :], reads=[self.B(("st_own", g))], writes=[BS2])
                    P.tt("dve", S[:], S[:], S2[:], ALU.subtract, [BS, BS2], [BS])
                    P.cp("act", Sb[:], S[:], [BS], [BSb])
                i2 = it % 2
                it += 1
                r0 = blk * 128
                P.dma("sp", lf[i2][:], self.hlf[0][r0:r0 + 128, c0:c0 + 512], reads=[self.B(("hlf", 0, blk, c0))], writes=[Blf[i2]])
                P.dma("sp", lfb[i2][:], self.hlf[1][r0:r0 + 128, c0:c0 + 512], reads=[self.B(("hlf", 1, blk, c0))], writes=[Blfb[i2]])
                P.ts("dve", lf[i2][:], lf[i2][:], self.selv[:, d:d + 1], None, ALU.mult, None, [Blf[i2]], [Blf[i2]])
                P.stt("dve", lf[i2][:], lfb[i2][:], self.selv[:, 1 - d:2 - d], lf[i2][:], ALU.mult, ALU.add, [Blf[i2], Blfb[i2]], [Blf[i2]])
                P.dma("sp", qt[i2][:], self.hq[r0:r0 + 128, c0:c0 + 512], reads=[self.B(("hq", blk, c0))], writes=[Bq[i2]])
                P.dma("sp", vt[i2][:], self.hv[r0:r0 + 128, c0:c0 + 512], reads=[self.B(("hv", blk, c0))], writes=[Bv[i2]])
                if d == 1:
                    P.dma("sp", o1[i2][:], self.ho[r0:r0 + 128, c0:c0 + 512], reads=[self.B(("ho", blk, g))], writes=[Bo1[i2]])
                L_, Q_, V_ = lf[i2], qt[i2], vt[i2]
                import os
                CUT = int(os.environ.get("SCAN_CUT", "99"))
                if CUT < 2:
                    continue
                P.mm(b_ps[:], tri[:], L_[:], True, True, [Blf[i2]], [Bb])
                P.mm(r_ps[:], trix[:], L_[:], True, True, [Blf[i2]], [Br])
                for h in range(4):
                    P.mm(bl_ps[:, 2 * h:2 * h + 2], L_[:, h * 128:(h + 1) * 128], self.ind[:], True, True, [Blf[i2]], [Bbl])
                if CUT < 3:
                    continue
                MSK = int(os.environ.get("SCAN_MSK", "127"))
                if MSK & 1:
                    P.actf(E1[:], b_ps[:], AF.Exp, [Bb], [BE1])
                if MSK & 2:
                    P.recip(E2[:], E1[:], [BE1], [BE2])
                    P.ts("dve", E2[:], E2[:], 5.0e34, None, ALU.min, None, [BE2], [BE2])
                if MSK & 8:
                    P.actf(E3[:], r_ps[:], AF.Exp, [Br], [BE3])
                if MSK & 16:
                    P.actf(kk[:], L_[:], AF.Exp, [Blf[i2]], [Bk])
                if MSK & 32:
                    P.actf(dec[:], bl_ps[:, 0:8], AF.Exp, [Bbl], [Bdec])
                if MSK & 64:
                    P.ts("dve", kk[:], kk[:], -1.0, 1.0, ALU.mult, ALU.add, [Bk], [Bk])
                if CUT < 4:
                    continue
                P.tt("dve", Qt[:], Q_[:], E1[:], ALU.mult, [Bq[i2], BE1], [BQt])
                P.tt("dve", Kt[:], kk[:], E2[:], ALU.mult, [Bk, BE2], [BKt])
                P.tt("dve", Kd[:], kk[:], E3[:], ALU.mult, [Bk, BE3], [BKd])
                if CUT < 5:
                    continue
                for h in range(4):
                    P.tr(pT[:, h * 128:(h + 1) * 128], Qt[:, h * 128:(h + 1) * 128], self.identb[:], [BQt], [BpT])
                for h in range(4):
                    P.tr(pT2[:, h * 128:(h + 1) * 128], Kt[:, h * 128:(h + 1) * 128], self.identb[:], [BKt], [BpT2])
                P.cp("act", QtT[:], pT[:, 0:512], [BpT], [BQtT])
                P.cp("dve", KtT[:], pT2[:, 0:512], [BpT2], [BKtT])
                if CUT < 6:
                    continue
                for h in range(4):
                    hs = slice(h * 128, (h + 1) * 128)
                    P.mm(att_ps[:, hs], KtT[:, hs], QtT[:, hs], True, True, [BKtT, BQtT], [Batp])
                P.tt("dve", attT[:], att_ps[:], mask[:], ALU.mult, [Batp], [Batt])
                if CUT < 7:
                    continue
                for ci in corder:
                    ps_ = slice(ci * 64, ci * 64 + 64)
                    for h in range(4):
                        hs = slice(h * 128, (h + 1) * 128)
                        tcs = slice(h * 128 + ci * 64, h * 128 + ci * 64 + 64)
                        P.mm(o_ps[ps_, hs], attT[ps_, tcs], V_[ps_, hs], True, False, [Batt, Bv[i2]], [Bop])
                        P.mm(o_ps[ps_, hs], QtT[:, tcs], Sb[:, hs], False, True, [BQtT, BSb], [Bop])
                    if CUT < 8:
                        continue
                    for h in range(4):
                        hs = slice(h * 128, (h + 1) * 128)
                        P.mm(ds_ps[:, hs], Kd[ps_, hs], V_[ps_, hs], True, True, [BKd, Bv[i2]], [Bdsp])
                    for h in range(4):
                        hs = slice(h * 128, (h + 1) * 128)
                        P.stt("dve", S[:, hs], S[:, hs], dec[:, 2 * h + ci:2 * h + ci + 1], ds_ps[:, hs], ALU.mult, ALU.add,
                              [BS, Bdec, Bdsp], [BS])
                    P.cp("act", Sb[:], S[:], [BS], [BSb])
                oj = it % 2
                if d == 0:
                    P.cp("act", osb[oj][:], o_ps[:], [Bop], [Bos[oj]])
                else:
                    P.tt("dve", osb[oj][:], o_ps[:], o1[i2][:], ALU.add, [Bop, Bo1[i2]], [Bos[oj]])
                P.dma("sp", self.ho[r0:r0 + 128, c0:c0 + 512], osb[oj][:], reads=[Bos[oj]], writes=[self.B(("ho", blk, g))])
            if d == 0:
                P.dma("sp", self.st_own[g * 128:(g + 1) * 128, :], S[:], reads=[BS], writes=[self.B(("st_own", g))])

    def mla_layer(self, l):
        c = self.cfg
        P = self.P
        nc = self.nc
        D = c.D
        j = l // 2
        last = (l == c.DEPTH - 1)
        QC, KC2 = c.QR // 128, c.KVR // 128
        if not hasattr(self, "cd"):
            self.cd = self.dint("cd", [c.NT, c.DW], F32)
            self.cqt = self.dint("cqt", [c.QR, c.NT], BF16)
            self.kvc = self.dint("kvc", [c.KVX, c.CTX], BF16)
            self.kvx = self.dint("kvx", [c.KVX, c.NTL], BF16)
            self.kvall = self.dint("kvall", [2 * c.KVX, c.NTL], BF16)
            self.oa = self.dint("oa", [c.NT, D], BF16)
        wd = self.wfull[f"mdown{l}"]
        with self.phase_scope() as st:
            fill = self.provider_norm(st, l, 0, 1, (self.ybuf, 5))
            self.cur_x = self.xs
            epi = self.store_epi(st, self.cd, "cd")
            nts = [("wd", n0, min(512, c.DW - n0)) for n0 in range(0, c.DW, 512)]
            self.gemm_g1(st, D, nts, lambda key, k0, k1, n0, w: (wd[k0:k1, n0:n0 + w], self.Bw[f"mdown{l}"]),
                         self.tiles(c.TT), fill, epi, c.TT)
        with self.phase_scope() as st:
            sbt = lambda n, s, dt: st.enter_context(nc.sbuf_tensor(_uniq(n), s, dt))
            cdt = [sbt(f"a2_cd{i}", [128, c.DW], F32) for i in range(2)]
            Bcd = [Buf(), Buf()]
            qn = sbt("a2_qn", [128, QC], F32)
            kn = sbt("a2_kn", [128, KC2], F32)
            Bn = Buf()
            P.dma("sp", qn[:], self.qnormT[j, :, :], writes=[Bn])
            P.dma("sp", kn[:], self.kvnormT[j, :, :], writes=[Bn])
            s4 = sbt("a2_st", [128, 4], F32)
            Bs4 = Buf(strict=True)
            cs = [sbt(f"a2_cs{i}", [128, 2, 64], F32) for i in range(2)]
            Bcs = [Buf(), Buf()]
            kr = sbt("a2_kr", [128, 128], F32)
            Bkr = Buf()
            outq = [sbt(f"a2_oq{i}", [128, QC, 128], BF16) for i in range(2)]
            outk = [sbt(f"a2_ok{i}", [128, KC2 + 1, 128], BF16) for i in range(2)]
            Boq, Bok = [Buf(), Buf()], [Buf(), Buf()]
            for blk in range(c.NB):
                i = blk % 2
                r0 = blk * 128
                X = cdt[i]
                P.dma("sp", X[:], self.cd[r0:r0 + 128, :], reads=[self.B(("cd", blk, n0)) for n0 in range(0, c.DW, 512)], writes=[Bcd[i]])
                P.dma("sp", cs[i][:, 0, :], self.c_cos_tok[r0:r0 + 128, :], writes=[Bcs[i]])
                P.dma("sp", cs[i][:, 1, :], self.c_sin_tok[r0:r0 + 128, :], writes=[Bcs[i]])
                for (a0, w, col) in ((0, c.QR, 0), (c.QR, c.KVR, 2)):
                    P.actf(self.junk[:, 0:w], X[:, a0:a0 + w], AF.Square, [Bcd[i]], [self.Bjunk, Bs4], scale=float(w) ** -0.5, accum=s4[:, col:col + 1])
                    P.actf(s4[:, col + 1:col + 2], s4[:, col:col + 1], AF.Sqrt, [Bs4], [Bs4], bias=self.epsc[:, 0:1])
                    P.recip(s4[:, col + 1:col + 2], s4[:, col + 1:col + 2], [Bs4], [Bs4])
                    P.ts("dve", X[:, a0:a0 + w], X[:, a0:a0 + w], s4[:, col + 1:col + 2], None, ALU.mult, None, [Bcd[i], Bs4], [Bcd[i]])
                k0 = c.QR + c.KVR
                P.tt("dve", kr[:, 0:64], X[:, k0:k0 + 64], cs[i][:, 0, :], ALU.mult, [Bcd[i], Bcs[i]], [Bkr])
                P.tt("dve", kr[:, 64:128], X[:, k0 + 64:k0 + 128], cs[i][:, 1, :], ALU.mult, [Bcd[i], Bcs[i]], [Bkr])
                P.tt("dve", kr[:, 0:64], kr[:, 0:64], kr[:, 64:128], ALU.add, [Bkr], [Bkr])
                self.transpose_f32(X, Bcd[i], QC, outq[i], Boq[i], 0, scale_cols=qn, Bsc=Bn)
                Xk = X[:, c.QR:c.QR + c.KVR]
                self._tr_cols(Xk, Bcd[i], KC2, outk[i], Bok[i], kn, Bn)
                pi = 4 + (self._trk % 2)
                self._trk += 1
                P.tr(self.pf[pi][0:64, 0:128], kr[:, 0:64], self.identf[:], [Bkr], [self.Bpf[pi]])
                P.cp("dve", outk[i][0:64, KC2, :], self.pf[pi][0:64, 0:128], [self.Bpf[pi]], [Bok[i]])
                P.dma("sp", self.cqt[:, r0:r0 + 128].rearrange("(c p) t -> p c t", p=128), outq[i][:], reads=[Boq[i]], writes=[self.B(("cqt", blk))])
                if blk < c.NBC:
                    dst, t0, key = self.kvc, r0, ("kvc", blk)
                else:
                    dst, t0, key = self.kvx, r0 - c.CTX, ("kvx", blk)
                P.dma("sp", dst[0:c.KVR, t0:t0 + 128].rearrange("(c p) t -> p c t", p=128), outk[i][:, 0:KC2, :], reads=[Bok[i]], writes=[self.B(key)])
                P.dma("sp", dst[c.KVR:c.KVR + 64, t0:t0 + 128], outk[i][0:64, KC2, :], reads=[Bok[i]], writes=[self.B(key + ("r",))])
        self.check_stop(f"a2_{l}")
        Ball = self.B(("kvall",))
        rd = []
        for blk in range(c.NBC, c.NB):
            rd += [self.B(("kvx", blk)), self.B(("kvx", blk, "r"))]
        self.Bkvall = []
        for a in range(0, c.KVX, 128):
            b_ = min(c.KVX, a + 128)
            bb = Buf()
            self.Bkvall.append(bb)
            P.coll("AllGather", ALU.bypass, [[0, 1], [2, 3], [4, 5], [6, 7]], self.kvx[a:b_, :], self.kvall[2 * a:2 * b_, :], rd, [bb])
        self.gather_next(l)
        with self.phase_scope() as st:
            self.mla_attention(st, l, j, last)
        self.check_stop(f"a3_{l}")
        wo = self.wfull[f"mout{l}"]
        with self.phase_scope() as st:
            sbt = lambda n, s, dt: st.enter_context(nc.sbuf_tensor(_uniq(n), s, dt))
            ot = [sbt(f"a4_o{i}", [128, D], BF16) for i in range(2)]
            Bo = [Buf(), Buf()]
            cnt = [0]

            def fill(blocks, actT, Bact):
                for i, blk in enumerate(blocks):
                    r0 = blk * 128
                    k = cnt[0] % 2
                    cnt[0] += 1
                    P.dma("sp", ot[k][:], self.oa[r0:r0 + 128, :], reads=[self.B(("oa", blk, h)) for h in range(c.H)], writes=[Bo[k]])
                    self.transpose_b16(ot[k], Bo[k], c.DC, actT, Bact, i * 128)
            epi = self.store_epi(st, self.ybuf, "y")
            nts = [("wo", n0, 512) for n0 in range(0, D, 512)]
            self.gemm_g1(st, D, nts, lambda key, k0, k1, n0, w: (wo[k0:k1, n0:n0 + w], self.Bw[f"mout{l}"]),
                         self.tiles(c.TT), fill, epi, c.TT)

    def _tr_cols(self, Xk, Bx, nch, out, Bout, scale, Bsc):
        P = self.P
        for c0 in range(0, nch, 4):
            n = min(4, nch - c0)
            pi = 4 + (self._trk % 2)
            self._trk += 1
            for jj in range(n):
                P.tr(self.pf[pi][:, jj * 128:(jj + 1) * 128], Xk[:, (c0 + jj) * 128:(c0 + jj + 1) * 128], self.identf[:], [Bx], [self.Bpf[pi]])
            for jj in range(n):
                P.actf(out[:, c0 + jj, :], self.pf[pi][:, jj * 128:(jj + 1) * 128], AF.Identity, [self.Bpf[pi], Bsc], [Bout],
                       scale=scale[:, c0 + jj:c0 + jj + 1])

    def mla_attention(self, st, l, j, last):
        c = self.cfg
        P = self.P
        nc = self.nc
        sbt = lambda n, s, dt: st.enter_context(nc.sbuf_tensor(_uniq(n), s, dt))
        QC, KC2 = c.QR // 128, c.KVR // 128
        NK, NT = c.NK, c.NT
        NKB = NK // 128
        scale = float(192 ** -0.5)
        kvT = sbt("at_kvT", [128, KC2, NK], BF16)
        krT = sbt("at_krT", [64, NK], BF16)
        cqT = sbt("at_cqT", [128, QC, NT], BF16)
        cosT = sbt("at_cos", [64, NT], F32)
        sinT = sbt("at_sin", [64, NT], F32)
        Bkv, Bcq, Bcs = Buf(), Buf(), Buf()
        rdc = []
        for blk in range(c.NBC):
            rdc += [self.B(("kvc", blk)), self.B(("kvc", blk, "r"))]
        P.dma("sp", kvT[:, :, 0:c.CTX], self.kvc[0:c.KVR, :].rearrange("(c p) t -> p c t", p=128), reads=rdc, writes=[Bkv])
        P.dma("sp", krT[:, 0:c.CTX], self.kvc[c.KVR:c.KVR + 64, :], reads=rdc, writes=[Bkv])
        Ball = self.B(("kvall",))
        for r in range(2):
            t0 = c.CTX + r * c.NTL
            for ch in range(KC2):
                a = ch * 128
                P.dma("sp", kvT[:, ch, t0:t0 + c.NTL], self.kvall[2 * a + r * 128:2 * a + (r + 1) * 128, :], reads=self.Bkvall, writes=[Bkv])
            a = c.KVR
            P.dma("sp", krT[:, t0:t0 + c.NTL], self.kvall[2 * a + r * 64:2 * a + (r + 1) * 64, :], reads=self.Bkvall, writes=[Bkv])
        P.dma("sp", cqT[:], self.cqt[:, :].rearrange("(c p) t -> p c t", p=128), reads=[self.B(("cqt", blk)) for blk in range(c.NB)], writes=[Bcq])
        P.dma("sp", cosT[:], self.c_cosT[:, :], writes=[Bcs])
        P.dma("sp", sinT[:], self.c_sinT[:, :], writes=[Bcs])
        wuq = self.wfull[f"muq{l}"]
        wukv = self.wfull[f"mukv{l}"]
        wq_sb = [sbt(f"at_wq{i}", [128, QC, 256], BF16) for i in range(2)]
        wk_sb = [sbt(f"at_wk{i}", [128, KC2, 256], BF16) for i in range(2)]
        Bwq, Bwk = [Buf(), Buf()], [Buf(), Buf()]
        kT = [sbt(f"at_kT{i}", [128, NK], BF16) for i in range(2)]
        V = [sbt(f"at_V{i}", [128, NKB, 132], BF16) for i in range(2)]
        qT = [sbt(f"at_qT{i}", [128, NT], BF16) for i in range(2)]
        qrT = [sbt(f"at_qr{i}", [64, NT], BF16) for i in range(2)]
        BkT, BV, BqT, Bqr = [Buf(), Buf()], [Buf(), Buf()], [Buf(), Buf()], [Buf(), Buf()]
        for i in range(2):
            P.memset("dve", V[i][:, :, 128:129], 1.0, [BV[i]])
        t1 = sbt("at_t1", [64, 384], F32)
        t2 = sbt("at_t2", [64, 384], F32)
        Bt1, Bt2 = Buf(), Buf()
        PT = [sbt(f"at_PT{i}", [128, 512], BF16) for i in range(3)]
        BPT = [Buf() for _ in range(3)]
        rs = sbt("at_rs", [128, 4], F32)
        Brs = Buf()
        osb = [sbt(f"at_o{i}", [128, 128], BF16) for i in range(4)]
        Bosb = [Buf() for _ in range(4)]
        pproj = [self.pf[4], self.pf[5]]
        Bpproj = [self.Bpf[4], self.Bpf[5]]
        pS = [self.pf[6], self.pf[7]]
        BpS = [self.Bpf[6], self.Bpf[7]]
        pO = [self.pf[0], self.pf[1], self.pf[2], self.pf[3]]
        BpO = [self.Bpf[0], self.Bpf[1], self.Bpf[2], self.Bpf[3]]
        prot = 0
        ocnt = 0
        ptc = 0
        qtiles = []
        if not last:
            for q0 in range(0, c.CTX, 512):
                qtiles.append((q0, min(512, c.CTX - q0), list(range(c.NBC))))
        for q0 in range(c.CTX, NT, 512):
            qtiles.append((q0, min(512, NT - q0), list(range(NKB))))
        for h in range(c.H):
            i = h % 2
            P.dma("sp", wq_sb[i][:], wuq[:, h * 256:(h + 1) * 256].rearrange("(c p) n -> p c n", p=128), reads=self.Bw[f"muq{l}"], writes=[Bwq[i]])
            P.dma("sp", wk_sb[i][:], wukv[:, h * 256:(h + 1) * 256].rearrange("(c p) n -> p c n", p=128), reads=self.Bw[f"mukv{l}"], writes=[Bwk[i]])
            for k0 in range(0, NK, 512):
                w = min(512, NK - k0)
                pi = prot % 2
                prot += 1
                for kc in range(KC2):
                    P.mm(pproj[pi][:, 0:w], wk_sb[i][:, kc, 0:128], kvT[:, kc, k0:k0 + w], kc == 0, kc == KC2 - 1, [Bwk[i], Bkv], [Bpproj[pi]])
                P.cp("dve" if (prot % 2) else "act", kT[i][:, k0:k0 + w], pproj[pi][:, 0:w], [Bpproj[pi]], [BkT[i]])
            for kb0 in range(0, NKB, 4):
                n = min(4, NKB - kb0)
                pi = prot % 2
                prot += 1
                for jj in range(n):
                    kb = kb0 + jj
                    for kc in range(KC2):
                        P.mm(pproj[pi][:, jj * 128:(jj + 1) * 128], kvT[:, kc, kb * 128:(kb + 1) * 128], wk_sb[i][:, kc, 128:256],
                             kc == 0, kc == KC2 - 1, [Bwk[i], Bkv], [Bpproj[pi]])
                P.cp("dve" if (prot % 2) else "act", V[i][:, kb0:kb0 + n, 0:128], pproj[pi][:, 0:n * 128].rearrange("p (k v) -> p k v", v=128),
                     [Bpproj[pi]], [BV[i]])
            for s0 in range(0, NT, 384):
                w = min(384, NT - s0)
                pi = prot % 2
                prot += 1
                for kc in range(QC):
                    P.mm(pproj[pi][:, 0:w], wq_sb[i][:, kc, 0:128], cqT[:, kc, s0:s0 + w], kc == 0, kc == QC - 1, [Bwq[i], Bcq], [Bpproj[pi]])
                P.cp("dve" if (prot % 2) else "act", qT[i][:, s0:s0 + w], pproj[pi][:, 0:w], [Bpproj[pi]], [BqT[i]])
                pa = prot % 2
                prot += 1
                for kc in range(QC):
                    P.mm(pproj[pa][0:64, 0:w], wq_sb[i][:, kc, 128:192], cqT[:, kc, s0:s0 + w], kc == 0, kc == QC - 1, [Bwq[i], Bcq], [Bpproj[pa]])
                P.tt("dve", t1[:, 0:w], pproj[pa][0:64, 0:w], cosT[:, s0:s0 + w], ALU.mult, [Bpproj[pa], Bcs], [Bt1])
                pb_ = prot % 2
                prot += 1
                for kc in range(QC):
                    P.mm(pproj[pb_][0:64, 0:w], wq_sb[i][:, kc, 192:256], cqT[:, kc, s0:s0 + w], kc == 0, kc == QC - 1, [Bwq[i], Bcq], [Bpproj[pb_]])
                P.tt("dve", t2[:, 0:w], pproj[pb_][0:64, 0:w], sinT[:, s0:s0 + w], ALU.mult, [Bpproj[pb_], Bcs], [Bt2])
                P.tt("dve", qrT[i][:, s0:s0 + w], t1[:, 0:w], t2[:, 0:w], ALU.add, [Bt1, Bt2], [Bqr[i]])
            for (q0, wq_, kbs) in qtiles:
                nqb = wq_ // 128

                def scores(kb):
                    si = kb % 2
                    P.mm(pS[si][:, 0:wq_], kT[i][:, kb * 128:(kb + 1) * 128], qT[i][:, q0:q0 + wq_], True, False, [BkT[i], BqT[i]], [BpS[si]])
                    P.mm(pS[si][:, 0:wq_], krT[:, kb * 128:(kb + 1) * 128], qrT[i][:, q0:q0 + wq_], False, True, [Bkv, Bqr[i]], [BpS[si]])
                scores(kbs[0])
                for n_, kb in enumerate(kbs):
                    if n_ + 1 < len(kbs):
                        scores(kbs[n_ + 1])
                    si = kb % 2
                    pj = ptc % 3
                    ptc += 1
                    P.actf(PT[pj][:, 0:wq_], pS[si][:, 0:wq_], AF.Exp, [BpS[si]], [BPT[pj]], scale=scale)
                    for qb in range(nqb):
                        P.mm(pO[qb][:, 0:129], PT[pj][:, qb * 128:(qb + 1) * 128], V[i][:, kb, 0:129], n_ == 0, n_ == len(kbs) - 1,
                             [BPT[pj], BV[i]], [BpO[qb]])
                for qb in range(nqb):
                    oj = ocnt % 4
                    ocnt += 1
                    P.recip(rs[:, qb:qb + 1], pO[qb][:, 128:129], [BpO[qb]], [Brs])
                    P.actf(osb[oj][:], pO[qb][:, 0:128], AF.Identity, [BpO[qb], Brs], [Bosb[oj]], scale=rs[:, qb:qb + 1])
                    blk = (q0 // 128) + qb
                    P.dma("sp", self.oa[blk * 128:(blk + 1) * 128, h * 128:(h + 1) * 128], osb[oj][:], reads=[Bosb[oj]], writes=[self.B(("oa", blk, h))])

    def final_out(self):
        c = self.cfg
        P = self.P
        nc = self.nc
        D = c.D
        l = c.DEPTH - 1
        with self.phase_scope() as st:
            sbt = lambda n, s, dt: st.enter_context(nc.sbuf_tensor(_uniq(n), s, dt))
            xt = [sbt(f"fo_x{i}", [128, D], F32) for i in range(2)]
            yt = [sbt(f"fo_y{i}", [128, D], F32) for i in range(2)]
            gt = sbt("fo_g", [128, D], F32)
            s4 = sbt("fo_s", [128, 2], F32)
            Bx, By, Bg, Bs = [Buf(), Buf()], [Buf(), Buf()], Buf(), Buf(strict=True)
            P.dma("sp", gt[:], self.vec[l, 5, 0:1, :].broadcast_to([128, D]), reads=[self.Bvec], writes=[Bg])
            rsq = float(D) ** -0.5
            for blk in range(c.NBC, c.NB):
                i = blk % 2
                r0 = blk * 128
                P.dma("sp", xt[i][:], self.cur_x[r0:r0 + 128, :], reads=[self.B(("x", blk))], writes=[Bx[i]])
                P.dma("sp", yt[i][:], self.ybuf[r0:r0 + 128, :], reads=[self.B(("y", blk, n0)) for n0 in range(0, D, 512)], writes=[By[i]])
                P.actf(self.junk[:, 0:D], yt[i][:], AF.Square, [By[i]], [self.Bjunk, Bs], scale=rsq, accum=s4[:, 0:1])
                P.actf(s4[:, 1:2], s4[:, 0:1], AF.Sqrt, [Bs], [Bs], bias=self.epsc[:, 0:1])
                P.recip(s4[:, 1:2], s4[:, 1:2], [Bs], [Bs])
                P.stt("dve", yt[i][:], yt[i][:], s4[:, 1:2], gt[:], ALU.mult, ALU.mult, [By[i], Bs, Bg], [By[i]])
                P.tt("dve", xt[i][:], xt[i][:], yt[i][:], ALU.add, [Bx[i], By[i]], [Bx[i]])
                P.dma("sp", self.out[r0 - c.CTX:r0 - c.CTX + 128, :], xt[i][:], reads=[Bx[i]], writes=[Buf()])


ROPE_PERM = np.concatenate([np.arange(16, 32), np.arange(0, 16), np.arange(48, 64), np.arange(32, 48)])
ROPE_SIGN = np.concatenate([-np.ones(16), np.ones(16), -np.ones(16), np.ones(16)]).astype(np.float32)


def _consts(cfg, half):
    s = np.arange(128)
    same = (s[:, None] // 64) == (s[None, :] // 64)
    trif = (same & (s[:, None] <= s[None, :])).astype(np.float32)
    trifx = (same & (s[:, None] > s[None, :])).astype(np.float32)
    ind = np.stack([(s < 64), (s >= 64)], 1).astype(np.float32)
    NTL = cfg.NTL
    loc = np.arange(NTL)
    pos = loc if half == 0 else (2 * NTL - 1 - loc)
    row = (pos // cfg.GRID_W).astype(np.float32)
    col = (pos % cfg.GRID_W).astype(np.float32)
    inv = (np.float32(10000.0) ** (-(np.arange(0, 32, 2, dtype=np.float32) / np.float32(32)))).astype(np.float32)
    ar = row[:, None] * inv
    ac = col[:, None] * inv
    ang = np.concatenate([ar, ar, ac, ac], -1).astype(np.float32)
    cos = np.concatenate([np.ones((cfg.CTX, 64), np.float32), np.cos(ang).astype(np.float32)], 0)
    sin = np.concatenate([np.zeros((cfg.CTX, 64), np.float32), np.sin(ang).astype(np.float32) * ROPE_SIGN[None, :]], 0)
    return {"c_ident": np.eye(128, dtype=np.float32), "c_trif": trif, "c_trifx": trifx, "c_ind": ind,
            "c_cos_tok": np.ascontiguousarray(cos), "c_sin_tok": np.ascontiguousarray(sin),
            "c_cosT": np.ascontiguousarray(cos.T), "c_sinT": np.ascontiguousarray(sin.T)}


def prep_inputs(cfg, inp):
    D, L = cfg.D, cfg.DEPTH
    f = lambda a: np.ascontiguousarray(np.asarray(a, dtype=np.float32))
    x, cc, ctx, c_ctx = f(inp["x"]), f(inp["c"]), f(inp["ctx"]), f(inp["c_ctx"])
    ada_down, ada_up, ada_bias = f(inp["ada_down"]), f(inp["ada_up"]), f(inp["ada_bias"])
    w1, w2 = inp["mlp_w1"], inp["mlp_w2"]
    hin, hout = inp["hg_w_in"], inp["hg_w_out"]
    mdown, muq, mukv, mout = inp["mla_w_down"], inp["mla_w_uq"], inp["mla_w_ukv"], inp["mla_w_out"]
    gains = f(np.stack([inp["norm_mix_pre"], inp["norm_mix_post"], inp["norm_mlp_pre"], inp["norm_mlp_post"]], 0))
    fm = lambda v: np.ascontiguousarray(np.asarray(v, np.float32).reshape(v.shape[0], -1, 128).transpose(0, 2, 1))
    hgnormT, qnormT, kvnormT = fm(inp["hg_norm"]), fm(inp["mla_q_norm"]), fm(inp["mla_kv_norm"])
    lbl = f(inp["hg_lb_logits"])
    maps = []
    consts = [_consts(cfg, 0), _consts(cfg, 1)]
    H = cfg.H
    for r in range(NCORES):
        b, half = r // 2, r % 2
        m = {}
        xl = x[b, half * cfg.NTL:(half + 1) * cfg.NTL]
        cx = ctx[b]
        if half:
            xl, cx = xl[::-1], cx[::-1]
        m["x_loc"] = np.ascontiguousarray(np.concatenate([cx, xl], 0))
        cond = np.stack([cc[b], c_ctx], 0)
        m["condT"] = np.ascontiguousarray(cond.reshape(2, cfg.DC, 128).transpose(2, 1, 0))
        r4 = r % 4
        def shw(a, bf=True):
            a = np.asarray(a)
            Kf, Nf = a.shape
            cr = chunk_rows(Kf, Nf, 2 if bf else 4)
            return np.ascontiguousarray(a.reshape(Kf // (4 * cr), 4, cr, Nf)[:, r4].reshape(Kf // 4, Nf), dtype=np.float32)
        sh8 = shw
        for l in range(L):
            m[f"adown{l}"] = shw(ada_down[l], False)
            m[f"aup{l}"] = shw(ada_up[l], False)
            m[f"w1_{l}"] = sh8(w1[l])
            m[f"w2_{l}"] = sh8(w2[l])
            j = l // 2
            if l % 2 == 0:
                wi = np.asarray(hin[j])
                m[f"hqig{l}"] = shw(np.concatenate([wi[:, 0:D], wi[:, D:2 * D], wi[:, 4 * D:5 * D]], 1))
                m[f"hz{l}"] = shw(wi[:, 2 * D:4 * D])
                m[f"hout{l}"] = sh8(hout[j])
            else:
                wd = np.asarray(mdown[j])
                kr = wd[:, cfg.QR + cfg.KVR:cfg.QR + cfg.KVR + 64]
                m[f"mdown{l}"] = shw(np.concatenate([wd, kr[:, ROPE_PERM]], 1))
                wq = np.asarray(muq[j]).reshape(-1, H, 192)
                m[f"muq{l}"] = shw(np.concatenate([wq, wq[:, :, 128:][:, :, ROPE_PERM]], 2).reshape(-1, H * 256))
                m[f"mukv{l}"] = sh8(mukv[j])
                m[f"mout{l}"] = sh8(mout[j])
        m["abias"] = ada_bias
        m["gains"] = gains
        m["lblog"] = lbl
        m["c_sel"] = np.tile(np.array([[1.0, 0.0]] if half == 0 else [[0.0, 1.0]], np.float32), (128, 1))
        m["hgnormT"], m["qnormT"], m["kvnormT"] = hgnormT, qnormT, kvnormT
        m.update(consts[half])
        maps.append(m)
    return maps


_NC_CACHE = {}


def run_cfg(cfg, inp, stop=None, dbg=(), raw=False):
    key = (cfg.D, cfg.HID, cfg.NTL, cfg.CTX, cfg.QR, cfg.KVR, cfg.ADAR, cfg.DEPTH, cfg.TT, cfg.TT2, stop, tuple(dbg))
    if key not in _NC_CACHE:
        _NC_CACHE[key] = K(cfg, stop, dbg).build()
    nc = _NC_CACHE[key]
    maps = prep_inputs(cfg, inp)
    res = run_bass_kernel_spmd(nc, maps, core_ids=list(range(NCORES)))
    if raw:
        return res.results
    B = NCORES // 2
    out = np.empty((B, 2 * cfg.NTL, cfg.D), np.float32)
    for r in range(NCORES):
        b, half = r // 2, r % 2
        o = np.asarray(res.results[r]["out"], dtype=np.float32)
        out[b, half * cfg.NTL:(half + 1) * cfg.NTL] = o[::-1] if half else o
    return out


def kernel(**inputs):
    return run_cfg(Cfg(), inputs)
```

```python
import contextlib
import numpy as np
import ml_dtypes
import concourse.bass as bass
import concourse.mybir as mybir
from concourse.bass_utils import run_bass_kernel_spmd

F32 = mybir.dt.float32
BF16 = mybir.dt.bfloat16
AF = mybir.ActivationFunctionType
ALU = mybir.AluOpType

NDSEM = 32
NPSEM = 8
NCSEM = 16
NCORES = 8
EPS = 1e-6


import os
_CAP = int(os.environ.get('KCAP', str(768 * 1024)))


def chunk_rows(K, N, nbytes, cap=_CAP):
    cr = K // 4
    while cr > 1 and cr * N * nbytes > cap:
        cr //= 2
    return cr


_UNIQ = [0]


def _uniq(n):
    _UNIQ[0] += 1
    return f"sb_{n}_{_UNIQ[0]}"


class Cfg:
    def __init__(self, D=4096, HID=16384, NTL=2048, CTX=256, QR=1024, KVR=512, ADAR=256, DEPTH=4,
                 TT=768, TT2=384, GRID_W=64):
        self.D, self.HID, self.NTL, self.CTX, self.QR, self.KVR, self.ADAR = D, HID, NTL, CTX, QR, KVR, ADAR
        self.DEPTH, self.TT, self.TT2, self.GRID_W = DEPTH, TT, TT2, GRID_W
        self.H = D // 128
        self.NT = NTL + CTX
        self.NB = self.NT // 128
        self.NBC = CTX // 128
        self.NK = CTX + 2 * NTL
        self.DC = D // 128
        self.NA = (DEPTH + 1) // 2
        self.NBL = DEPTH // 2
        self.DW = QR + KVR + 128
        self.KVX = KVR + 64


class Buf:
    __slots__ = ("w", "r", "const", "strict")

    def __init__(self, const=False, strict=False):
        self.w = None
        self.r = []
        self.const = const
        self.strict = strict


class Op:
    __slots__ = ("eng", "fn", "kind", "deps", "need_inc", "ms", "sem", "val", "phase", "sbuf")

    def __init__(self, eng, fn, kind, phase, sbuf):
        self.eng, self.fn, self.kind, self.phase, self.sbuf = eng, fn, kind, phase, sbuf
        self.deps = []
        self.need_inc = False
        self.ms = 0
        self.sem = None
        self.val = 0


class Prog:
    NAMES = ["pe", "act", "dve", "pool", "sp"]

    def __init__(self, nc, st):
        self.nc = nc
        self.pending = []
        self.phase = 0
        self.dma_n = 0
        self.pdma_n = 0
        self.cc_n = 0
        self.dcount = [0] * NDSEM
        self.dlast = [None] * NDSEM
        self.ccount = [0] * NCSEM
        self.clast = [None] * NCSEM
        self.cnt = {n: 0 for n in self.NAMES}
        self.esem = {n: st.enter_context(nc.semaphore("es_" + n)) for n in self.NAMES}
        self.dsem = [st.enter_context(nc.semaphore(f"ds_{i}")) for i in range(NDSEM)]
        self.csem = [st.enter_context(nc.semaphore(f"cs_{i}")) for i in range(NCSEM)]
        self.block = st.enter_context(nc.Block())
        self.engobj = {"pe": nc.tensor, "act": nc.scalar, "dve": nc.vector, "pool": nc.gpsimd, "sp": nc.sync}
        self.waited = {n: {} for n in self.NAMES}
        self.last = {n: None for n in self.NAMES}
        self.phase_dmas = []
        self.n_ins = 0

    def op(self, eng, fn, reads=(), writes=(), kind="c", sbuf=True):
        o = Op(eng, fn, kind, self.phase, sbuf)
        deps = {}
        strict = set()
        for b in reads:
            if b.w is not None:
                deps[id(b.w)] = b.w
                if b.strict:
                    strict.add(id(b.w))
        for b in writes:
            if b.w is not None:
                deps[id(b.w)] = b.w
                if b.strict:
                    strict.add(id(b.w))
            for r in b.r:
                deps[id(r)] = r
        if kind == "d":
            if eng == "pool":
                k = NDSEM - NPSEM + (self.pdma_n % NPSEM)
                self.pdma_n += 1
            else:
                k = self.dma_n % (NDSEM - NPSEM)
                self.dma_n += 1
            self.dcount[k] += 16
            o.sem = ("d", k)
            o.val = self.dcount[k]
            if self.dlast[k] is not None:
                deps[id(self.dlast[k])] = self.dlast[k]
            self.dlast[k] = o
            if sbuf:
                self.phase_dmas.append(o)
        elif kind == "k":
            k = self.cc_n % NCSEM
            self.cc_n += 1
            self.ccount[k] += 1
            o.sem = ("k", k)
            o.val = self.ccount[k]
            if self.clast[k] is not None:
                deps[id(self.clast[k])] = self.clast[k]
            self.clast[k] = o
        for d in deps.values():
            if d.kind == "c":
                if d.phase < self.phase:
                    continue
                if d.eng == eng and kind == "c" and id(d) not in strict:
                    continue
                d.need_inc = True
            o.deps.append(d)
        for b in reads:
            if not b.const:
                b.r.append(o)
        for b in writes:
            b.w = o
            b.r = []
        self.pending.append(o)
        if kind == "c":
            self.last[eng] = o
        return o

    def mm(self, out, lhsT, rhs, start, stop, reads, writes):
        return self.op("pe", lambda e: e.matmul(out, lhsT, rhs, start=start, stop=stop), reads, writes)

    def tr(self, out, in_, ident, reads, writes):
        return self.op("pe", lambda e: e.transpose(out, in_, ident), reads, writes)

    def actf(self, out, in_, func, reads, writes, scale=None, bias=None, accum=None, eng="act"):
        kw = {}
        if scale is not None:
            kw["scale"] = scale
        if bias is not None:
            kw["bias"] = bias
        if accum is not None:
            kw["accum_out"] = accum
        return self.op(eng, lambda e: e.activation(out, in_, func, **kw), reads, writes)

    def tt(self, eng, out, in0, in1, op, reads, writes):
        return self.op(eng, lambda e: e.tensor_tensor(out, in0, in1, op), reads, writes)

    def ts(self, eng, out, in0, s1, s2, op0, op1, reads, writes):
        if op1 is None:
            return self.op(eng, lambda e: e.tensor_scalar(out, in0, s1, None, op0), reads, writes)
        return self.op(eng, lambda e: e.tensor_scalar(out, in0, s1, s2, op0, op1), reads, writes)

    def stt(self, eng, out, in0, scalar, in1, op0, op1, reads, writes):
        return self.op(eng, lambda e: e.scalar_tensor_tensor(out, in0, scalar, in1, op0, op1), reads, writes)

    def cp(self, eng, out, in_, reads, writes):
        if eng == "act":
            return self.op(eng, lambda e: e.activation(out, in_, AF.Copy), reads, writes)
        return self.op(eng, lambda e: e.tensor_copy(out, in_), reads, writes)

    def recip(self, out, in_, reads, writes):
        return self.op("dve", lambda e: e.reciprocal(out, in_), reads, writes)

    def memset(self, eng, ap, val, writes):
        return self.op(eng, lambda e: e.memset(ap, val), (), writes)

    def dma(self, eng, out, in_, reads=(), writes=(), sbuf=True, slow=False):
        if slow:
            return self.op(eng, lambda e: e.dma_start(out=out, in_=in_, allow_slow_non_contiguous=True), reads, writes, kind="d", sbuf=sbuf)
        return self.op(eng, lambda e: e.dma_start(out=out, in_=in_), reads, writes, kind="d", sbuf=sbuf)

    def coll(self, kind, op, groups, in_ap, out_ap, reads, writes):
        return self.op("pool", lambda e: e.collective_compute(kind, op, replica_groups=groups, ins=[in_ap], outs=[out_ap]),
                       reads, writes, kind="k", sbuf=False)

    def _semof(self, d):
        if d.kind == "c":
            return ("e", d.eng), self.esem[d.eng], d.ms
        if d.kind == "d":
            return d.sem, self.dsem[d.sem[1]], d.val
        return d.sem, self.csem[d.sem[1]], d.val

    def _emit_waits(self, eng, deps):
        E = self.engobj[eng]
        w = self.waited[eng]
        for d in deps:
            key, s, v = self._semof(d)
            if w.get(key, 0) < v:
                E.wait_ge(s, v)
                self.n_ins += 1
                w[key] = v

    def end_phase(self):
        lasts = [o for o in self.last.values() if o is not None and o.phase == self.phase]
        for o in lasts:
            o.need_inc = True
        for o in self.pending:
            if o.kind == "c" and o.need_inc:
                self.cnt[o.eng] += 1
                o.ms = self.cnt[o.eng]
        for o in self.pending:
            self._emit_waits(o.eng, o.deps)
            ins = o.fn(self.engobj[o.eng])
            self.n_ins += 1
            if o.kind == "d":
                ins.then_inc(self.dsem[o.sem[1]], 16)
            elif o.kind == "k":
                ins.then_inc(self.csem[o.sem[1]], 1)
            elif o.need_inc:
                ins.then_inc(self.esem[o.eng], 1)
            o.fn = None
        bar = lasts + self.phase_dmas
        for n in self.NAMES:
            self._emit_waits(n, [d for d in bar if not (d.kind == "c" and d.eng == n)])
        self.pending = []
        self.phase_dmas = []
        self.phase += 1

    def finish(self):
        self.end_phase()
        E = self.engobj["sp"]
        for k in range(NDSEM):
            if self.dcount[k]:
                E.wait_ge(self.dsem[k], self.dcount[k])
        for k in range(NCSEM):
            if self.ccount[k]:
                E.wait_ge(self.csem[k], self.ccount[k])


class K:
    def __init__(self, cfg, stop=None, dbg=()):
        self.cfg = cfg
        self.stop = stop
        self.dbg = list(dbg)
        self.nc = bass.Bass("TRN2", target_bir_lowering=False)
        self.ext_in = {}
        self.bufs = {}

    def check_stop(self, name):
        if self.stop == name:
            raise StopIteration

    def din(self, name, shape, dt=F32):
        t = self.nc.dram_tensor(name, list(shape), dt, kind="ExternalInput")
        self.ext_in[name] = (tuple(shape), dt)
        return t

    def dint(self, name, shape, dt):
        return self.nc.dram_tensor(name, list(shape), dt)

    def B(self, key):
        b = self.bufs.get(key)
        if b is None:
            b = self.bufs[key] = Buf()
        return b

    def build(self):
        c = self.cfg
        nc = self.nc
        with contextlib.ExitStack() as gst:
            self.P = P = Prog(nc, gst)
            self.gst = gst
            self.declare_io()
            self.alloc_global(gst)
            self.xs = self.dint("xs", [c.NT, c.D], F32)
            self.ybuf = self.dint("ybuf", [c.NT, c.D], F32)
            try:
                self.prologue()
                self.cur_x = self.x_loc
                self.pending = None
                self.check_stop("prologue")
                for l in range(c.DEPTH):
                    if l % 2 == 0:
                        self.hgrn2_layer(l)
                    else:
                        self.mla_layer(l)
                    self.check_stop(f"mix{l}")
                    self.mlp_layer(l)
                    self.check_stop(f"mlp{l}")
                self.final_out()
            except StopIteration:
                pass
            if self.dbg:
                P.end_phase()
                E = P.engobj["sp"]
                for k in range(NDSEM):
                    if P.dcount[k]:
                        E.wait_ge(P.dsem[k], P.dcount[k])
                        P.waited["sp"][("d", k)] = P.dcount[k]
                for k in range(NCSEM):
                    if P.ccount[k]:
                        E.wait_ge(P.csem[k], P.ccount[k])
                        P.waited["sp"][("k", k)] = P.ccount[k]
                for name in self.dbg:
                    if name in ("hlf0", "hlf1"):
                        t = self.hlf[int(name[-1])]
                    else:
                        t = getattr(self, name) if hasattr(self, name) else self.wfull[name]
                    shp = list(t.shape)
                    o = nc.dram_tensor("dbg_" + name, shp, t.dtype, kind="ExternalOutput")
                    if len(shp) == 2:
                        P.dma("pool", o[:, :], t[:, :], sbuf=False)
                    elif len(shp) == 3:
                        P.dma("pool", o[:, :, :], t[:, :, :], sbuf=False)
                    else:
                        P.dma("pool", o[:, :, :, :], t[:, :, :, :], sbuf=False)
            P.finish()
        return nc

    def declare_io(self):
        c = self.cfg
        D = c.D
        L = c.DEPTH
        self.x_loc = self.din("x_loc", [c.NT, D])
        self.condT = self.din("condT", [128, c.DC, 2])
        self.out = self.nc.dram_tensor("out", [c.NTL, D], F32, kind="ExternalOutput")
        self.ws = {}
        for l in range(L):
            self.ws[f"adown{l}"] = (self.din(f"adown{l}", [D // 4, c.ADAR]), [D, c.ADAR], F32, 4)
            self.ws[f"aup{l}"] = (self.din(f"aup{l}", [c.ADAR // 4, 6 * D]), [c.ADAR, 6 * D], F32, 4)
            self.ws[f"w1_{l}"] = (self.din(f"w1_{l}", [D // 4, c.HID]), [D, c.HID], BF16, 4)
            self.ws[f"w2_{l}"] = (self.din(f"w2_{l}", [c.HID // 4, D]), [c.HID, D], BF16, 4)
            if l % 2 == 0:
                self.ws[f"hqig{l}"] = (self.din(f"hqig{l}", [D // 4, 3 * D]), [D, 3 * D], BF16, 4)
                self.ws[f"hz{l}"] = (self.din(f"hz{l}", [D // 4, 2 * D]), [D, 2 * D], BF16, 4)
                self.ws[f"hout{l}"] = (self.din(f"hout{l}", [D // 4, D]), [D, D], BF16, 4)
            else:
                self.ws[f"mdown{l}"] = (self.din(f"mdown{l}", [D // 4, c.DW]), [D, c.DW], BF16, 4)
                self.ws[f"muq{l}"] = (self.din(f"muq{l}", [c.QR // 4, c.H * 256]), [c.QR, c.H * 256], BF16, 4)
                self.ws[f"mukv{l}"] = (self.din(f"mukv{l}", [c.KVR // 4, c.H * 256]), [c.KVR, c.H * 256], BF16, 4)
                self.ws[f"mout{l}"] = (self.din(f"mout{l}", [D // 4, D]), [D, D], BF16, 4)
        self.abias = self.din("abias", [L, 6 * D])
        self.gains = self.din("gains", [4, L, D])
        self.lblog = self.din("lblog", [c.NA, 2, D])
        self.hgnormT = self.din("hgnormT", [c.NA, 128, c.DC])
        self.qnormT = self.din("qnormT", [c.NBL, 128, c.QR // 128])
        self.kvnormT = self.din("kvnormT", [c.NBL, 128, c.KVR // 128])
        self.c_ident = self.din("c_ident", [128, 128])
        self.c_trif = self.din("c_trif", [128, 128])
        self.c_trifx = self.din("c_trifx", [128, 128])
        self.c_ind = self.din("c_ind", [128, 2])
        self.c_sel = self.din("c_sel", [128, 2])
        self.c_cos_tok = self.din("c_cos_tok", [c.NT, 64])
        self.c_sin_tok = self.din("c_sin_tok", [c.NT, 64])
        self.c_cosT = self.din("c_cosT", [64, c.NT])
        self.c_sinT = self.din("c_sinT", [64, c.NT])

    def alloc_global(self, st):
        nc = self.nc
        c = self.cfg
        self.pf = [st.enter_context(nc.psum_tensor(f"pf{i}", [128, 512], F32)) for i in range(8)]
        self.Bpf = [Buf() for _ in range(8)]
        self.pb = [self.pf[6][:].bitcast(BF16), self.pf[7][:].bitcast(BF16)]
        self.Bpb = [self.Bpf[6], self.Bpf[7]]
        sb = lambda n, s, d: st.enter_context(nc.sbuf_tensor(_uniq(n), s, d))
        self.identf = sb("identf", [128, 128], F32)
        self.identb = sb("identb", [128, 128], BF16)
        self.trif = sb("trif", [128, 128], F32)
        self.trifx = sb("trifx", [128, 128], F32)
        self.trib = sb("trib", [128, 128], F32)
        self.tribx = sb("tribx", [128, 128], F32)
        self.ind = sb("ind", [128, 2], F32)
        self.maskf = sb("maskf", [128, 512], F32)
        self.maskb = sb("maskb", [128, 512], F32)
        self.epsc = sb("epsc", [128, 1], F32)
        self.selv = sb("selv", [128, 2], F32)
        self.Bconst = Buf()
        P = self.P
        Bc = self.Bconst
        P.dma("sp", self.identf[:], self.c_ident[:, :], writes=[Bc])
        P.dma("sp", self.trif[:], self.c_trif[:, :], writes=[Buf()])
        P.dma("sp", self.trifx[:], self.c_trifx[:, :], writes=[Buf()])
        P.dma("sp", self.ind[:], self.c_ind[:, :], writes=[Buf()])
        P.dma("sp", self.selv[:], self.c_sel[:, :], writes=[Buf()])
        P.end_phase()
        P.cp("dve", self.identb[:], self.identf[:], [], [Bc])
        P.memset("dve", self.epsc[:], EPS, [Bc])
        P.tr(self.pf[0][:, 0:128], self.trif[:], self.identf[:], [], [self.Bpf[0]])
        P.tr(self.pf[0][:, 128:256], self.trifx[:], self.identf[:], [], [self.Bpf[0]])
        P.cp("dve", self.trib[:], self.pf[0][:, 0:128], [self.Bpf[0]], [Bc])
        P.cp("dve", self.tribx[:], self.pf[0][:, 128:256], [self.Bpf[0]], [Bc])
        for h in range(4):
            P.cp("dve", self.maskf[:, h * 128:(h + 1) * 128], self.trif[:], [], [Bc])
            P.cp("dve", self.maskb[:, h * 128:(h + 1) * 128], self.trib[:], [Bc], [Bc])
        P.end_phase()
        self.Bconst = Buf(const=True)

    def prologue(self):
        c = self.cfg
        P = self.P
        D = c.D
        self.wfull = {}
        self.Bw = {}
        g8 = [[0, 1, 2, 3], [4, 5, 6, 7]]
        order = []
        for l in range(c.DEPTH):
            order += [f"adown{l}", f"aup{l}"]
        for l in range(c.DEPTH):
            if l % 2 == 0:
                order += [f"hqig{l}", f"hz{l}", f"hout{l}"]
            else:
                order += [f"mdown{l}", f"muq{l}", f"mukv{l}", f"mout{l}"]
            order += [f"w1_{l}", f"w2_{l}"]
        self.shards = {}
        self.Bcast = {}
        self.g8 = g8
        groups = {"ada": [n for n in order if n.startswith("adown") or n.startswith("aup")]}
        for l in range(c.DEPTH):
            groups[l] = [n for n in order if not (n.startswith("adown") or n.startswith("aup")) and n.endswith(str(l))]
        self.wgroups = groups
        self.cast_group(groups["ada"])
        self.gather_group(groups["ada"])
        self.cast_group(groups[0])
        self.gather_group(groups[0])
        for l in range(1, c.DEPTH):
            self.cast_group(groups[l])
        P.end_phase()
        self.vec = self.dint("vec", [c.DEPTH, 6, 2, D], F32)
        self.Bvec = Buf()
        self.lbv = self.dint("lbv", [c.NA, 2, 2, D], F32)
        self.Blbv = Buf()
        nc = self.nc
        with contextlib.ExitStack() as st:
            sb = lambda n, s, d: st.enter_context(nc.sbuf_tensor(_uniq(n), s, d))
            cs = sb("cs", [128, c.DC, 2], F32)
            adn = sb("adn", [128, c.DC, c.ADAR], F32)
            RC = c.ADAR // 128
            tT = sb("tT", [128, RC, 2], F32)
            CW = 2048
            aup = [sb(f"aup{i}", [128, RC, CW], F32) for i in range(2)]
            modk = sb("modk", [2, D], F32)
            biask = sb("biask", [2, D], F32)
            gnk = sb("gnk", [2, D], F32)
            resk = sb("resk", [2, D], F32)
            Bcs, Badn, BtT, Bmod, Bbias, Bgn, Bres = (Buf() for _ in range(7))
            Baup = [Buf(), Buf()]
            P.dma("sp", cs[:], self.condT[:, :, :], writes=[Bcs])
            P.actf(cs[:], cs[:], AF.Silu, [Bcs], [Bcs])
            kmap = {0: (1, None), 1: (0, 0), 2: (2, 1), 3: (4, None), 4: (3, 2), 5: (5, 3)}
            npk = D // CW if D >= CW else 1
            cw = min(CW, D)
            api = 0
            for l in range(c.DEPTH):
                P.dma("sp", adn[:], self.wfull[f"adown{l}"][:, :].rearrange("(c p) r -> p c r", p=128), reads=self.Bw[f"adown{l}"], writes=[Badn])
                for rc in range(RC):
                    for kc in range(c.DC):
                        P.mm(self.pf[0][:, 0:2], adn[:, kc, rc * 128:(rc + 1) * 128], cs[:, kc, :], kc == 0, kc == c.DC - 1,
                             [Badn, Bcs], [self.Bpf[0]])
                    P.cp("dve", tT[:, rc, :], self.pf[0][:, 0:2], [self.Bpf[0]], [BtT])
                for kd in range(6):
                    outk, gi = kmap[kd]
                    P.dma("sp", biask[:], self.abias[l:l + 1, kd * D:(kd + 1) * D].broadcast_to([2, D]), writes=[Bbias])
                    if gi is not None:
                        P.dma("sp", gnk[:], self.gains[gi, l:l + 1, :].broadcast_to([2, D]), writes=[Bgn])
                    for pc in range(npk):
                        ab = api % 2
                        api += 1
                        col0 = kd * D + pc * cw
                        P.dma("sp", aup[ab][:, :, 0:cw], self.wfull[f"aup{l}"][:, col0:col0 + cw].rearrange("(c p) n -> p c n", p=128),
                              reads=self.Bw[f"aup{l}"], writes=[Baup[ab]])
                        for jj in range(cw // 512):
                            pi = 1 + (jj % 2)
                            for rc in range(RC):
                                P.mm(self.pf[pi][0:2, :], tT[:, rc, :], aup[ab][:, rc, jj * 512:(jj + 1) * 512], rc == 0, rc == RC - 1,
                                     [BtT, Baup[ab]], [self.Bpf[pi]])
                            n0 = pc * cw + jj * 512
                            P.tt("dve", modk[:, n0:n0 + 512], self.pf[pi][0:2, :], biask[:, n0:n0 + 512], ALU.add, [self.Bpf[pi], Bbias], [Bmod])
                    if kd in (1, 4):
                        P.stt("dve", resk[:], modk[:], 1.0, gnk[:], ALU.add, ALU.mult, [Bmod, Bgn], [Bres])
                    elif kd in (2, 5):
                        P.tt("dve", resk[:], modk[:], gnk[:], ALU.mult, [Bmod, Bgn], [Bres])
                    else:
                        P.cp("dve", resk[:], modk[:], [Bmod], [Bres])
                    P.dma("sp", self.vec[l, outk, :, :], resk[:], reads=[Bres], writes=[self.Bvec])
            P.end_phase()
        with contextlib.ExitStack() as st:
            sb = lambda n, s, d: st.enter_context(nc.sbuf_tensor(_uniq(n), s, d))
            lg = sb("lg", [2, c.NA, D], F32)
            ex = sb("ex", [2, c.NA, D], F32)
            sm = sb("sm", [2, D], F32)
            lbt = sb("lbt", [2, c.NA, 2, D], F32)
            Blg, Bex, Bsm, Blbt = Buf(), Buf(), Buf(), Buf()
            P.dma("sp", lg[:], self.lblog[:, :, :].rearrange("j r d -> r j d"), writes=[Blg])
            P.actf(ex[:], lg[:], AF.Exp, [Blg], [Bex])
            P.cp("dve", sm[:], ex[:, 0, :], [Bex], [Bsm])
            for j in range(1, c.NA):
                P.tt("dve", sm[:], sm[:], ex[:, j, :], ALU.add, [Bex, Bsm], [Bsm])
            P.recip(sm[:], sm[:], [Bsm], [Bsm])
            P.memset("dve", lbt[:, 0, 0, :], 0.0, [Blbt])
            P.memset("dve", lbt[:, 0, 1, :], 1.0, [Blbt])
            for j in range(1, c.NA):
                P.tt("dve", ex[:, j, :], ex[:, j, :], sm[:], ALU.mult, [Bex, Bsm], [Bex])
                if j == 1:
                    P.cp("dve", lbt[:, j, 0, :], ex[:, j, :], [Bex], [Blbt])
                else:
                    P.tt("dve", lbt[:, j, 0, :], lbt[:, j - 1, 0, :], ex[:, j, :], ALU.add, [Bex, Blbt], [Blbt])
                P.ts("dve", lbt[:, j, 1, :], lbt[:, j, 0, :], -1.0, 1.0, ALU.mult, ALU.add, [Blbt], [Blbt])
            P.dma("sp", self.lbv[:, :, :, :].rearrange("j r k d -> r j k d"), lbt[:], reads=[Blbt], writes=[self.Blbv])
            P.end_phase()

    def cast_group(self, names):
        P = self.P
        for name in names:
            src, full_shape, dt, nr = self.ws[name]
            rows = full_shape[0] // nr
            sh = self.dint(name + "_s", [rows, full_shape[1]], dt)
            self.wfull[name] = self.dint(name + "_f", full_shape, dt)
            self.shards[name] = sh
            bl = []
            step = max(1, (1 << 20) // full_shape[1])
            r0 = 0
            while r0 < rows:
                r1 = min(rows, r0 + step)
                b = Buf()
                bl.append(b)
                P.dma("pool", sh[r0:r1, :], src[r0:r1, :], writes=[b], sbuf=False)
                r0 = r1
            self.Bcast[name] = bl

    def gather_group(self, names):
        P = self.P
        for name in names:
            src, full_shape, dt, nr = self.ws[name]
            Kf, Nf = full_shape
            cr = chunk_rows(Kf, Nf, 4 if dt == F32 else 2)
            bl = []
            for i in range((Kf // 4) // cr):
                b = Buf()
                bl.append(b)
                P.coll("AllGather", ALU.bypass, self.g8, self.shards[name][i * cr:(i + 1) * cr, :],
                       self.wfull[name][i * 4 * cr:(i + 1) * 4 * cr, :], self.Bcast[name], [b])
            self.Bw[name] = bl

    def gather_next(self, l):
        if l + 1 < self.cfg.DEPTH:
            self.gather_group(self.wgroups[l + 1])

    def tiles(self, TT):
        c = self.cfg
        tb = TT // 128
        b0 = c.NBC if getattr(self, "skip_ctx", False) else 0
        return [list(range(i, min(c.NB, i + tb))) for i in range(b0, c.NB, tb)]

    def row_of(self, blk):
        return 1 if blk < self.cfg.NBC else 0

    def provider_norm(self, st, l, kindA, kindB, pend):
        c = self.cfg
        P = self.P
        nc = self.nc
        D = c.D
        sb = lambda n, s, d: st.enter_context(nc.sbuf_tensor(_uniq(n), s, d))
        xt = sb("pn_x", [128, D], F32)
        yt = sb("pn_y", [128, D], F32)
        gt = sb("pn_g", [128, D], F32) if pend else None
        ab = sb("pn_ab", [128, 2, 2, c.DC], F32)
        stt_ = sb("pn_st", [128, 4], F32)
        Bx, By, Bg, Bab, Bst = Buf(), Buf(), Buf(), Buf(), Buf(strict=True)
        for r in range(2):
            P.dma("sp", ab[:, r, 0, :], self.vec[l, kindA, r, :].rearrange("(c p) -> p c", p=128), reads=[self.Bvec], writes=[Bab], slow=True)
            P.dma("sp", ab[:, r, 1, :], self.vec[l, kindB, r, :].rearrange("(c p) -> p c", p=128), reads=[self.Bvec], writes=[Bab], slow=True)
        state = {"grow": None}
        xsrc = self.cur_x
        Bxsrc = self.B(("x",))
        xdst = self.xs
        rsq = float(D) ** -0.5

        def fill(blocks, actT, Bact):
            for i, blk in enumerate(blocks):
                row = self.row_of(blk)
                r0 = blk * 128
                P.dma("sp", xt[:], xsrc[r0:r0 + 128, :], reads=[self.B(("x", blk))], writes=[Bx])
                if pend:
                    ydram, kindG = pend
                    if state["grow"] != row:
                        P.dma("sp", gt[:], self.vec[l if kindG == 2 else l - 1, kindG, row:row + 1, :].broadcast_to([128, D]),
                              reads=[self.Bvec], writes=[Bg])
                        state["grow"] = row
                    P.dma("sp", yt[:], ydram[r0:r0 + 128, :], reads=[self.B(("y", blk, n0)) for n0 in range(0, D, 512)], writes=[By])
                    P.actf(self.junk[:, 0:D], yt[:], AF.Square, [By], [self.Bjunk, Bst], scale=rsq, accum=stt_[:, 0:1])
                    P.actf(stt_[:, 1:2], stt_[:, 0:1], AF.Sqrt, [Bst], [Bst], bias=self.epsc[:, 0:1])
                    P.recip(stt_[:, 1:2], stt_[:, 1:2], [Bst], [Bst])
                    P.stt("dve", yt[:], yt[:], stt_[:, 1:2], gt[:], ALU.mult, ALU.mult, [By, Bst, Bg], [By])
                    P.tt("dve", xt[:], xt[:], yt[:], ALU.add, [Bx, By], [Bx])
                    P.dma("sp", xdst[r0:r0 + 128, :], xt[:], reads=[Bx], writes=[self.B(("x", blk))])
                P.actf(self.junk[:, 0:D], xt[:], AF.Square, [Bx], [self.Bjunk, Bst], scale=rsq, accum=stt_[:, 2:3])
                P.actf(stt_[:, 3:4], stt_[:, 2:3], AF.Sqrt, [Bst], [Bst], bias=self.epsc[:, 0:1])
                P.recip(stt_[:, 3:4], stt_[:, 3:4], [Bst], [Bst])
                P.ts("dve", yt[:], xt[:], stt_[:, 3:4], None, ALU.mult, None, [Bx, Bst], [By])
                self.transpose_f32(yt, By, c.DC, actT, Bact, i * 128, scale_cols=ab[:, row, 0, :], bias_cols=ab[:, row, 1, :], Bsc=Bab)
        return fill

    def transpose_f32(self, src, Bsrc, nch, actT, Bact, toff, scale_cols=None, bias_cols=None, Bsc=None, dst_c0=0):
        P = self.P
        k = 0
        for c0 in range(0, nch, 4):
            n = min(4, nch - c0)
            pi = 4 + (self._trk % 2)
            self._trk += 1
            for j in range(n):
                cc = c0 + j
                P.tr(self.pf[pi][:, j * 128:(j + 1) * 128], src[:, cc * 128:(cc + 1) * 128], self.identf[:], [Bsrc], [self.Bpf[pi]])
            for j in range(n):
                cc = c0 + j
                eng = "act"
                if scale_cols is not None:
                    P.actf(actT[:, dst_c0 + cc, toff:toff + 128], self.pf[pi][:, j * 128:(j + 1) * 128], AF.Identity,
                           [self.Bpf[pi], Bsc], [Bact], scale=scale_cols[:, cc:cc + 1],
                           bias=(bias_cols[:, cc:cc + 1] if bias_cols is not None else None))
                else:
                    P.cp("act" if (pi % 2) else "dve", actT[:, dst_c0 + cc, toff:toff + 128], self.pf[pi][:, j * 128:(j + 1) * 128],
                         [self.Bpf[pi]], [Bact])
                k += 1

    def transpose_b16(self, src, Bsrc, nch, actT, Bact, toff):
        P = self.P
        for c0 in range(0, nch, 8):
            n = min(8, nch - c0)
            pi = self._trk % 2
            self._trk += 1
            for j in range(n):
                P.tr(self.pb[pi][:, j * 128:(j + 1) * 128], src[:, (c0 + j) * 128:(c0 + j + 1) * 128], self.identb[:], [Bsrc], [self.Bpb[pi]])
            P.cp("act" if (pi % 2) else "dve", actT[:, c0:c0 + n, toff:toff + 128],
                 self.pb[pi][:, 0:n * 128].rearrange("p (c t) -> p c t", t=128), [self.Bpb[pi]], [Bact])

    def gemm_g1(self, st, K, ntiles, wsrc, tiles, fill, epi, TTW, nw=2):
        c = self.cfg
        P = self.P
        nc = self.nc
        KC = K // 128
        KP = (KC + 31) // 32
        KCP = KC // KP
        sb = lambda n, s, d: st.enter_context(nc.sbuf_tensor(_uniq(n), s, d))
        actT = sb("g1_act", [128, KC, TTW], BF16)
        Bact = Buf()
        wt = [sb(f"g1_w{i}", [128, KCP, 512], BF16) for i in range(nw)]
        Bwt = [Buf() for _ in range(nw)]
        wi = 0
        rot = 0
        for blocks in tiles:
            fill(blocks, actT, Bact)
            seq = [(nt, kp) for nt in range(len(ntiles)) for kp in range(KP)]
            loaded = {}

            def load(idx):
                nonlocal wi
                nt, kp = seq[idx]
                key, n0, width = ntiles[nt]
                ap, bw = wsrc(key, kp * KCP * 128, (kp + 1) * KCP * 128, n0, width)
                j = wi % nw
                wi += 1
                P.dma("sp", wt[j][:, :, 0:width], ap.rearrange("(c p) n -> p c n", p=128), reads=bw, writes=[Bwt[j]])
                loaded[idx] = j
            load(0)
            for idx, (nt, kp) in enumerate(seq):
                if idx + 1 < len(seq):
                    load(idx + 1)
                j = loaded.pop(idx)
                key, n0, width = ntiles[nt]
                for ti, blk in enumerate(blocks):
                    if KP == 1:
                        pi = rot % 4
                        rot += 1
                    else:
                        pi = (nt % 2) * 3 + ti
                    for kc in range(KCP):
                        P.mm(self.pf[pi][:, 0:width], actT[:, kp * KCP + kc, ti * 128:(ti + 1) * 128], wt[j][:, kc, 0:width],
                             kp == 0 and kc == 0, kp == KP - 1 and kc == KCP - 1, [Bact, Bwt[j]], [self.Bpf[pi]])
                    if kp == KP - 1:
                        epi(key, n0, width, blk, self.pf[pi][:, 0:width], self.Bpf[pi])

    def gemm_g2(self, st, K, ngroups, wsrc, tiles, fill, epi, TTW, sub=384):
        P = self.P
        nc = self.nc
        KC = K // 128
        sb = lambda n, s, d: st.enter_context(nc.sbuf_tensor(_uniq(n), s, d))
        actT = sb("g2_act", [128, KC, TTW], BF16)
        Bact = Buf()
        wt = [sb(f"g2_w{i}", [128, KC, 512], BF16) for i in range(2)]
        Bwt = [Buf(), Buf()]
        wi = 0
        rot = 0
        for blocks in tiles:
            fill(blocks, actT, Bact)
            tw = len(blocks) * 128
            subs = [(s0, min(sub, tw - s0)) for s0 in range(0, tw, sub)]

            def load(g):
                nonlocal wi
                key, n0, width = ngroups[g]
                ap, bw = wsrc(key, 0, K, n0, width)
                j = wi % 2
                wi += 1
                P.dma("sp", wt[j][:, :, 0:width], ap.rearrange("(c p) n -> p c n", p=128), reads=bw, writes=[Bwt[j]])
                return j
            jn = load(0)
            for g, (key, n0, width) in enumerate(ngroups):
                j = jn
                if g + 1 < len(ngroups):
                    jn = load(g + 1)
                for nb in range(width // 128):
                    for (s0, sw) in subs:
                        pi = rot % 4
                        rot += 1
                        for kc in range(KC):
                            P.mm(self.pf[pi][:, 0:sw], wt[j][:, kc, nb * 128:(nb + 1) * 128], actT[:, kc, s0:s0 + sw],
                                 kc == 0, kc == KC - 1, [Bact, Bwt[j]], [self.Bpf[pi]])
                        epi(key, n0 + nb * 128, blocks, s0, sw, self.pf[pi][:, 0:sw], self.Bpf[pi])

    def phase_scope(self):
        k = self

        class _S:
            def __enter__(s):
                s.st = contextlib.ExitStack()
                s.st.__enter__()
                k._trk = 0
                k.junk = s.st.enter_context(k.nc.sbuf_tensor(_uniq("junk"), [128, k.cfg.D], BF16))
                k.Bjunk = Buf()
                return s.st

            def __exit__(s, *a):
                if a[0] is None:
                    k.P.end_phase()
                return s.st.__exit__(*a)
        return _S()

    def store_epi(self, st, dram, bkey, dt=F32, func=None, nbuf=3):
        P = self.P
        nc = self.nc
        ob = [st.enter_context(nc.sbuf_tensor(_uniq(f"se_{bkey}_{i}"), [128, 512], dt)) for i in range(nbuf)]
        Bo = [Buf() for _ in range(nbuf)]
        cnt = [0]

        def epi(key, n0, width, blk, ps, Bps):
            j = cnt[0] % nbuf
            cnt[0] += 1
            if func is None:
                P.cp("act" if (cnt[0] % 2) else "dve", ob[j][:, 0:width], ps, [Bps], [Bo[j]])
            else:
                P.actf(ob[j][:, 0:width], ps, func, [Bps], [Bo[j]])
            P.dma("sp", dram[blk * 128:(blk + 1) * 128, n0:n0 + width], ob[j][:, 0:width], reads=[Bo[j]],
                  writes=[self.B((bkey, blk, n0))])
        return epi

    def mlp_layer(self, l):
        c = self.cfg
        P = self.P
        nc = self.nc
        D, HID = c.D, c.HID
        if not hasattr(self, "h1t"):
            self.h1t = self.dint("h1t", [HID, c.NT], BF16)
        w1 = self.wfull[f"w1_{l}"]
        w2 = self.wfull[f"w2_{l}"]
        with self.phase_scope() as st:
            fill = self.provider_norm(st, l, 3, 4, (self.ybuf, 2))
            self.cur_x = self.xs
            tmp = [st.enter_context(nc.sbuf_tensor(_uniq(f"m1_t{i}"), [128, 384], F32)) for i in range(2)]
            ob = [st.enter_context(nc.sbuf_tensor(_uniq(f"m1_o{i}"), [128, 384], BF16)) for i in range(3)]
            Bt = [Buf(), Buf()]
            Bo = [Buf() for _ in range(3)]
            cnt = [0]

            def epi(key, n0, blocks, s0, sw, ps, Bps):
                i = cnt[0]
                cnt[0] += 1
                a, b = i % 2, i % 3
                P.actf(tmp[a][:, 0:sw], ps, AF.Relu, [Bps], [Bt[a]])
                P.tt("dve", ob[b][:, 0:sw], tmp[a][:, 0:sw], tmp[a][:, 0:sw], ALU.mult, [Bt[a]], [Bo[b]])
                t0 = blocks[0] * 128 + s0
                P.dma("sp", self.h1t[n0:n0 + 128, t0:t0 + sw], ob[b][:, 0:sw], reads=[Bo[b]], writes=[self.B(("h1", n0, t0))])
            groups = [("w1", n0, 512) for n0 in range(0, HID, 512)]
            self.gemm_g2(st, D, groups, lambda key, k0, k1, n0, w: (w1[k0:k1, n0:n0 + w], self.Bw[f"w1_{l}"]),
                         self.tiles(c.TT), fill, epi, c.TT)
        with self.phase_scope() as st:
            KC = HID // 128

            def fill2(blocks, actT, Bact):
                t0 = blocks[0] * 128
                tw = len(blocks) * 128
                for kp in range(0, KC, 32):
                    n = min(32, KC - kp)
                    rd = [self.B(("h1", (kp + cc) * 128, t0s)) for cc in range(n) for t0s in self._h1_cols(t0, tw)]
                    P.dma("sp", actT[:, kp:kp + n, 0:tw], self.h1t[kp * 128:(kp + n) * 128, t0:t0 + tw].rearrange("(c p) t -> p c t", p=128),
                          reads=rd, writes=[Bact])
            epi2 = self.store_epi(st, self.ybuf, "y")
            nts = [("w2", n0, 512) for n0 in range(0, D, 512)]
            self.gemm_g1(st, HID, nts, lambda key, k0, k1, n0, w: (w2[k0:k1, n0:n0 + w], self.Bw[f"w2_{l}"]),
                         self.tiles(c.TT2), fill2, epi2, c.TT2)
        self.pending_kind = 5

    def _h1_cols(self, t0, tw):
        c = self.cfg
        out = []
        for blocks in self.tiles(c.TT):
            b0 = blocks[0] * 128
            w = len(blocks) * 128
            for s0 in range(0, w, 384):
                a = b0 + s0
                e = a + min(384, w - s0)
                if a < t0 + tw and e > t0:
                    out.append(a)
        return out

    def hgrn2_layer(self, l):
        c = self.cfg
        P = self.P
        nc = self.nc
        D = c.D
        j = l // 2
        if not hasattr(self, "hq"):
            self.hq = self.dint("hq", [c.NT, D], F32)
            self.hv = self.dint("hv", [c.NT, D], BF16)
            self.hg = self.dint("hg", [c.NT, D], F32)
            self.hlf = [self.dint(f"hlf{i}", [c.NT, D], F32) for i in range(2)]
            self.ho = self.dint("ho", [c.NT, D], F32)
            self.st_own = self.dint("st_own", [(c.H // 4) * 128, 512], F32)
            self.st_sum = self.dint("st_sum", [(c.H // 4) * 128, 512], F32)
        wq = self.wfull[f"hqig{l}"]
        wz = self.wfull[f"hz{l}"]
        with self.phase_scope() as st:
            pend = (self.ybuf, 5) if l > 0 else None
            fill = self.provider_norm(st, l, 0, 1, pend)
            if pend:
                self.cur_x = self.xs
            sbt = lambda n, s, d: st.enter_context(nc.sbuf_tensor(_uniq(n), s, d))
            lbt = sbt("h1_lb", [128, 2, 512], F32)
            Blb = Buf()
            e_q = self.store_epi(st, self.hq, "hq", F32, AF.Silu, nbuf=2)
            e_g = self.store_epi(st, self.hg, "hg", F32, AF.Silu, nbuf=2)
            e_v = self.store_epi(st, self.hv, "hv", BF16, None, nbuf=2)
            zt = [sbt(f"h1_z{i}", [128, 512], F32) for i in range(2)]
            Bz = [Buf(), Buf()]
            zc = [0]
            cur = {"key": None}

            def epi(key, n0, width, blk, ps, Bps):
                kind, col = key
                if kind == 0:
                    return e_q(key, col, width, blk, ps, Bps)
                if kind == 1:
                    return e_v(key, col, width, blk, ps, Bps)
                if kind == 4:
                    return e_g(key, col, width, blk, ps, Bps)
                d = kind - 2
                if cur["key"] != key:
                    cur["key"] = key
                    P.dma("sp", lbt[:, 0, :], self.lbv[j, d, 0:1, col:col + 512].broadcast_to([128, 512]), reads=[self.Blbv], writes=[Blb])
                    P.dma("sp", lbt[:, 1, :], self.lbv[j, d, 1:2, col:col + 512].broadcast_to([128, 512]), reads=[self.Blbv], writes=[Blb])
                i = zc[0] % 2
                zc[0] += 1
                z = zt[i]
                P.actf(z[:], ps, AF.Exp, [Bps], [Bz[i]], scale=-1.0)
                P.ts("dve", z[:], z[:], 1.0, None, ALU.add, None, [Bz[i]], [Bz[i]])
                P.recip(z[:], z[:], [Bz[i]], [Bz[i]])
                P.tt("dve", z[:], z[:], lbt[:, 1, :], ALU.mult, [Bz[i], Blb], [Bz[i]])
                P.tt("dve", z[:], z[:], lbt[:, 0, :], ALU.add, [Bz[i], Blb], [Bz[i]])
                P.actf(z[:], z[:], AF.Ln, [Bz[i]], [Bz[i]])
                P.dma("sp", self.hlf[d][blk * 128:(blk + 1) * 128, col:col + 512], z[:], reads=[Bz[i]], writes=[self.B(("hlf", d, blk, col))])
            nts = []
            for kind in (0, 1, 2, 3, 4):
                for col in range(0, D, 512):
                    nts.append(((kind, col), col, 512))

            def wsrc(key, k0, k1, n0, w):
                kind, col = key
                if kind in (2, 3):
                    return wz[k0:k1, (kind - 2) * D + col:(kind - 2) * D + col + w], self.Bw[f"hz{l}"]
                sel = {0: 0, 1: 1, 4: 2}[kind]
                return wq[k0:k1, sel * D + col:sel * D + col + w], self.Bw[f"hqig{l}"]
            self.gemm_g1(st, D, nts, wsrc, self.tiles(c.TT), fill, epi, c.TT)
        self.check_stop(f"h1_{l}")
        groups2 = [[0, 1], [2, 3], [4, 5], [6, 7]]
        for d in (0, 1):
            with self.phase_scope() as st:
                self.hg_scan(st, d)
            self.check_stop(f"scan{d}_{l}")
            if d == 0:
                for g in range(c.H // 4):
                    P.coll("AllReduce", ALU.add, groups2, self.st_own[g * 128:(g + 1) * 128, :], self.st_sum[g * 128:(g + 1) * 128, :],
                           [self.B(("st_own", g))], [self.B(("st_sum", g))])
                self.gather_next(l)
        wo = self.wfull[f"hout{l}"]
        with self.phase_scope() as st:
            sbt = lambda n, s, d: st.enter_context(nc.sbuf_tensor(_uniq(n), s, d))
            ot = sbt("ro_o", [128, D], F32)
            gtile = sbt("ro_g", [128, D], F32)
            gn = sbt("ro_gn", [128, c.DC], F32)
            s4 = sbt("ro_st", [128, 2], F32)
            Bo, Bg, Bgn, Bs4 = Buf(), Buf(), Buf(), Buf(strict=True)
            P.dma("sp", gn[:], self.hgnormT[j, :, :], writes=[Bgn])
            rsq = float(D) ** -0.5

            def fill(blocks, actT, Bact):
                for i, blk in enumerate(blocks):
                    r0 = blk * 128
                    P.dma("sp", ot[:], self.ho[r0:r0 + 128, :], reads=[self.B(("ho", blk, g)) for g in range(c.H // 4)], writes=[Bo])
                    P.dma("sp", gtile[:], self.hg[r0:r0 + 128, :], reads=[self.B(("hg", blk, n0)) for n0 in range(0, D, 512)], writes=[Bg])
                    P.actf(self.junk[:, 0:D], ot[:], AF.Square, [Bo], [self.Bjunk, Bs4], scale=rsq, accum=s4[:, 0:1])
                    P.actf(s4[:, 1:2], s4[:, 0:1], AF.Sqrt, [Bs4], [Bs4], bias=self.epsc[:, 0:1])
                    P.recip(s4[:, 1:2], s4[:, 1:2], [Bs4], [Bs4])
                    P.stt("dve", ot[:], ot[:], s4[:, 1:2], gtile[:], ALU.mult, ALU.mult, [Bo, Bs4, Bg], [Bo])
                    self.transpose_f32(ot, Bo, c.DC, actT, Bact, i * 128, scale_cols=gn, Bsc=Bgn)
            epi = self.store_epi(st, self.ybuf, "y")
            nts = [("wo", n0, 512) for n0 in range(0, D, 512)]
            self.gemm_g1(st, D, nts, lambda key, k0, k1, n0, w: (wo[k0:k1, n0:n0 + w], self.Bw[f"hout{l}"]),
                         self.tiles(c.TT), fill, epi, c.TT)

    def hg_scan(self, st, d):
        c = self.cfg
        P = self.P
        nc = self.nc
        sbt = lambda n, s, dt: st.enter_context(nc.sbuf_tensor(_uniq(n), s, dt))
        NG = c.H // 4
        tri = self.trif if d == 0 else self.trib
        trix = self.trifx if d == 0 else self.tribx
        mask = self.maskf if d == 0 else self.maskb
        lf = [sbt(f"sc_lf{i}", [128, 512], F32) for i in range(2)]
        lfb = [sbt(f"sc_lfb{i}", [128, 512], F32) for i in range(2)]
        Blfb = [Buf(), Buf()]
        qt = [sbt(f"sc_q{i}", [128, 512], F32) for i in range(2)]
        vt = [sbt(f"sc_v{i}", [128, 512], BF16) for i in range(2)]
        o1 = [sbt(f"sc_o1{i}", [128, 512], F32) for i in range(2)]
        Blf, Bq, Bv, Bo1 = [Buf(), Buf()], [Buf(), Buf()], [Buf(), Buf()], [Buf(), Buf()]
        E1 = sbt("sc_e1", [128, 512], F32)
        E2 = sbt("sc_e2", [128, 512], F32)
        E3 = sbt("sc_e3", [128, 512], F32)
        kk = sbt("sc_k", [128, 512], F32)
        Qt = sbt("sc_Qt", [128, 512], BF16)
        Kt = sbt("sc_Kt", [128, 512], BF16)
        Kd = sbt("sc_Kd", [128, 512], BF16)
        QtT = sbt("sc_QtT", [128, 512], BF16)
        KtT = sbt("sc_KtT", [128, 512], BF16)
        attT = sbt("sc_att", [128, 512], BF16)
        dec = sbt("sc_dec", [128, 8], F32)
        S = sbt("sc_S", [128, 512], F32)
        Sb = sbt("sc_Sb", [128, 512], BF16)
        S2 = sbt("sc_S2", [128, 512], F32)
        osb = [sbt(f"sc_os{i}", [128, 512], F32) for i in range(2)]
        BE1, BE2, BE3, Bk, BQt, BKt, BKd, BQtT, BKtT, Batt, Bdec, BS, BSb, BS2 = (Buf() for _ in range(14))
        Bos = [Buf(), Buf()]
        b_ps, r_ps, bl_ps, att_ps, o_ps, ds_ps = (self.pf[i] for i in range(6))
        Bb, Br, Bbl, Batp, Bop, Bdsp = (self.Bpf[i] for i in range(6))
        pT = self.pb[0]
        BpT = self.Bpb[0]
        pT2 = self.pb[1]
        BpT2 = self.Bpb[1]
        if d == 0:
            order = list(range(c.NB))
        else:
            order = list(range(c.NBC - 1, -1, -1)) + list(range(c.NB - 1, c.NBC - 1, -1))
        corder = (0, 1) if d == 0 else (1, 0)
        it = 0
        for g in range(NG):
            c0 = g * 512
            P.memset("dve", S[:], 0.0, [BS])
            P.memset("dve", Sb[:], 0.0, [BSb])
            for bi, blk in enumerate(order):
                if d == 1 and bi == c.NBC:
                    P.dma("sp", S[:], self.st_sum[g * 128:(g + 1) * 128, :], reads=[self.B(("st_sum", g))], writes=[BS])
                    P.dma("sp", S2[:], self.st_own[g * 128:(g + 1) * 128, :], reads=[self.B(("st_own", g))], writes=[BS2])
                    P.tt("dve", S[:], S[:], S2[:], ALU.subtract, [BS, BS2], [BS])
                    P.cp("act", Sb[:], S[:], [BS], [BSb])
                i2 = it % 2
                it += 1
                r0 = blk * 128
                P.dma("sp", lf[i2][:], self.hlf[0][r0:r0 + 128, c0:c0 + 512], reads=[self.B(("hlf", 0, blk, c0))], writes=[Blf[i2]])
                P.dma("sp", lfb[i2][:], self.hlf[1][r0:r0 + 128, c0:c0 + 512], reads=[self.B(("hlf", 1, blk, c0))], writes=[Blfb[i2]])
                P.ts("dve", lf[i2][:], lf[i2][:], self.selv[:, d:d + 1], None, ALU.mult, None, [Blf[i2]], [Blf[i2]])
                P.stt("dve", lf[i2][:], lfb[i2][:], self.selv[:, 1 - d:2 - d], lf[i2][:], ALU.mult, ALU.add, [Blf[i2], Blfb[i2]], [Blf[i2]])
                P.dma("sp", qt[i2][:], self.hq[r0:r0 + 128, c0:c0 + 512], reads=[self.B(("hq", blk, c0))], writes=[Bq[i2]])
                P.dma("sp", vt[i2][:], self.hv[r0:r0 + 128, c0:c0 + 512], reads=[self.B(("hv", blk, c0))], writes=[Bv[i2]])
                if d == 1:
                    P.dma("sp", o1[i2][:], self.ho[r0:r0 + 128, c0:c0 + 512], reads=[self.B(("ho", blk, g))], writes=[Bo1[i2]])
                L_, Q_, V_ = lf[i2], qt[i2], vt[i2]
                import os
                CUT = int(os.environ.get("SCAN_CUT", "99"))
                if CUT < 2:
                    continue
                P.mm(b_ps[:], tri[:], L_[:], True, True, [Blf[i2]], [Bb])
                P.mm(r_ps[:], trix[:], L_[:], True, True, [Blf[i2]], [Br])
                for h in range(4):
                    P.mm(bl_ps[:, 2 * h:2 * h + 2], L_[:, h * 128:(h + 1) * 128], self.ind[:], True, True, [Blf[i2]], [Bbl])
                if CUT < 3:
                    continue
                MSK = int(os.environ.get("SCAN_MSK", "127"))
                if MSK & 1:
                    P.actf(E1[:], b_ps[:], AF.Exp, [Bb], [BE1])
                if MSK & 2:
                    P.recip(E2[:], E1[:], [BE1], [BE2])
                    P.ts("dve", E2[:], E2[:], 5.0e34, None, ALU.min, None, [BE2], [BE2])
                if MSK & 8:
                    P.actf(E3[:], r_ps[:], AF.Exp, [Br], [BE3])
                if MSK & 16:
                    P.actf(kk[:], L_[:], AF.Exp, [Blf[i2]], [Bk])
                if MSK & 32:
                    P.actf(dec[:], bl_ps[:, 0:8], AF.Exp, [Bbl], [Bdec])
                if MSK & 64:
                    P.ts("dve", kk[:], kk[:], -1.0, 1.0, ALU.mult, ALU.add, [Bk], [Bk])
                if CUT < 4:
                    continue
                P.tt("dve", Qt[:], Q_[:], E1[:], ALU.mult, [Bq[i2], BE1], [BQt])
                P.tt("dve", Kt[:], kk[:], E2[:], ALU.mult, [Bk, BE2], [BKt])
                P.tt("dve", Kd[:], kk[:], E3[:], ALU.mult, [Bk, BE3], [BKd])
                if CUT < 5:
                    continue
                for h in range(4):
                    P.tr(pT[:, h * 128:(h + 1) * 128], Qt[:, h * 128:(h + 1) * 128], self.identb[:], [BQt], [BpT])
                for h in range(4):
                    P.tr(pT2[:, h * 128:(h + 1) * 128], Kt[:, h * 128:(h + 1) * 128], self.identb[:], [BKt], [BpT2])
                P.cp("act", QtT[:], pT[:, 0:512], [BpT], [BQtT])
                P.cp("dve", KtT[:], pT2[:, 0:512], [BpT2], [BKtT])
                if CUT < 6:
                    continue
                for h in range(4):
                    hs = slice(h * 128, (h + 1) * 128)
                    P.mm(att_ps[:, hs], KtT[:, hs], QtT[:, hs], True, True, [BKtT, BQtT], [Batp])
                P.tt("dve", attT[:], att_ps[:], mask[:], ALU.mult, [Batp], [Batt])
                if CUT < 7:
                    continue
                for ci in corder:
                    ps_ = slice(ci * 64, ci * 64 + 64)
                    for h in range(4):
                        hs = slice(h * 128, (h + 1) * 128)
                        tcs = slice(h * 128 + ci * 64, h * 128 + ci * 64 + 64)
                        P.mm(o_ps[ps_, hs], attT[ps_, tcs], V_[ps_, hs], True, False, [Batt, Bv[i2]], [Bop])
                        P.mm(o_ps[ps_, hs], QtT[:, tcs], Sb[:, hs], False, True, [BQtT, BSb], [Bop])
                    if CUT < 8:
                        continue
                    for h in range(4):
                        hs = slice(h * 128, (h + 1) * 128)
                        P.mm(ds_ps[:, hs], Kd[ps_, hs], V_[ps_, hs], True, True, [BKd, Bv[i2]], [Bdsp])
                    for h in range(4):
                        hs = slice(h * 128, (h + 1) * 128)
                        P.stt("dve", S[:, hs], S[:, hs], dec[:, 2 * h + ci:2 * h + ci + 1], ds_ps[:, hs], ALU.mult, ALU.add,
                              [BS, Bdec, Bdsp], [BS])
                    P.cp("act", Sb[:], S[:], [BS], [BSb])
                oj = it % 2
                if d == 0:
                    P.cp("act", osb[oj][:], o_ps[:], [Bop], [Bos[oj]])
                else:
                    P.tt("dve", osb[oj][:], o_ps[:], o1[i2][:], ALU.add, [Bop, Bo1[i2]], [Bos[oj]])
                P.dma("sp", self.ho[r0:r0 + 128, c0:c0 + 512], osb[oj][:], reads=[Bos[oj]], writes=[self.B(("ho", blk, g))])
            if d == 0:
                P.dma("sp", self.st_own[g * 128:(g + 1) * 128, :], S[:], reads=[BS], writes=[self.B(("st_own", g))])

    def mla_layer(self, l):
        c = self.cfg
        P = self.P
        nc = self.nc
        D = c.D
        j = l // 2
        last = (l == c.DEPTH - 1)
        QC, KC2 = c.QR // 128, c.KVR // 128
        if not hasattr(self, "cd"):
            self.cd = self.dint("cd", [c.NT, c.DW], F32)
            self.cqt = self.dint("cqt", [c.QR, c.NT], BF16)
            self.kvc = self.dint("kvc", [c.KVX, c.CTX], BF16)
            self.kvx = self.dint("kvx", [c.KVX, c.NTL], BF16)
            self.kvall = self.dint("kvall", [2 * c.KVX, c.NTL], BF16)
            self.oa = self.dint("oa", [c.NT, D], BF16)
        wd = self.wfull[f"mdown{l}"]
        with self.phase_scope() as st:
            fill = self.provider_norm(st, l, 0, 1, (self.ybuf, 5))
            self.cur_x = self.xs
            epi = self.store_epi(st, self.cd, "cd")
            nts = [("wd", n0, min(512, c.DW - n0)) for n0 in range(0, c.DW, 512)]
            self.gemm_g1(st, D, nts, lambda key, k0, k1, n0, w: (wd[k0:k1, n0:n0 + w], self.Bw[f"mdown{l}"]),
                         self.tiles(c.TT), fill, epi, c.TT)
        with self.phase_scope() as st:
            sbt = lambda n, s, dt: st.enter_context(nc.sbuf_tensor(_uniq(n), s, dt))
            cdt = [sbt(f"a2_cd{i}", [128, c.DW], F32) for i in range(2)]
            Bcd = [Buf(), Buf()]
            qn = sbt("a2_qn", [128, QC], F32)
            kn = sbt("a2_kn", [128, KC2], F32)
            Bn = Buf()
            P.dma("sp", qn[:], self.qnormT[j, :, :], writes=[Bn])
            P.dma("sp", kn[:], self.kvnormT[j, :, :], writes=[Bn])
            s4 = sbt("a2_st", [128, 4], F32)
            Bs4 = Buf(strict=True)
            cs = [sbt(f"a2_cs{i}", [128, 2, 64], F32) for i in range(2)]
            Bcs = [Buf(), Buf()]
            kr = sbt("a2_kr", [128, 128], F32)
            Bkr = Buf()
            outq = [sbt(f"a2_oq{i}", [128, QC, 128], BF16) for i in range(2)]
            outk = [sbt(f"a2_ok{i}", [128, KC2 + 1, 128], BF16) for i in range(2)]
            Boq, Bok = [Buf(), Buf()], [Buf(), Buf()]
            for blk in range(c.NB):
                i = blk % 2
                r0 = blk * 128
                X = cdt[i]
                P.dma("sp", X[:], self.cd[r0:r0 + 128, :], reads=[self.B(("cd", blk, n0)) for n0 in range(0, c.DW, 512)], writes=[Bcd[i]])
                P.dma("sp", cs[i][:, 0, :], self.c_cos_tok[r0:r0 + 128, :], writes=[Bcs[i]])
                P.dma("sp", cs[i][:, 1, :], self.c_sin_tok[r0:r0 + 128, :], writes=[Bcs[i]])
                for (a0, w, col) in ((0, c.QR, 0), (c.QR, c.KVR, 2)):
                    P.actf(self.junk[:, 0:w], X[:, a0:a0 + w], AF.Square, [Bcd[i]], [self.Bjunk, Bs4], scale=float(w) ** -0.5, accum=s4[:, col:col + 1])
                    P.actf(s4[:, col + 1:col + 2], s4[:, col:col + 1], AF.Sqrt, [Bs4], [Bs4], bias=self.epsc[:, 0:1])
                    P.recip(s4[:, col + 1:col + 2], s4[:, col + 1:col + 2], [Bs4], [Bs4])
                    P.ts("dve", X[:, a0:a0 + w], X[:, a0:a0 + w], s4[:, col + 1:col + 2], None, ALU.mult, None, [Bcd[i], Bs4], [Bcd[i]])
                k0 = c.QR + c.KVR
                P.tt("dve", kr[:, 0:64], X[:, k0:k0 + 64], cs[i][:, 0, :], ALU.mult, [Bcd[i], Bcs[i]], [Bkr])
                P.tt("dve", kr[:, 64:128], X[:, k0 + 64:k0 + 128], cs[i][:, 1, :], ALU.mult, [Bcd[i], Bcs[i]], [Bkr])
                P.tt("dve", kr[:, 0:64], kr[:, 0:64], kr[:, 64:128], ALU.add, [Bkr], [Bkr])
                self.transpose_f32(X, Bcd[i], QC, outq[i], Boq[i], 0, scale_cols=qn, Bsc=Bn)
                Xk = X[:, c.QR:c.QR + c.KVR]
                self._tr_cols(Xk, Bcd[i], KC2, outk[i], Bok[i], kn, Bn)
                pi = 4 + (self._trk % 2)
                self._trk += 1
                P.tr(self.pf[pi][0:64, 0:128], kr[:, 0:64], self.identf[:], [Bkr], [self.Bpf[pi]])
                P.cp("dve", outk[i][0:64, KC2, :], self.pf[pi][0:64, 0:128], [self.Bpf[pi]], [Bok[i]])
                P.dma("sp", self.cqt[:, r0:r0 + 128].rearrange("(c p) t -> p c t", p=128), outq[i][:], reads=[Boq[i]], writes=[self.B(("cqt", blk))])
                if blk < c.NBC:
                    dst, t0, key = self.kvc, r0, ("kvc", blk)
                else:
                    dst, t0, key = self.kvx, r0 - c.CTX, ("kvx", blk)
                P.dma("sp", dst[0:c.KVR, t0:t0 + 128].rearrange("(c p) t -> p c t", p=128), outk[i][:, 0:KC2, :], reads=[Bok[i]], writes=[self.B(key)])
                P.dma("sp", dst[c.KVR:c.KVR + 64, t0:t0 + 128], outk[i][0:64, KC2, :], reads=[Bok[i]], writes=[self.B(key + ("r",))])
        self.check_stop(f"a2_{l}")
        Ball = self.B(("kvall",))
        rd = []
        for blk in range(c.NBC, c.NB):
            rd += [self.B(("kvx", blk)), self.B(("kvx", blk, "r"))]
        self.Bkvall = []
        for a in range(0, c.KVX, 128):
            b_ = min(c.KVX, a + 128)
            bb = Buf()
            self.Bkvall.append(bb)
            P.coll("AllGather", ALU.bypass, [[0, 1], [2, 3], [4, 5], [6, 7]], self.kvx[a:b_, :], self.kvall[2 * a:2 * b_, :], rd, [bb])
        self.gather_next(l)
        with self.phase_scope() as st:
            self.mla_attention(st, l, j, last)
        self.check_stop(f"a3_{l}")
        if last:
            self.skip_ctx = True
        wo = self.wfull[f"mout{l}"]
        with self.phase_scope() as st:
            sbt = lambda n, s, dt: st.enter_context(nc.sbuf_tensor(_uniq(n), s, dt))
            ot = [sbt(f"a4_o{i}", [128, D], BF16) for i in range(2)]
            Bo = [Buf(), Buf()]
            cnt = [0]

            def fill(blocks, actT, Bact):
                for i, blk in enumerate(blocks):
                    r0 = blk * 128
                    k = cnt[0] % 2
                    cnt[0] += 1
                    P.dma("sp", ot[k][:], self.oa[r0:r0 + 128, :], reads=[self.B(("oa", blk, h)) for h in range(c.H)], writes=[Bo[k]])
                    self.transpose_b16(ot[k], Bo[k], c.DC, actT, Bact, i * 128)
            epi = self.store_epi(st, self.ybuf, "y")
            nts = [("wo", n0, 512) for n0 in range(0, D, 512)]
            self.gemm_g1(st, D, nts, lambda key, k0, k1, n0, w: (wo[k0:k1, n0:n0 + w], self.Bw[f"mout{l}"]),
                         self.tiles(c.TT), fill, epi, c.TT)

    def _tr_cols(self, Xk, Bx, nch, out, Bout, scale, Bsc):
        P = self.P
        for c0 in range(0, nch, 4):
            n = min(4, nch - c0)
            pi = 4 + (self._trk % 2)
            self._trk += 1
            for jj in range(n):
                P.tr(self.pf[pi][:, jj * 128:(jj + 1) * 128], Xk[:, (c0 + jj) * 128:(c0 + jj + 1) * 128], self.identf[:], [Bx], [self.Bpf[pi]])
            for jj in range(n):
                P.actf(out[:, c0 + jj, :], self.pf[pi][:, jj * 128:(jj + 1) * 128], AF.Identity, [self.Bpf[pi], Bsc], [Bout],
                       scale=scale[:, c0 + jj:c0 + jj + 1])

    def mla_attention(self, st, l, j, last):
        c = self.cfg
        P = self.P
        nc = self.nc
        sbt = lambda n, s, dt: st.enter_context(nc.sbuf_tensor(_uniq(n), s, dt))
        QC, KC2 = c.QR // 128, c.KVR // 128
        NK, NT = c.NK, c.NT
        NKB = NK // 128
        scale = float(192 ** -0.5)
        kvT = sbt("at_kvT", [128, KC2, NK], BF16)
        krT = sbt("at_krT", [64, NK], BF16)
        cqT = sbt("at_cqT", [128, QC, NT], BF16)
        cosT = sbt("at_cos", [64, NT], F32)
        sinT = sbt("at_sin", [64, NT], F32)
        Bkv, Bcq, Bcs = Buf(), Buf(), Buf()
        rdc = []
        for blk in range(c.NBC):
            rdc += [self.B(("kvc", blk)), self.B(("kvc", blk, "r"))]
        P.dma("sp", kvT[:, :, 0:c.CTX], self.kvc[0:c.KVR, :].rearrange("(c p) t -> p c t", p=128), reads=rdc, writes=[Bkv])
        P.dma("sp", krT[:, 0:c.CTX], self.kvc[c.KVR:c.KVR + 64, :], reads=rdc, writes=[Bkv])
        Ball = self.B(("kvall",))
        for r in range(2):
            t0 = c.CTX + r * c.NTL
            for ch in range(KC2):
                a = ch * 128
                P.dma("sp", kvT[:, ch, t0:t0 + c.NTL], self.kvall[2 * a + r * 128:2 * a + (r + 1) * 128, :], reads=self.Bkvall, writes=[Bkv])
            a = c.KVR
            P.dma("sp", krT[:, t0:t0 + c.NTL], self.kvall[2 * a + r * 64:2 * a + (r + 1) * 64, :], reads=self.Bkvall, writes=[Bkv])
        P.dma("sp", cqT[:], self.cqt[:, :].rearrange("(c p) t -> p c t", p=128), reads=[self.B(("cqt", blk)) for blk in range(c.NB)], writes=[Bcq])
        P.dma("sp", cosT[:], self.c_cosT[:, :], writes=[Bcs])
        P.dma("sp", sinT[:], self.c_sinT[:, :], writes=[Bcs])
        wuq = self.wfull[f"muq{l}"]
        wukv = self.wfull[f"mukv{l}"]
        wq_sb = [sbt(f"at_wq{i}", [128, QC, 256], BF16) for i in range(2)]
        wk_sb = [sbt(f"at_wk{i}", [128, KC2, 256], BF16) for i in range(2)]
        Bwq, Bwk = [Buf(), Buf()], [Buf(), Buf()]
        kT = [sbt(f"at_kT{i}", [128, NK], BF16) for i in range(2)]
        V = [sbt(f"at_V{i}", [128, NKB, 132], BF16) for i in range(2)]
        qT = [sbt(f"at_qT{i}", [128, NT], BF16) for i in range(2)]
        qrT = [sbt(f"at_qr{i}", [64, NT], BF16) for i in range(2)]
        BkT, BV, BqT, Bqr = [Buf(), Buf()], [Buf(), Buf()], [Buf(), Buf()], [Buf(), Buf()]
        for i in range(2):
            P.memset("dve", V[i][:, :, 128:129], 1.0, [BV[i]])
        t1 = sbt("at_t1", [64, 384], F32)
        t2 = sbt("at_t2", [64, 384], F32)
        Bt1, Bt2 = Buf(), Buf()
        PT = [sbt(f"at_PT{i}", [128, 512], BF16) for i in range(3)]
        BPT = [Buf() for _ in range(3)]
        rs = sbt("at_rs", [128, 4], F32)
        Brs = Buf()
        osb = [sbt(f"at_o{i}", [128, 128], BF16) for i in range(4)]
        Bosb = [Buf() for _ in range(4)]
        pproj = [self.pf[4], self.pf[5]]
        Bpproj = [self.Bpf[4], self.Bpf[5]]
        pS = [self.pf[6], self.pf[7]]
        BpS = [self.Bpf[6], self.Bpf[7]]
        pO = [self.pf[0], self.pf[1], self.pf[2], self.pf[3]]
        BpO = [self.Bpf[0], self.Bpf[1], self.Bpf[2], self.Bpf[3]]
        prot = 0
        ocnt = 0
        ptc = 0
        qtiles = []
        if not last:
            for q0 in range(0, c.CTX, 512):
                qtiles.append((q0, min(512, c.CTX - q0), list(range(c.NBC))))
        for q0 in range(c.CTX, NT, 512):
            qtiles.append((q0, min(512, NT - q0), list(range(NKB))))
        for h in range(c.H):
            i = h % 2
            P.dma("sp", wq_sb[i][:], wuq[:, h * 256:(h + 1) * 256].rearrange("(c p) n -> p c n", p=128), reads=self.Bw[f"muq{l}"], writes=[Bwq[i]])
            P.dma("sp", wk_sb[i][:], wukv[:, h * 256:(h + 1) * 256].rearrange("(c p) n -> p c n", p=128), reads=self.Bw[f"mukv{l}"], writes=[Bwk[i]])
            for k0 in range(0, NK, 512):
                w = min(512, NK - k0)
                pi = prot % 2
                prot += 1
                for kc in range(KC2):
                    P.mm(pproj[pi][:, 0:w], wk_sb[i][:, kc, 0:128], kvT[:, kc, k0:k0 + w], kc == 0, kc == KC2 - 1, [Bwk[i], Bkv], [Bpproj[pi]])
                P.cp("dve" if (prot % 2) else "act", kT[i][:, k0:k0 + w], pproj[pi][:, 0:w], [Bpproj[pi]], [BkT[i]])
            for kb0 in range(0, NKB, 4):
                n = min(4, NKB - kb0)
                pi = prot % 2
                prot += 1
                for jj in range(n):
                    kb = kb0 + jj
                    for kc in range(KC2):
                        P.mm(pproj[pi][:, jj * 128:(jj + 1) * 128], kvT[:, kc, kb * 128:(kb + 1) * 128], wk_sb[i][:, kc, 128:256],
                             kc == 0, kc == KC2 - 1, [Bwk[i], Bkv], [Bpproj[pi]])
                P.cp("dve" if (prot % 2) else "act", V[i][:, kb0:kb0 + n, 0:128], pproj[pi][:, 0:n * 128].rearrange("p (k v) -> p k v", v=128),
                     [Bpproj[pi]], [BV[i]])
            for s0 in range(0, NT, 384):
                w = min(384, NT - s0)
                pi = prot % 2
                prot += 1
                for kc in range(QC):
                    P.mm(pproj[pi][:, 0:w], wq_sb[i][:, kc, 0:128], cqT[:, kc, s0:s0 + w], kc == 0, kc == QC - 1, [Bwq[i], Bcq], [Bpproj[pi]])
                P.cp("dve" if (prot % 2) else "act", qT[i][:, s0:s0 + w], pproj[pi][:, 0:w], [Bpproj[pi]], [BqT[i]])
                pa = prot % 2
                prot += 1
                for kc in range(QC):
                    P.mm(pproj[pa][0:64, 0:w], wq_sb[i][:, kc, 128:192], cqT[:, kc, s0:s0 + w], kc == 0, kc == QC - 1, [Bwq[i], Bcq], [Bpproj[pa]])
                P.tt("dve", t1[:, 0:w], pproj[pa][0:64, 0:w], cosT[:, s0:s0 + w], ALU.mult, [Bpproj[pa], Bcs], [Bt1])
                pb_ = prot % 2
                prot += 1
                for kc in range(QC):
                    P.mm(pproj[pb_][0:64, 0:w], wq_sb[i][:, kc, 192:256], cqT[:, kc, s0:s0 + w], kc == 0, kc == QC - 1, [Bwq[i], Bcq], [Bpproj[pb_]])
                P.tt("dve", t2[:, 0:w], pproj[pb_][0:64, 0:w], sinT[:, s0:s0 + w], ALU.mult, [Bpproj[pb_], Bcs], [Bt2])
                P.tt("dve", qrT[i][:, s0:s0 + w], t1[:, 0:w], t2[:, 0:w], ALU.add, [Bt1, Bt2], [Bqr[i]])
            for (q0, wq_, kbs) in qtiles:
                nqb = wq_ // 128

                def scores(kb):
                    si = kb % 2
                    P.mm(pS[si][:, 0:wq_], kT[i][:, kb * 128:(kb + 1) * 128], qT[i][:, q0:q0 + wq_], True, False, [BkT[i], BqT[i]], [BpS[si]])
                    P.mm(pS[si][:, 0:wq_], krT[:, kb * 128:(kb + 1) * 128], qrT[i][:, q0:q0 + wq_], False, True, [Bkv, Bqr[i]], [BpS[si]])
                scores(kbs[0])
                for n_, kb in enumerate(kbs):
                    if n_ + 1 < len(kbs):
                        scores(kbs[n_ + 1])
                    si = kb % 2
                    pj = ptc % 3
                    ptc += 1
                    P.actf(PT[pj][:, 0:wq_], pS[si][:, 0:wq_], AF.Exp, [BpS[si]], [BPT[pj]], scale=scale)
                    for qb in range(nqb):
                        P.mm(pO[qb][:, 0:129], PT[pj][:, qb * 128:(qb + 1) * 128], V[i][:, kb, 0:129], n_ == 0, n_ == len(kbs) - 1,
                             [BPT[pj], BV[i]], [BpO[qb]])
                for qb in range(nqb):
                    oj = ocnt % 4
                    ocnt += 1
                    P.recip(rs[:, qb:qb + 1], pO[qb][:, 128:129], [BpO[qb]], [Brs])
                    P.actf(osb[oj][:], pO[qb][:, 0:128], AF.Identity, [BpO[qb], Brs], [Bosb[oj]], scale=rs[:, qb:qb + 1])
                    blk = (q0 // 128) + qb
                    P.dma("sp", self.oa[blk * 128:(blk + 1) * 128, h * 128:(h + 1) * 128], osb[oj][:], reads=[Bosb[oj]], writes=[self.B(("oa", blk, h))])

    def final_out(self):
        c = self.cfg
        P = self.P
        nc = self.nc
        D = c.D
        l = c.DEPTH - 1
        with self.phase_scope() as st:
            sbt = lambda n, s, dt: st.enter_context(nc.sbuf_tensor(_uniq(n), s, dt))
            xt = [sbt(f"fo_x{i}", [128, D], F32) for i in range(2)]
            yt = [sbt(f"fo_y{i}", [128, D], F32) for i in range(2)]
            gt = sbt("fo_g", [128, D], F32)
            s4 = sbt("fo_s", [128, 2], F32)
            Bx, By, Bg, Bs = [Buf(), Buf()], [Buf(), Buf()], Buf(), Buf(strict=True)
            P.dma("sp", gt[:], self.vec[l, 5, 0:1, :].broadcast_to([128, D]), reads=[self.Bvec], writes=[Bg])
            rsq = float(D) ** -0.5
            for blk in range(c.NBC, c.NB):
                i = blk % 2
                r0 = blk * 128
                P.dma("sp", xt[i][:], self.cur_x[r0:r0 + 128, :], reads=[self.B(("x", blk))], writes=[Bx[i]])
                P.dma("sp", yt[i][:], self.ybuf[r0:r0 + 128, :], reads=[self.B(("y", blk, n0)) for n0 in range(0, D, 512)], writes=[By[i]])
                P.actf(self.junk[:, 0:D], yt[i][:], AF.Square, [By[i]], [self.Bjunk, Bs], scale=rsq, accum=s4[:, 0:1])
                P.actf(s4[:, 1:2], s4[:, 0:1], AF.Sqrt, [Bs], [Bs], bias=self.epsc[:, 0:1])
                P.recip(s4[:, 1:2], s4[:, 1:2], [Bs], [Bs])
                P.stt("dve", yt[i][:], yt[i][:], s4[:, 1:2], gt[:], ALU.mult, ALU.mult, [By[i], Bs, Bg], [By[i]])
                P.tt("dve", xt[i][:], xt[i][:], yt[i][:], ALU.add, [Bx[i], By[i]], [Bx[i]])
                P.dma("sp", self.out[r0 - c.CTX:r0 - c.CTX + 128, :], xt[i][:], reads=[Bx[i]], writes=[Buf()])


ROPE_PERM = np.concatenate([np.arange(16, 32), np.arange(0, 16), np.arange(48, 64), np.arange(32, 48)])
ROPE_SIGN = np.concatenate([-np.ones(16), np.ones(16), -np.ones(16), np.ones(16)]).astype(np.float32)


def _consts(cfg, half):
    s = np.arange(128)
    same = (s[:, None] // 64) == (s[None, :] // 64)
    trif = (same & (s[:, None] <= s[None, :])).astype(np.float32)
    trifx = (same & (s[:, None] > s[None, :])).astype(np.float32)
    ind = np.stack([(s < 64), (s >= 64)], 1).astype(np.float32)
    NTL = cfg.NTL
    loc = np.arange(NTL)
    pos = loc if half == 0 else (2 * NTL - 1 - loc)
    row = (pos // cfg.GRID_W).astype(np.float32)
    col = (pos % cfg.GRID_W).astype(np.float32)
    inv = (np.float32(10000.0) ** (-(np.arange(0, 32, 2, dtype=np.float32) / np.float32(32)))).astype(np.float32)
    ar = row[:, None] * inv
    ac = col[:, None] * inv
    ang = np.concatenate([ar, ar, ac, ac], -1).astype(np.float32)
    cos = np.concatenate([np.ones((cfg.CTX, 64), np.float32), np.cos(ang).astype(np.float32)], 0)
    sin = np.concatenate([np.zeros((cfg.CTX, 64), np.float32), np.sin(ang).astype(np.float32) * ROPE_SIGN[None, :]], 0)
    return {"c_ident": np.eye(128, dtype=np.float32), "c_trif": trif, "c_trifx": trifx, "c_ind": ind,
            "c_cos_tok": np.ascontiguousarray(cos), "c_sin_tok": np.ascontiguousarray(sin),
            "c_cosT": np.ascontiguousarray(cos.T), "c_sinT": np.ascontiguousarray(sin.T)}


def prep_inputs(cfg, inp):
    D, L = cfg.D, cfg.DEPTH
    f = lambda a: np.ascontiguousarray(np.asarray(a, dtype=np.float32))
    x, cc, ctx, c_ctx = f(inp["x"]), f(inp["c"]), f(inp["ctx"]), f(inp["c_ctx"])
    ada_down, ada_up, ada_bias = f(inp["ada_down"]), f(inp["ada_up"]), f(inp["ada_bias"])
    w1, w2 = inp["mlp_w1"], inp["mlp_w2"]
    hin, hout = inp["hg_w_in"], inp["hg_w_out"]
    mdown, muq, mukv, mout = inp["mla_w_down"], inp["mla_w_uq"], inp["mla_w_ukv"], inp["mla_w_out"]
    gains = f(np.stack([inp["norm_mix_pre"], inp["norm_mix_post"], inp["norm_mlp_pre"], inp["norm_mlp_post"]], 0))
    fm = lambda v: np.ascontiguousarray(np.asarray(v, np.float32).reshape(v.shape[0], -1, 128).transpose(0, 2, 1))
    hgnormT, qnormT, kvnormT = fm(inp["hg_norm"]), fm(inp["mla_q_norm"]), fm(inp["mla_kv_norm"])
    lbl = f(inp["hg_lb_logits"])
    maps = []
    consts = [_consts(cfg, 0), _consts(cfg, 1)]
    H = cfg.H
    for r in range(NCORES):
        b, half = r // 2, r % 2
        m = {}
        xl = x[b, half * cfg.NTL:(half + 1) * cfg.NTL]
        cx = ctx[b]
        if half:
            xl, cx = xl[::-1], cx[::-1]
        m["x_loc"] = np.ascontiguousarray(np.concatenate([cx, xl], 0))
        cond = np.stack([cc[b], c_ctx], 0)
        m["condT"] = np.ascontiguousarray(cond.reshape(2, cfg.DC, 128).transpose(2, 1, 0))
        r4 = r % 4
        def shw(a, bf=True):
            a = np.asarray(a)
            Kf, Nf = a.shape
            cr = chunk_rows(Kf, Nf, 2 if bf else 4)
            return np.ascontiguousarray(a.reshape(Kf // (4 * cr), 4, cr, Nf)[:, r4].reshape(Kf // 4, Nf), dtype=np.float32)
        sh8 = shw
        for l in range(L):
            m[f"adown{l}"] = shw(ada_down[l], False)
            m[f"aup{l}"] = shw(ada_up[l], False)
            m[f"w1_{l}"] = sh8(w1[l])
            m[f"w2_{l}"] = sh8(w2[l])
            j = l // 2
            if l % 2 == 0:
                wi = np.asarray(hin[j])
                m[f"hqig{l}"] = shw(np.concatenate([wi[:, 0:D], wi[:, D:2 * D], wi[:, 4 * D:5 * D]], 1))
                m[f"hz{l}"] = shw(wi[:, 2 * D:4 * D])
                m[f"hout{l}"] = sh8(hout[j])
            else:
                wd = np.asarray(mdown[j])
                kr = wd[:, cfg.QR + cfg.KVR:cfg.QR + cfg.KVR + 64]
                m[f"mdown{l}"] = shw(np.concatenate([wd, kr[:, ROPE_PERM]], 1))
                wq = np.asarray(muq[j]).reshape(-1, H, 192)
                m[f"muq{l}"] = shw(np.concatenate([wq, wq[:, :, 128:][:, :, ROPE_PERM]], 2).reshape(-1, H * 256))
                m[f"mukv{l}"] = sh8(mukv[j])
                m[f"mout{l}"] = sh8(mout[j])
        m["abias"] = ada_bias
        m["gains"] = gains
        m["lblog"] = lbl
        m["c_sel"] = np.tile(np.array([[1.0, 0.0]] if half == 0 else [[0.0, 1.0]], np.float32), (128, 1))
        m["hgnormT"], m["qnormT"], m["kvnormT"] = hgnormT, qnormT, kvnormT
        m.update(consts[half])
        maps.append(m)
    return maps


_NC_CACHE = {}


def run_cfg(cfg, inp, stop=None, dbg=(), raw=False):
    key = (cfg.D, cfg.HID, cfg.NTL, cfg.CTX, cfg.QR, cfg.KVR, cfg.ADAR, cfg.DEPTH, cfg.TT, cfg.TT2, stop, tuple(dbg))
    if key not in _NC_CACHE:
        _NC_CACHE[key] = K(cfg, stop, dbg).build()
    nc = _NC_CACHE[key]
    maps = prep_inputs(cfg, inp)
    res = run_bass_kernel_spmd(nc, maps, core_ids=list(range(NCORES)))
    if raw:
        return res.results
    B = NCORES // 2
    out = np.empty((B, 2 * cfg.NTL, cfg.D), np.float32)
    for r in range(NCORES):
        b, half = r // 2, r % 2
        o = np.asarray(res.results[r]["out"], dtype=np.float32)
        out[b, half * cfg.NTL:(half + 1) * cfg.NTL] = o[::-1] if half else o
    return out


def kernel(**inputs):
    return run_cfg(Cfg(), inputs)
```
